# Optimizing a Trainium2 kernel written in Bass

```python
import math
import jax, jax.numpy as jnp
from jax import lax
import numpy as np

D_MODEL = 1024
BATCH = 8
SEQ = 2048
DEPTH = 2
DEC_BATCH = 128
DEC_SEQ = 1
PAST_LEN = 16384
PAGE_SIZE = 128

HG_HEADS = 8
HG_DK = 128
HG_DV = 128
HG_WIDTH = HG_HEADS * HG_DK
CHUNK = 64
LB_FLOOR = 1e-30
LRU_WIDTH = D_MODEL
LRU_BLOCKS = 8
LRU_BS = LRU_WIDTH // LRU_BLOCKS
LRU_C = 8.0
CONV_W = 4
D_FF = 4 * D_MODEL
EPS = 1e-6
COL_SIZES = [HG_WIDTH, HG_WIDTH, HG_HEADS * HG_DV, HG_HEADS * HG_DV,
             LRU_WIDTH, LRU_WIDTH, D_MODEL, D_MODEL]
N_IN = sum(COL_SIZES)

kernel_name = "hgrn2_rglru_gated_parallel_decoder_step"

F32 = jnp.float32


def _rmsnorm(x, g):
    xf = x.astype(F32)
    return xf * lax.rsqrt(jnp.mean(xf * xf, axis=-1, keepdims=True) + EPS) * g.astype(F32)


def _split_cols(u):
    idx = [int(v) for v in np.cumsum(COL_SIZES)[:-1]]
    return jnp.split(u, idx, axis=-1)


def _hgrn2(q, k, v, logf, S0):
    B, L, H, DK = q.shape
    DV = v.shape[-1]
    C = math.gcd(L, CHUNK)
    N = L // C

    def to_chunks(t):
        return t.reshape(B, N, C, H, t.shape[-1]).transpose(1, 0, 3, 2, 4)

    xs = tuple(to_chunks(t.astype(F32)) for t in (q, k, v, logf))
    mask = jnp.tril(jnp.ones((C, C), dtype=bool))[None, None, :, :, None]

    def step(S, inp):
        qc, kc, vc, gc = inp
        G = jnp.cumsum(gc, axis=2)
        o_inter = jnp.einsum('bhtk,bhkv->bhtv', qc * jnp.exp(G), S)
        diff = G[:, :, :, None, :] - G[:, :, None, :, :]
        decay = jnp.where(mask, jnp.exp(jnp.where(mask, diff, 0.0)), 0.0)
        A = jnp.einsum('bhtk,bhsk,bhtsk->bhts', qc, kc, decay)
        o = o_inter + jnp.einsum('bhts,bhsv->bhtv', A, vc)
        G_last = G[:, :, -1:, :]
        S_new = (jnp.exp(G_last[:, :, 0, :])[..., None] * S
                 + jnp.einsum('bhsk,bhsv->bhkv', kc * jnp.exp(G_last - G), vc))
        return S_new, o

    S_fin, o = lax.scan(step, S0, xs)
    o = o.transpose(1, 0, 3, 2, 4).reshape(B, L, H, DV)
    return o, S_fin


def _causal_conv(x, buf, w, b):
    L = x.shape[1]
    full = jnp.concatenate([buf.astype(F32), x.astype(F32)], axis=1)
    out = b.astype(F32) + sum(full[:, j:j + L] * w[j].astype(F32) for j in range(CONV_W))
    return out, full[:, -(CONV_W - 1):]


def _blockdiag(x, w):
    B, L, _ = x.shape
    xb = x.reshape(B, L, LRU_BLOCKS, LRU_BS)
    return jnp.einsum('blnd,nde->blne', xb, w.astype(F32)).reshape(B, L, LRU_WIDTH)


def _rglru(xc, r, i, lam, h0):
    log_a = -LRU_C * r * jax.nn.softplus(-lam.astype(F32))
    a = jnp.exp(log_a)
    b = jnp.sqrt(-jnp.expm1(2.0 * log_a)) * (i * xc)
    b = b.at[:, 0].add(a[:, 0] * h0)

    def comb(left, right):
        a1, b1 = left
        a2, b2 = right
        return a1 * a2, a2 * b1 + b2

    _, h = lax.associative_scan(comb, (a, b), axis=1)
    return h, h[:, -1]


def _trunk(x, c, st_hg, st_lru, st_conv, lb, weights):
    (w_mod, b_mod, norm1_g, norm2_g, w_in, hg_onorm_g, lru_conv_w, lru_conv_b,
     lru_wa, lru_ba, lru_wx, lru_bx, lru_lambda, w_branch_a, w_branch_b, w_out,
     w_ff1, w_ff2, final_norm_g) = weights
    dt = x.dtype
    B, L, _ = x.shape
    cs = jax.nn.silu(c.astype(F32))
    new_hg, new_lru, new_conv = [], [], []
    for l in range(DEPTH):
        mod = cs @ w_mod[l].astype(F32) + b_mod[l].astype(F32)
        sh1, sc1, g1, sh2, sc2, g2 = [m[:, None, :] for m in jnp.split(mod, 6, axis=-1)]

        h = _rmsnorm(x, norm1_g[l]) * (1.0 + sc1) + sh1
        u = h @ w_in[l].astype(F32)
        qa, fa, ia, oga, xb, yb, ga, gb = _split_cols(u)

        lbl = lb[l]
        q = jax.nn.silu(qa).reshape(B, L, HG_HEADS, HG_DK)
        logf = jnp.logaddexp(jnp.log(jnp.maximum(lbl, LB_FLOOR)),
                             jnp.log1p(-lbl) + jax.nn.log_sigmoid(fa))
        k = (1.0 - lbl) * jax.nn.sigmoid(-fa)
        o, S_new = _hgrn2(q, k.reshape(B, L, HG_HEADS, HG_DK), ia.reshape(B, L, HG_HEADS, HG_DV),
                          logf.reshape(B, L, HG_HEADS, HG_DK), st_hg[l].astype(F32))
        o = _rmsnorm(o, hg_onorm_g[l]) * jax.nn.silu(oga.reshape(B, L, HG_HEADS, HG_DV))
        o_a = o.reshape(B, L, HG_HEADS * HG_DV)

        xc, buf = _causal_conv(xb, st_conv[l], lru_conv_w[l], lru_conv_b[l])
        r = jax.nn.sigmoid(_blockdiag(xc, lru_wa[l]) + lru_ba[l].astype(F32))
        ig = jax.nn.sigmoid(_blockdiag(xc, lru_wx[l]) + lru_bx[l].astype(F32))
        hseq, h_last = _rglru(xc, r, ig, lru_lambda[l], st_lru[l].astype(F32))
        o_b = hseq * jax.nn.gelu(yb)

        merged = (jax.nn.sigmoid(ga) * (o_a @ w_branch_a[l].astype(F32))
                  + jax.nn.sigmoid(gb) * (o_b @ w_branch_b[l].astype(F32)))
        x = (x.astype(F32) + g1 * (merged @ w_out[l].astype(F32))).astype(dt)

        h2 = _rmsnorm(x, norm2_g[l]) * (1.0 + sc2) + sh2
        ff = jnp.square(jax.nn.relu(h2 @ w_ff1[l].astype(F32))) @ w_ff2[l].astype(F32)
        x = (x.astype(F32) + g2 * ff).astype(dt)

        new_hg.append(S_new)
        new_lru.append(h_last)
        new_conv.append(buf)
    y = _rmsnorm(x, final_norm_g).astype(dt)
    return (y, jnp.stack(new_hg).astype(st_hg.dtype), jnp.stack(new_lru).astype(st_lru.dtype),
            jnp.stack(new_conv).astype(st_conv.dtype))


def setup_inputs(seed: int = 0) -> dict:
    key = jax.random.key(seed)
    ks = jax.random.split(key, 32)
    nrm = jax.random.normal
    D = D_MODEL
    u = jax.random.uniform(ks[20], (DEPTH, LRU_WIDTH), minval=0.9, maxval=0.999)
    a0 = u ** (1.0 / LRU_C)
    lam = jnp.log(a0) - jnp.log1p(-a0)
    return {
        "x_prompt": nrm(ks[0], (BATCH, SEQ, D), F32),
        "x_sample": nrm(ks[1], (DEC_BATCH, DEC_SEQ, D), F32),
        "state_hgrn": 0.5 * nrm(ks[2], (DEPTH, DEC_BATCH, HG_HEADS, HG_DK, HG_DV), F32),
        "state_rglru": nrm(ks[3], (DEPTH, DEC_BATCH, LRU_WIDTH), F32),
        "state_conv": nrm(ks[4], (DEPTH, DEC_BATCH, CONV_W - 1, LRU_WIDTH), F32),
        "c_prompt": nrm(ks[5], (BATCH, D), F32),
        "c_sample": nrm(ks[6], (DEC_BATCH, D), F32),
        "w_mod": 0.5 * D ** -0.5 * nrm(ks[7], (DEPTH, D, 6 * D), F32),
        "b_mod": 0.02 * nrm(ks[8], (DEPTH, 6 * D), F32),
        "norm1_g": 1.0 + 0.02 * nrm(ks[9], (DEPTH, D), F32),
        "norm2_g": 1.0 + 0.02 * nrm(ks[10], (DEPTH, D), F32),
        "w_in": D ** -0.5 * nrm(ks[11], (DEPTH, D, N_IN), F32),
        "hg_lower": nrm(ks[12], (DEPTH, HG_WIDTH), F32),
        "hg_onorm_g": 1.0 + 0.02 * nrm(ks[13], (DEPTH, HG_DV), F32),
        "lru_conv_w": CONV_W ** -0.5 * nrm(ks[14], (DEPTH, CONV_W, LRU_WIDTH), F32),
        "lru_conv_b": 0.02 * nrm(ks[15], (DEPTH, LRU_WIDTH), F32),
        "lru_wa": LRU_BS ** -0.5 * nrm(ks[16], (DEPTH, LRU_BLOCKS, LRU_BS, LRU_BS), F32),
        "lru_ba": 0.02 * nrm(ks[17], (DEPTH, LRU_WIDTH), F32),
        "lru_wx": LRU_BS ** -0.5 * nrm(ks[18], (DEPTH, LRU_BLOCKS, LRU_BS, LRU_BS), F32),
        "lru_bx": 0.02 * nrm(ks[19], (DEPTH, LRU_WIDTH), F32),
        "lru_lambda": lam,
        "w_branch_a": HG_WIDTH ** -0.5 * nrm(ks[21], (DEPTH, HG_HEADS * HG_DV, D), F32),
        "w_branch_b": LRU_WIDTH ** -0.5 * nrm(ks[22], (DEPTH, LRU_WIDTH, D), F32),
        "w_out": D ** -0.5 * nrm(ks[23], (DEPTH, D, D), F32),
        "w_ff1": D ** -0.5 * nrm(ks[24], (DEPTH, D, D_FF), F32),
        "w_ff2": D_FF ** -0.5 * nrm(ks[25], (DEPTH, D_FF, D), F32),
        "final_norm_g": 1.0 + 0.02 * nrm(ks[26], (D,), F32),
    }


def reference(x_prompt, x_sample, state_hgrn, state_rglru, state_conv, c_prompt, c_sample,
              w_mod, b_mod, norm1_g, norm2_g, w_in, hg_lower, hg_onorm_g, lru_conv_w, lru_conv_b,
              lru_wa, lru_ba, lru_wx, lru_bx, lru_lambda, w_branch_a, w_branch_b, w_out,
              w_ff1, w_ff2, final_norm_g):
    p = jax.nn.softmax(hg_lower.astype(F32), axis=0)
    lb = jnp.cumsum(p, axis=0) - p[0:1]
    weights = (w_mod, b_mod, norm1_g, norm2_g, w_in, hg_onorm_g, lru_conv_w, lru_conv_b,
               lru_wa, lru_ba, lru_wx, lru_bx, lru_lambda, w_branch_a, w_branch_b, w_out,
               w_ff1, w_ff2, final_norm_g)
    Bp = x_prompt.shape[0]
    zero_hg = jnp.zeros((DEPTH, Bp, HG_HEADS, HG_DK, HG_DV), state_hgrn.dtype)
    zero_lru = jnp.zeros((DEPTH, Bp, LRU_WIDTH), state_rglru.dtype)
    zero_conv = jnp.zeros((DEPTH, Bp, CONV_W - 1, LRU_WIDTH), state_conv.dtype)
    y_prompt, hg_p, lru_p, conv_p = _trunk(x_prompt, c_prompt, zero_hg, zero_lru, zero_conv, lb, weights)
    y_sample, hg_s, lru_s, conv_s = _trunk(x_sample, c_sample, state_hgrn, state_rglru, state_conv, lb, weights)
    return (y_prompt, y_sample, hg_p, lru_p, conv_p, hg_s, lru_s, conv_s)
```

```python
import numpy as np
from contextlib import ExitStack
import concourse.bass as bass
import concourse.mybir as mybir
from concourse.bass_utils import run_bass_kernel_spmd

F32 = mybir.dt.float32
BF16 = mybir.dt.bfloat16
AF = mybir.ActivationFunctionType
ALU = mybir.AluOpType

D = 1024
SEQ = 2048
TH = 1024
TB = 512
NS = 16
NCORE = 8
EPS = 1e-6
CH = 32
NCH = TB // CH

R_N1G, R_N2G, R_CONVW, R_CONVB, R_BA, R_BX, R_LAM, R_HGL, R_FNG, R_BMOD, R_ONORM = 0, 2, 4, 12, 14, 16, 18, 20, 22, 23, 35
R_CP, R_CS, R_XS, R_SRG, R_SCV = 37, 38, 54, 70, 102
NR = 198
O_YS, O_LRUP, O_CONVP, O_LRUS, O_CONVS = 0, 16, 18, 24, 56
NRO = 152
SEG = 20000

ENGNAME = {'pe': 'tensor', 'act': 'scalar', 'dve': 'vector', 'pool': 'gpsimd', 'sp': 'sync'}


class Op:
    __slots__ = ('eng', 'fn', 'dma', 'signal', 'signo', 'we', 'wd', 'deps', 'dvals', 'cost', 'tset', 'nbytes',
                 'start', 'fin', 'idx')


def _free_elems(ap):
    n = 1
    for st, c in ap.ap[1:]:
        n *= c
    return n


class Prog:
    def __init__(self, nc):
        self.nc = nc
        self.ops = []
        self.hist = {}
        self.dma_cnt = {}
        self.last_dma = {}
        self.marks = []

    def mark(self, label):
        self.marks.append((label, len(self.ops)))

    def regs(self, aps):
        out = []
        for ap in aps:
            if ap is None or not hasattr(ap, 'ap'):
                continue
            if 'DRAM' in str(ap.space).upper():
                continue
            pat = ap.ap
            ps = pat[0][0]
            es = 4 if ap.dtype == F32 else 2
            p0 = ap.offset // ps
            f0 = ap.offset % ps
            span = sum((c - 1) * abs(st) for st, c in pat[1:]) + 1
            if 'PSUM' in str(ap.space).upper():
                out.append((ap.tensor.name, 0, 128, 0, 2048))
            else:
                out.append((ap.tensor.name, p0, p0 + pat[0][1], f0 * es, (f0 + span) * es))
        return out

    def add(self, eng, fn, ins, outs, dma=None, cost=0.2, tset=None, nbytes=0):
        idx = len(self.ops)
        R = self.regs(ins)
        W = self.regs(outs)
        deps = {}
        for (n, p0, p1, a, b) in R:
            psum = n.startswith('ps')
            for k, v in self.hist.get(n, {}).items():
                if k[0] < p1 and p0 < k[1] and k[2] < b and a < k[3]:
                    if k[5]:
                        deps[v] = True
                    elif psum and k[4] != eng:
                        for vv in v:
                            deps.setdefault(vv, False)
        for (n, p0, p1, a, b) in W:
            for k, v in self.hist.get(n, {}).items():
                if k[0] < p1 and p0 < k[1] and k[2] < b and a < k[3]:
                    if k[5]:
                        deps.setdefault(v, False)
                    else:
                        for vv in v:
                            deps.setdefault(vv, False)
        for (n, p0, p1, a, b) in W:
            L = self.hist.setdefault(n, {})
            for k in [k for k in L if p0 <= k[0] and k[1] <= p1 and a <= k[2] and k[3] <= b]:
                del L[k]
            L[(p0, p1, a, b, eng, True)] = idx
        for (n, p0, p1, a, b) in R:
            self.hist.setdefault(n, {}).setdefault((p0, p1, a, b, eng, False), []).append(idx)
        op = Op()
        op.eng = eng; op.fn = fn; op.dma = dma; op.signal = False; op.signo = 0
        op.cost = cost; op.tset = tset; op.nbytes = nbytes; op.idx = idx
        dvals = {}
        for pidx in list(deps):
            pop = self.ops[pidx]
            if pop.dma is not None:
                val = self.dma_cnt[pop.dma]
                dvals[pop.dma] = val
                deps.setdefault(self.last_dma[pop.dma], deps[pidx])
        if dma is not None:
            self.dma_cnt[dma] = self.dma_cnt.get(dma, 0) + 16
            self.last_dma[dma] = idx
        op.deps = deps
        op.dvals = dvals
        self.ops.append(op)

    def schedule(self, window=900):
        import heapq, os
        ops = self.ops
        n = len(ops)
        succ = [[] for _ in range(n)]
        nrem = [0] * n
        for i, op in enumerate(ops):
            nrem[i] = len(op.deps)
            for p in op.deps:
                succ[p].append(i)
        ready = [0.0] * n
        cp = [0.0] * n
        for i in range(n - 1, -1, -1):
            m = 0.0
            for j in succ[i]:
                if cp[j] > m:
                    m = cp[j]
            cp[i] = m + ops[i].cost + 0.12
        USECP = os.environ.get('K_CP', '0') == '1'
        inorder = {'sp': [], 'pool': []}
        for i, op in enumerate(ops):
            if op.eng in inorder:
                inorder[op.eng].append(i)
        ptr = {'sp': 0, 'pool': 0}
        pend = {e: [] for e in ('pe', 'act', 'dve')}
        avail = {e: [] for e in ('pe', 'act', 'dve')}
        free = {e: 0.0 for e in ENGNAME}
        cur_set = [None]
        dma_free = [0.0]
        done = [False] * n
        nsched = 0
        low = {e: 0 for e in ('pe', 'act', 'dve')}
        eng_ops = {e: [i for i, op in enumerate(ops) if op.eng == e] for e in ('pe', 'act', 'dve')}
        eng_ptr = {e: 0 for e in ('pe', 'act', 'dve')}
        released = [False] * n
        for i, op in enumerate(ops):
            if nrem[i] == 0 and op.eng in pend:
                heapq.heappush(pend[op.eng], (0.0, i)); released[i] = True
        LAT = 0.12
        order = []

        def tswitch(op):
            t = op.tset
            if t is None:
                return False
            c = cur_set[0]
            if t == c:
                return False
            if t == 'tanh' and c in ('silu', 'gelu', 'sigmoid'):
                return False
            return True

        while nsched < n:
            best = None
            for e in ('pe', 'act', 'dve'):
                while eng_ptr[e] < len(eng_ops[e]) and done[eng_ops[e][eng_ptr[e]]]:
                    eng_ptr[e] += 1
                lo = eng_ops[e][eng_ptr[e]] if eng_ptr[e] < len(eng_ops[e]) else n
                P_, A_ = pend[e], avail[e]
                while P_ and P_[0][0] <= free[e]:
                    r, i = heapq.heappop(P_)
                    heapq.heappush(A_, i)
                cand = None
                if A_:
                    if USECP:
                        inw = [j for j in A_ if j <= lo + window]
                        i = None
                        if inw:
                            if e == 'act':
                                ns = [j for j in inw if not tswitch(ops[j])]
                                best_ns = max(ns, key=lambda j: cp[j]) if ns else None
                                best_all = max(inw, key=lambda j: cp[j])
                                i = best_ns if (best_ns is not None and cp[best_ns] + 6.0 >= cp[best_all]) else best_all
                            else:
                                i = max(inw, key=lambda j: cp[j])
                    else:
                        i = A_[0]
                        if i > lo + window:
                            i = None
                        if i is not None and e == 'act' and tswitch(ops[i]):
                            alt = [j for j in heapq.nsmallest(6, A_) if j <= lo + window and not tswitch(ops[j])]
                            if alt:
                                i = alt[0]
                    if i is not None:
                        cand = (free[e], i)
                if cand is None and P_:
                    r, i = P_[0]
                    if i <= lo + window:
                        cand = (r, i)
                    else:
                        j = min(P_, key=lambda x: x[1])
                        cand = (max(j[0], free[e]), j[1])
                if cand is not None and (best is None or (cand[0], cand[1]) < (best[0], best[1])):
                    best = (cand[0], cand[1], e)
            for e in ('sp', 'pool'):
                if ptr[e] < len(inorder[e]):
                    i = inorder[e][ptr[e]]
                    if nrem[i] == 0:
                        st_ = max(free[e], ready[i])
                        if best is None or (st_, i) < (best[0], best[1]):
                            best = (st_, i, e)
            assert best is not None, "scheduler deadlock"
            st_, i, e = best
            op = ops[i]
            if e in pend:
                if i in avail[e]:
                    avail[e].remove(i); heapq.heapify(avail[e])
                else:
                    pend[e].remove((ready[i], i)); heapq.heapify(pend[e])
            else:
                ptr[e] += 1
            c = op.cost
            if e == 'act' and op.tset is not None:
                if tswitch(op):
                    c += 1.3
                    cur_set[0] = 'silu' if op.tset == 'tanh' else op.tset
            op.start = st_
            if op.dma is not None:
                free[e] = st_ + 0.06
                t0 = max(st_, dma_free[0])
                dma_free[0] = t0 + op.nbytes / 300e3
                op.fin = dma_free[0] + 2.0
            else:
                free[e] = st_ + c
                op.fin = st_ + c
            done[i] = True
            nsched += 1
            order.append(i)
            for sidx in succ[i]:
                nrem[sidx] -= 1
                ready[sidx] = max(ready[sidx], op.fin + LAT)
                if nrem[sidx] == 0 and ops[sidx].eng in pend:
                    heapq.heappush(pend[ops[sidx].eng], (ready[sidx], sidx))
        self.ndummy = {}
        last_fin = None
        for i in order:
            op = ops[i]
            if op.eng != 'pe':
                continue
            if last_fin is not None:
                gap = op.start - last_fin
                if gap > 1.0 and i >= getattr(self, 'warm_from', 1 << 60):
                    self.ndummy[i] = min(int(gap * 0.6 / 0.13), 24)
            last_fin = op.fin
        self.order = order
        self.makespan = max(op.fin for op in ops)
        return order

    def emit(self, stack, reorder=True):
        import os
        nc = self.nc
        ops = self.ops
        if reorder:
            order = self.schedule()
        else:
            order = list(range(len(ops)))
        seen = {e: {} for e in ENGNAME}
        for i in order:
            op = ops[i]
            eng = op.eng
            sn = seen[eng]
            we = {}
            wd = {}
            for k, v in op.dvals.items():
                if sn.get(('d', k), 0) < v:
                    wd[k] = v
            for pidx, raw in op.deps.items():
                pop = ops[pidx]
                if pop.dma is not None:
                    continue
                pe_ = pop.eng
                if pe_ == eng and (eng == 'pe' or not raw):
                    continue
                we.setdefault(pe_, []).append(pidx)
            op.we = we
            op.wd = wd
            for k, v in wd.items():
                sn[('d', k)] = v
        pos = {}
        cnt = {e: 0 for e in ENGNAME}
        for i in order:
            cnt[ops[i].eng] += 1
            pos[i] = cnt[ops[i].eng]
        seenp = {e: {} for e in ENGNAME}
        for i in order:
            op = ops[i]
            red = {}
            for pe_, lst in op.we.items():
                p = max(lst, key=lambda x: pos[x])
                if seenp[op.eng].get(pe_, 0) < pos[p]:
                    red[pe_] = p
                    seenp[op.eng][pe_] = pos[p]
                    ops[p].signal = True
            op.we = red
        cnt = {e: 0 for e in ENGNAME}
        for i in order:
            op = ops[i]
            if op.dma is None and op.signal:
                cnt[op.eng] += 1
                op.signo = cnt[op.eng]
        esem = {}
        for e in ENGNAME:
            esem[e] = [stack.enter_context(nc.semaphore(f"s_{e}{i}")) for i in range(cnt[e] // SEG + 1)]
        dsem = {k: stack.enter_context(nc.semaphore(f"d_{k}")) for k in self.dma_cnt}
        mk = dict((i, l) for l, i in self.marks)
        mout = []
        names = {}
        for oi in order:
            op = ops[oi]
            E = getattr(nc, ENGNAME[op.eng])
            if getattr(self, 'warm', None) is not None and reorder:
                for _ in range(self.ndummy.get(oi, 0)):
                    self.warm()
            for pe_, pidx in op.we.items():
                sn = ops[pidx].signo - 1
                E.wait_ge(esem[pe_][sn // SEG], sn % SEG + 1)
            for k, v in op.wd.items():
                E.wait_ge(dsem[k], v)
            ins = op.fn()
            if os.environ.get('K_MARKS'):
                try:
                    names[oi] = str(ins.ins.name)
                except Exception as ex:
                    names[oi] = '?'
            if op.dma is not None:
                ins.then_inc(dsem[op.dma], 16)
            elif op.signal:
                sn = op.signo - 1
                ins.then_inc(esem[op.eng][sn // SEG], 1)
        for k, v in self.dma_cnt.items():
            nc.sync.wait_ge(dsem[k], v)
        if os.environ.get('K_MARKS'):
            import json
            json.dump({'marks': self.marks, 'names': names, 'makespan': getattr(self, 'makespan', None)},
                      open(os.environ['K_MARKS'], 'w'))

    def act(self, out, in_, func, scale=1.0, bias=None):
        nc = self.nc
        ins = [in_]
        kw = {}
        if hasattr(scale, 'ap'):
            ins.append(scale)
        if bias is not None:
            kw['bias'] = bias
            if hasattr(bias, 'ap'):
                ins.append(bias)
        tset = {AF.Silu: 'silu', AF.Tanh: 'tanh', AF.Ln: 'lnexp', AF.Exp: 'lnexp', AF.Sigmoid: 'sigmoid',
                AF.Gelu_apprx_tanh: 'gelu'}.get(func)
        self.add('act', lambda: nc.scalar.activation(out=out, in_=in_, func=func, scale=scale, **kw), ins, [out],
                 cost=0.25 + 0.1 * (len(ins) - 1) + _free_elems(in_) / 1200.0, tset=tset)

    def _e(self, eng):
        return getattr(self.nc, ENGNAME[eng])

    def tt(self, eng, out, in0, in1, op):
        E = self._e(eng)
        self.add(eng, lambda: E.tensor_tensor(out, in0, in1, op), [in0, in1], [out], cost=self._vc(eng, out))

    def ts(self, eng, out, in0, s1, s2, op0, op1=None):
        E = self._e(eng)
        ins = [in0] + [s for s in (s1, s2) if hasattr(s, 'ap')]
        if op1 is None:
            self.add(eng, lambda: E.tensor_scalar(out, in0, s1, None, op0), ins, [out], cost=self._vc(eng, out))
        else:
            self.add(eng, lambda: E.tensor_scalar(out, in0, s1, s2, op0, op1), ins, [out], cost=self._vc(eng, out))

    def stt(self, out, in0, scalar, in1, op0, op1):
        nc = self.nc
        ins = [in0, in1] + ([scalar] if hasattr(scalar, 'ap') else [])
        self.add('dve', lambda: nc.vector.scalar_tensor_tensor(out, in0, scalar, in1, op0, op1), ins, [out], cost=self._vc('dve', out))

    def scan(self, out, d0, d1, init, op0, op1):
        nc = self.nc
        ins = [d0, d1] + ([init] if hasattr(init, 'ap') else [])
        self.add('dve', lambda: nc.vector.tensor_tensor_scan(out, d0, d1, init, op0, op1), ins, [out], cost=0.1 + 2 * _free_elems(out) / 960.0)

    def copy(self, eng, out, in_):
        if eng == 'act':
            return self.act(out, in_, AF.Copy)
        E = self._e(eng)
        self.add(eng, lambda: E.tensor_copy(out, in_), [in_], [out], cost=self._vc(eng, out))

    def recip(self, out, in_):
        nc = self.nc
        self.add('dve', lambda: nc.vector.reciprocal(out, in_), [in_], [out], cost=0.1 + _free_elems(out) / 220.0)

    def memset(self, eng, out, val):
        E = self._e(eng)
        self.add(eng, lambda: E.memset(out, val), [], [out], cost=self._vc(eng, out))

    def mm(self, out, lhsT, rhs, start=True, stop=True):
        nc = self.nc
        self.add('pe', lambda: nc.tensor.matmul(out, lhsT, rhs, start=start, stop=stop), [lhsT, rhs], [out],
                 cost=max(0.064, 0.014 + _free_elems(rhs) / 2400.0))

    def mmg(self, out, pairs):
        n = len(pairs)
        for i, (l, r) in enumerate(pairs):
            self.mm(out, l, r, start=(i == 0), stop=(i == n - 1))

    def tr(self, out, in_, ident):
        nc = self.nc
        self.add('pe', lambda: nc.tensor.transpose(out, in_, ident), [in_, ident], [out],
                 cost=0.06 + 4 * max(_free_elems(ident), 48) / 2400.0 * (1 if in_.dtype == F32 else 0.25))

    def _vc(self, eng, out):
        n = _free_elems(out)
        if eng == 'pool':
            return 0.35 + n / 480.0
        return 0.09 + n / 960.0

    def dma(self, q, out, in_, key, slow=False):
        E = self._e(q)
        src = in_ if 'DRAM' in str(in_.space).upper() else out
        nb = 1
        for st_, c in src.ap:
            nb *= c
        nb *= 4 if src.dtype == F32 else 2
        if slow:
            self.add(q, lambda: E.dma_start(out=out, in_=in_, allow_slow_non_contiguous=True), [in_], [out], dma=key, nbytes=nb)
        else:
            self.add(q, lambda: E.dma_start(out=out, in_=in_), [in_], [out], dma=key, nbytes=nb)


class Arena:
    def __init__(self, base_ap_f32, n):
        self.base = base_ap_f32
        self.n = n
        self.pos = 0

    def reset(self, pos=0):
        self.pos = pos

    def f32(self, n, parts=128):
        a = self.base[0:parts, self.pos:self.pos + n]
        self.pos += n
        assert self.pos <= self.n, ("arena overflow", self.pos, self.n)
        return a

    def bf16(self, n, parts=128):
        m = (n + 1) // 2
        a = self.base[0:parts, self.pos:self.pos + m].bitcast(BF16)
        self.pos += m
        assert self.pos <= self.n, ("arena overflow", self.pos, self.n)
        return a


def bc(ap, axis, shape):
    return ap.unsqueeze(axis).to_broadcast(shape)


def build_nc(dbg=None, limit=None):
    nc = bass.Bass("TRN2", target_bir_lowering=False)
    dt = lambda n, s, k="ExternalInput", d=F32: nc.dram_tensor(n, s, d, kind=k).ap()
    xp = dt("xp", [SEQ, D])
    rows = dt("rows", [NR, D])
    shg = dt("shg", [2, NS, 8, 128, 128])
    cst = dt("cst", [128, 1280])
    w_mod = dt("w_mod", [2, D, 6 * D])
    wp = dt("wp", [2, D, 10240])
    lru_wa = dt("lru_wa", [2, 8, 128, 128])
    lru_wx = dt("lru_wx", [2, 8, 128, 128])
    w_out = dt("w_out", [2, D, D])
    w_ff1 = dt("w_ff1", [2, D, 4 * D])
    w_ff2p = dt("w_ff2p", [2, 8, 128, 4 * D])
    yp = dt("yp", [SEQ, D], "ExternalOutput")
    orows = dt("orows", [NRO, D], "ExternalOutput")
    hgp = dt("hgp", [2, 8, 128, 128], "ExternalOutput")
    hgs = dt("hgs", [2, NS, 8, 128, 128], "ExternalOutput")
    dbg_out = {}

    MSZ = 53200
    with ExitStack() as st:
        M = st.enter_context(nc.sbuf_tensor("M", [128, MSZ], F32))
        banks = [st.enter_context(nc.psum_tensor(f"ps{i}", [128, 512], F32)) for i in range(8)]
        P = Prog(nc)

        pos = [0]

        def carve_f32(n):
            a = M[:, pos[0]:pos[0] + n]
            pos[0] += n
            return a

        def carve_bf16(n):
            a = M[:, pos[0]:pos[0] + n // 2].bitcast(BF16)
            pos[0] += n // 2
            return a

        xT = carve_f32(8 * TH).rearrange("p (c t) -> p c t", c=8)
        hT = carve_bf16(8 * TH).rearrange("p (c t) -> p c t", c=8)
        BIGf = carve_f32(16 * TH)
        BIG = BIGf.bitcast(BF16).rearrange("p (c t) -> p c t", c=32)
        OA = BIG[:, 0:8, :]
        OB = BIG[:, 8:16, :]
        MG = BIG[:, 16:24, :]
        FF = BIG
        V = carve_f32(8 * NR).rearrange("p (c r) -> p c r", c=8)
        VO = carve_f32(8 * NRO).rearrange("p (c r) -> p c r", c=8)
        MOD = [carve_f32(48 * 17).rearrange("p (m b) -> p m b", m=48) for _ in range(2)]
        CF = carve_f32(1280)
        IDF = CF[:, 0:128]
        MASK = CF[:, 256:768]
        RESET = CF[:, 768:1280]
        CBt = carve_bf16(256)
        IDB = CBt[:, 0:128]
        ONESB = CBt[:, 128:256]
        xsT = carve_f32(8 * NS).rearrange("p (c b) -> p c b", c=8)
        hsT = carve_bf16(8 * NS).rearrange("p (c b) -> p c b", c=8)
        OAS = carve_bf16(8 * NS).rearrange("p (c b) -> p c b", c=8)
        OBS = carve_bf16(8 * NS).rearrange("p (c b) -> p c b", c=8)
        MGS = carve_bf16(8 * NS).rearrange("p (c b) -> p c b", c=8)
        FFS = carve_bf16(32 * NS).rearrange("p (c b) -> p c b", c=32)
        CSb = carve_bf16(8 * 18).rearrange("p (c b) -> p c b", c=8)
        SST = [carve_f32(8 * 128).rearrange("p (h v) -> p h v", h=8) for _ in range(2)]
        HIST = [carve_f32(24).rearrange("p (c j) -> p c j", c=8) for _ in range(2)]
        HC = [carve_f32(8) for _ in range(2)]
        A1P = [carve_f32(8) for _ in range(2)]
        A2P = [carve_f32(8) for _ in range(2)]
        A1S = [carve_f32(8 * NS).rearrange("p (c b) -> p c b", c=8) for _ in range(2)]
        A2S = [carve_f32(8 * NS).rearrange("p (c b) -> p c b", c=8) for _ in range(2)]
        LB = [carve_f32(8) for _ in range(2)]
        OML = [carve_f32(8) for _ in range(2)]
        C8 = [carve_f32(8) for _ in range(2)]
        C16 = [carve_f32(8) for _ in range(2)]
        C8H = [carve_f32(8) for _ in range(2)]
        HBA = [carve_f32(8) for _ in range(2)]
        HBX = [carve_f32(8) for _ in range(2)]
        WS = [carve_bf16(8 * 512) for _ in range(2)]
        arena_n = MSZ - pos[0]
        AR = Arena(M[:, pos[0]:MSZ], arena_n)
        AR2 = Arena(BIGf[:, 12288:16384], 4096)

        import os
        WARM = os.environ.get('K_WARM', '1') == '1'
        NROT = 7
        B8 = os.environ.get('K_B8', '0') == '1'
        if B8:
            WARM = False
        bstate = {'b': 0, 'w': 0, 'o': 0}

        def bank():
            b = banks[bstate['b'] % (8 if (B8 and bstate.get('half') == 1) else NROT)]
            bstate['b'] += 1
            return b

        PS7 = banks[7]
        P.warm = (lambda: nc.tensor.matmul(banks[7][:, 256:512], CBt[:, 128:256], CBt[:, 0:256], start=True, stop=True)) if WARM else None

        def wslot():
            i = bstate['w'] % 2
            bstate['w'] += 1
            return WS[i], f"w{i}"

        def okey():
            bstate['o'] += 1
            return f"o{bstate['o'] % 12}"

        def dump(name, ap, shape):
            if dbg is None or name not in dbg:
                return
            o = nc.dram_tensor("dbg_" + name, list(shape), ap.dtype, kind="ExternalOutput").ap()
            dbg_out[name] = o
            P.dma('sp', o, ap, 'dbg')

        P.dma('sp', CF, cst[:, :], 'ld0')
        P.dma('pool', CBt, cst[:, 0:256], 'ld1')
        for l in range(2):
            P.memset('dve', SST[l], 0.0)
            P.memset('dve', HIST[l], 0.0)
            P.memset('dve', HC[l], 0.0)
        P.memset('dve', LB[0], 0.5)
        P.memset('dve', VO, 0.0)
        P.memset('dve', OML[0], 0.5)

        AR.reset()
        RS = [AR.f32(D), AR.f32(D)]
        for ti, (r0, nr) in enumerate(((0, 128), (128, NR - 128))):
            P.dma('sp', RS[ti][0:nr, :], rows[r0:r0 + nr, :], f'ldr{ti}')
            for c in range(8):
                ps = bank()
                P.tr(ps[:, 0:nr], RS[ti][0:nr, c * 128:(c + 1) * 128], IDF[0:nr, 0:nr])
                P.copy('dve' if c % 2 else 'act', V[:, c, r0:r0 + nr], ps[:, 0:nr])
        P.copy('dve', xsT, V[:, :, R_XS:R_XS + NS])
        tmp8 = AR.f32(8)
        P.tt('dve', tmp8, V[:, :, R_HGL + 1], V[:, :, R_HGL], ALU.subtract)
        P.act(LB[1], tmp8, AF.Sigmoid)
        P.act(OML[1], tmp8, AF.Sigmoid, scale=-1.0)
        P.ts('dve', OML[1], OML[1], 0.5, None, ALU.mult)
        P.tt('dve', LB[1], LB[1], OML[1], ALU.add)
        for l in range(2):
            P.act(tmp8, V[:, :, R_LAM + l], AF.Exp, scale=-1.0)
            P.act(tmp8, tmp8, AF.Ln, bias=1.0)
            P.ts('dve', C8[l], tmp8, -8.0, None, ALU.mult)
            P.ts('dve', C16[l], tmp8, -16.0, None, ALU.mult)
            P.ts('dve', C8H[l], tmp8, -4.0, None, ALU.mult)
            P.ts('dve', HBA[l], V[:, :, R_BA + l], 0.5, None, ALU.mult)
            P.ts('dve', HBX[l], V[:, :, R_BX + l], 0.5, None, ALU.mult)
        P.act(CSb[:, :, 0:17], V[:, :, R_CP:R_CP + 17], AF.Silu)

        def wview(W):
            return W.rearrange("p (k n) -> p k n", k=8)

        def wsrc(w2d, c0, n):
            return w2d[:, c0:c0 + n].rearrange("(k p) n -> p k n", p=128)

        def mod_load(l, g):
            W, key = wslot()
            Wv = wview(W)
            P.dma('pool', Wv, wsrc(w_mod[l], g * 512, 512), key)
            return Wv

        def mod_compute(l, g, Wv):
            ps = bank()
            for fc in range(4):
                P.mmg(ps[:, fc * 32:fc * 32 + 17],
                      [(Wv[:, kc, fc * 128:(fc + 1) * 128], CSb[:, kc, 0:17]) for kc in range(8)])
            for fc in range(4):
                m = 4 * g + fc
                i, c = m // 8, m % 8
                P.act(MOD[l][:, m, :], ps[:, fc * 32:fc * 32 + 17], AF.Identity,
                      bias=V[:, c, R_BMOD + 6 * l + i:R_BMOD + 6 * l + i + 1])

        def mod_finish(l):
            P.stt(A1P[l], MOD[l][:, 8:16, 0], 1.0, V[:, :, R_N1G + l], ALU.add, ALU.mult)
            P.stt(A2P[l], MOD[l][:, 32:40, 0], 1.0, V[:, :, R_N2G + l], ALU.add, ALU.mult)
            P.stt(A1S[l], MOD[l][:, 8:16, 1:17], 1.0, bc(V[:, :, R_N1G + l], 2, [128, 8, NS]), ALU.add, ALU.mult)
            P.stt(A2S[l], MOD[l][:, 32:40, 1:17], 1.0, bc(V[:, :, R_N2G + l], 2, [128, 8, NS]), ALU.add, ALU.mult)

        def norm_prompt(Acol, sh_ap, out3):
            AR.reset()
            sq = AR.bf16(8 * TB).rearrange("p (c t) -> p c t", c=8)
            rt = AR.f32(TB)
            tmps = [AR.f32(TB), AR.f32(TB)]
            for tb in range(2):
                tsl = slice(tb * TB, (tb + 1) * TB)
                P.act(sq[:, 0:5, :], xT[:, 0:5, tsl], AF.Square)
                P.tt('dve', sq[:, 5:8, :], xT[:, 5:8, tsl], xT[:, 5:8, tsl], ALU.mult)
                ps = bank()
                P.mmg(ps[:, :], [(ONESB, sq[:, c, :]) for c in range(8)])
                P.act(rt, ps[:, :], AF.Ln, scale=1.0 / D, bias=EPS)
                P.act(rt, rt, AF.Exp, scale=-0.5)
                for c in range(8):
                    t = tmps[c % 2]
                    P.stt(t, xT[:, c, tsl], Acol[:, c:c + 1], rt, ALU.mult, ALU.mult)
                    if sh_ap is None:
                        P.copy('act', out3[:, c, tsl], t)
                    elif c in (3, 7):
                        P.ts('dve', out3[:, c, tsl], t, sh_ap[:, c:c + 1], None, ALU.add)
                    else:
                        P.act(out3[:, c, tsl], t, AF.Identity, bias=sh_ap[:, c:c + 1])

        def norm_samples(A3, sh3, out3):
            sq = AR.bf16(8 * NS).rearrange("p (c b) -> p c b", c=8)
            rt = AR.f32(NS)
            t1 = AR.f32(8 * NS).rearrange("p (c b) -> p c b", c=8)
            P.act(sq, xsT, AF.Square)
            ps = bank()
            P.mmg(ps[:, 0:NS], [(ONESB, sq[:, c, :]) for c in range(8)])
            P.act(rt, ps[:, 0:NS], AF.Ln, scale=1.0 / D, bias=EPS)
            P.act(rt, rt, AF.Exp, scale=-0.5)
            P.tt('dve', t1, xsT, bc(rt, 1, [128, 8, NS]), ALU.mult)
            P.tt('dve', t1, t1, A3, ALU.mult)
            if sh3 is None:
                P.copy('dve', out3, t1)
            else:
                P.tt('dve', out3, t1, sh3, ALU.add)

        def hgrn_load(l, h):
            W, key = wslot()
            Wv = wview(W)
            P.dma('pool', Wv, wsrc(wp[l], h * 512, 512), key)
            return Wv

        def hgrn_alloc():
            AR.reset()
            S = {}
            for n in ('F0', 'F1', 'F2', 'F3', 'osb', 'rt'):
                S[n] = AR.f32(TB)
            for n in ('q', 'kt', 'vT', 'osq'):
                S[n] = AR.bf16(TB)
            S['qh2'] = [AR.bf16(TB), AR.bf16(TB)]
            S['sog2'] = [AR.bf16(TB), AR.bf16(TB)]
            S['Dt2'] = [AR.f32(NCH), AR.f32(NCH)]
            S['AT'] = AR.bf16(TB)
            S['ktok'] = AR.bf16(NCH * 128).rearrange("p (c k) -> p c k", c=NCH)
            S['vtok'] = AR.bf16(NCH * 128).rearrange("p (c k) -> p c k", c=NCH)
            S['Ub'] = AR.f32(NCH * 128).rearrange("p (c v) -> p c v", c=NCH)
            S['SB'] = AR.bf16(NCH * 128).rearrange("p (c v) -> p c v", c=NCH)
            return S

        def hgrn_tb(l, h, tb, Wv, S):
            tsl = slice(tb * TB, (tb + 1) * TB)
            S = dict(S)
            S['qh'] = S['qh2'][tb]; S['sog'] = S['sog2'][tb]; S['Dt'] = S['Dt2'][tb]
            pss = [bank() for _ in range(4)]
            for g in range(4):
                P.mmg(pss[g][:, :], [(Wv[:, kc, g * 128:(g + 1) * 128], hT[:, kc, tsl]) for kc in range(8)])
            F0, F1, F2, F3 = (S[n] for n in ('F0', 'F1', 'F2', 'F3'))
            P.act(S['q'], pss[0][:, :], AF.Silu)
            P.act(F0, pss[1][:, :], AF.Tanh, scale=0.5)
            P.act(S['vT'], pss[2][:, :], AF.Copy)
            P.act(S['sog'], pss[3][:, :], AF.Silu)
            P.ts('dve', F0, F0, OML[l][:, h:h + 1], LB[l][:, h:h + 1], ALU.mult, ALU.add)
            P.act(F1, F0, AF.Ln)
            P.ts('dve', F0, F0, -1.0, 1.0, ALU.mult, ALU.add)
            P.scan(F2, RESET, F1, 0.0, ALU.mult, ALU.add)
            P.act(F3, F2, AF.Exp)
            P.act(F1, F2, AF.Exp, scale=-1.0)
            P.tt('dve', S['qh'], S['q'], F3, ALU.mult)
            P.tt('dve', S['kt'], F0, F1, ALU.mult)
            P.copy('dve', S['Dt'], F3[:, CH - 1::CH])
            qh, kt, AT, ktok, vtok, Ub, SB, Dt = (S[n] for n in ('qh', 'kt', 'AT', 'ktok', 'vtok', 'Ub', 'SB', 'Dt'))
            psA = bank()
            for c in range(NCH):
                cs = slice(c * CH, (c + 1) * CH)
                P.mm(psA[0:CH, cs], kt[:, cs], qh[:, cs])
            P.tt('dve', AT[0:CH, :], psA[0:CH, :], MASK[0:CH, :], ALU.mult)
            for src, dst in ((kt, ktok), (S['vT'], vtok)):
                for half in range(2):
                    psT = bank()[:, :].bitcast(BF16)
                    for i in range(8):
                        c = half * 8 + i
                        P.tr(psT[0:CH, i * 128:(i + 1) * 128], src[:, c * CH:(c + 1) * CH], IDB)
                    P.copy('act' if half else 'dve',
                           dst[0:CH, half * 8:(half + 1) * 8, :].rearrange("p c k -> p (c k)"), psT[0:CH, :])
            psUs = []
            for q4 in range(4):
                psU = bank()
                psUs.append(psU)
                for i in range(4):
                    c = 4 * q4 + i
                    P.mm(psU[:, i * 128:(i + 1) * 128], ktok[0:CH, c, :], vtok[0:CH, c, :])
            Sp = SST[l][:, h, :]
            P.copy('pool', SB[:, 0, :], Sp)
            P.tt('dve', Ub[:, 0, :], psUs[0][:, 0:128], Sp, ALU.add)
            for c in range(1, NCH):
                P.stt(Ub[:, c, :], Ub[:, c - 1, :], Dt[:, c - 1:c], psUs[c // 4][:, (c % 4) * 128:(c % 4 + 1) * 128],
                      ALU.mult, ALU.add)
            for c in range(NCH - 1):
                P.ts('pool', SB[:, c + 1, :], Ub[:, c, :], Dt[:, c:c + 1], 1.0, ALU.mult, ALU.mult)
            P.ts('dve', Sp, Ub[:, NCH - 1, :], Dt[:, NCH - 1:NCH], None, ALU.mult)
            psO = bank()
            for c in range(NCH):
                cs = slice(c * CH, (c + 1) * CH)
                P.mm(psO[:, cs], vtok[0:CH, c, :], AT[0:CH, cs], start=True, stop=False)
                P.mm(psO[:, cs], SB[:, c, :], qh[:, cs], start=False, stop=True)
            osb, rt_ = S['osb'], S['rt']
            P.act(S['osq'], psO[:, :], AF.Square)
            P.copy('dve', osb, psO[:, :])
            psS = bank()
            P.mm(psS[:, :], ONESB, S['osq'])
            P.act(rt_, psS[:, :], AF.Ln, scale=1.0 / 128, bias=EPS)
            P.act(rt_, rt_, AF.Exp, scale=-0.5)
            P.tt('dve', osb, osb, rt_, ALU.mult)
            P.stt(OA[:, h, tsl], osb, V[:, 0, R_ONORM + l:R_ONORM + l + 1], S['sog'], ALU.mult, ALU.mult)

        def hgrn_smp(l, h, Wv, AR=AR2):
            AR.reset()
            a0 = 0
            ps = bank()
            for g in range(4):
                P.mmg(ps[:, g * NS:(g + 1) * NS], [(Wv[:, kc, g * 128:(g + 1) * 128], hsT[:, kc, :]) for kc in range(8)])
            qs = AR.bf16(NS); fs = AR.f32(NS); vTs = AR.bf16(NS); sogs = AR.bf16(NS); kTs = AR.bf16(NS)
            P.act(qs, ps[:, 0:NS], AF.Silu)
            P.act(fs, ps[:, NS:2 * NS], AF.Tanh, scale=0.5)
            P.act(vTs, ps[:, 2 * NS:3 * NS], AF.Copy)
            P.act(sogs, ps[:, 3 * NS:4 * NS], AF.Silu)
            P.ts('dve', fs, fs, OML[l][:, h:h + 1], LB[l][:, h:h + 1], ALU.mult, ALU.add)
            P.ts('dve', kTs, fs, -1.0, 1.0, ALU.mult, ALU.add)
            psT = bank()[:, :].bitcast(BF16)
            P.tr(psT[0:NS, 0:128], kTs, IDB)
            P.tr(psT[0:NS, 128:256], vTs, IDB)
            kv = AR.bf16(256)
            P.copy('dve', kv[0:NS, :], psT[0:NS, 0:256])
            vmask = AR.bf16(NS * 128).rearrange("p (b v) -> p b v", b=NS)
            P.tt('dve', vmask[0:NS], bc(kv[0:NS, 128:256], 1, [NS, NS, 128]), bc(IDF[0:NS, 0:NS], 2, [NS, NS, 128]), ALU.mult)
            Sin = AR.f32(8 * 128).rearrange("p (b v) -> p b v", b=8)
            Sn = AR.f32(8 * 128).rearrange("p (b v) -> p b v", b=8)
            Snb = AR.bf16(8 * 128).rearrange("p (b v) -> p b v", b=8)
            psOs = PS7
            for bg in range(2):
                P.dma('sp', Sin, shg[l, bg * 8:(bg + 1) * 8, h].rearrange("b k v -> k b v"), 'sin')
                P.tt('dve', Sn, Sin, bc(fs[:, bg * 8:(bg + 1) * 8], 2, [128, 8, 128]), ALU.mult)
                for q in range(2):
                    psKV = bank()
                    b0 = bg * 8 + q * 4
                    P.mm(psKV[:, :], kv[0:NS, 0:128], vmask[0:NS, b0:b0 + 4, :].rearrange("p b v -> p (b v)"))
                    sl = Sn[:, q * 4:(q + 1) * 4, :].rearrange("p b v -> p (b v)")
                    P.tt('dve', sl, sl, psKV[:, :], ALU.add)
                P.copy('act', Snb, Sn)
                P.dma('sp', hgs[l, bg * 8:(bg + 1) * 8, h].rearrange("b k v -> k b v"), Sn, okey())
                for b in range(8):
                    bb = bg * 8 + b
                    P.mm(psOs[:, bb:bb + 1], Snb[:, b, :], qs[:, bb:bb + 1])
            osq = AR.bf16(NS); osb = AR.f32(NS); rt = AR.f32(NS)
            P.act(osq, psOs[:, 0:NS], AF.Square)
            P.copy('dve', osb, psOs[:, 0:NS])
            psS = bank()
            P.mm(psS[:, 0:NS], ONESB, osq)
            P.act(rt, psS[:, 0:NS], AF.Ln, scale=1.0 / 128, bias=EPS)
            P.act(rt, rt, AF.Exp, scale=-0.5)
            P.tt('dve', osb, osb, rt, ALU.mult)
            P.stt(OAS[:, h, :], osb, V[:, 0, R_ONORM + l:R_ONORM + l + 1], sogs, ALU.mult, ALU.mult)
            AR.reset(a0)

        def lru_load(l, n):
            W, key = wslot()
            Wv = wview(W)
            P.dma('pool', Wv[:, :, 0:256], wsrc(wp[l], 4096 + n * 256, 256), key)
            P.dma('pool', Wv[:, 0, 256:384], lru_wa[l, n], key)
            P.dma('pool', Wv[:, 0, 384:512], lru_wx[l, n], key)
            return Wv

        def cvec(r, n):
            return V[:, n, r:r + 1]

        def lru_alloc(A=None, nset=3):
            A = A or AR
            A.reset()
            out = []
            for _ in range(nset):
                d = {'XP': A.f32(TB + 4), 'gel': A.bf16(TB), 'xc': A.f32(TB), 'xcb': A.bf16(TB), 'r': A.f32(TB),
                     'ig': A.f32(TB), 'a': A.f32(TB), 'hs': A.f32(TB)}
                d['a2'] = d['r']
                out.append(d)
            return out

        def lru_tb(l, n, tb, Wv, S):
            tsl = slice(tb * TB, (tb + 1) * TB)
            ps1, ps2 = bank(), bank()
            P.mmg(ps1[:, :], [(Wv[:, kc, 0:128], hT[:, kc, tsl]) for kc in range(8)])
            P.mmg(ps2[:, :], [(Wv[:, kc, 128:256], hT[:, kc, tsl]) for kc in range(8)])
            XP, xc, ig, a, a2, hs = S['XP'], S['xc'], S['ig'], S['a'], S['a2'], S['hs']
            P.copy('dve', XP[:, 0:3], HIST[l][:, n, :])
            P.act(XP[:, 3:3 + TB], ps1[:, :], AF.Copy)
            P.act(S['gel'], ps2[:, :], AF.Gelu_apprx_tanh)
            P.copy('dve', HIST[l][:, n, :], XP[:, TB:TB + 3])
            P.ts('dve', xc, XP[:, 0:TB], cvec(R_CONVW + 4 * l, n), cvec(R_CONVB + l, n), ALU.mult, ALU.add)
            for j in range(1, 4):
                P.stt(xc, XP[:, j:j + TB], cvec(R_CONVW + 4 * l + j, n), xc, ALU.mult, ALU.add)
            P.copy('dve', S['xcb'], xc)
            psr, psi = bank(), bank()
            P.mm(psr[:, :], Wv[:, 0, 256:384], S['xcb'])
            P.mm(psi[:, :], Wv[:, 0, 384:512], S['xcb'])
            P.act(S['r'], psr[:, :], AF.Tanh, scale=0.5, bias=HBA[l][:, n:n + 1])
            P.act(ig, psi[:, :], AF.Tanh, scale=0.5, bias=HBX[l][:, n:n + 1])
            P.act(a, S['r'], AF.Exp, scale=C8H[l][:, n:n + 1], bias=C8H[l][:, n:n + 1])
            P.act(a2, S['r'], AF.Exp, scale=C8[l][:, n:n + 1], bias=C8[l][:, n:n + 1])
            P.act(a2, a2, AF.Ln, scale=-1.0, bias=1.0)
            P.act(a2, a2, AF.Exp, scale=0.5)
            P.stt(ig, ig, 1.0, xc, ALU.add, ALU.mult)
            P.stt(ig, ig, 0.5, a2, ALU.mult, ALU.mult)
            P.scan(hs, a, ig, HC[l][:, n:n + 1], ALU.mult, ALU.add)
            P.copy('dve', HC[l][:, n:n + 1], hs[:, TB - 1:TB])
            P.tt('pool', OB[:, n, tsl], hs, S['gel'], ALU.mult)

        def lru_smp(l, n, Wv):
            a0 = AR.pos
            ps = bank()
            P.mmg(ps[:, 0:NS], [(Wv[:, kc, 0:128], hsT[:, kc, :]) for kc in range(8)])
            P.mmg(ps[:, NS:2 * NS], [(Wv[:, kc, 128:256], hsT[:, kc, :]) for kc in range(8)])
            xbs = AR.f32(NS); gels = AR.f32(NS); xcs = AR.f32(NS); xcsb = AR.bf16(NS)
            rs = AR.f32(NS); igs = AR.f32(NS); as_ = AR.f32(NS); a2s = AR.f32(NS)
            P.act(xbs, ps[:, 0:NS], AF.Copy)
            P.act(gels, ps[:, NS:2 * NS], AF.Gelu_apprx_tanh)
            CS = V[:, n, R_SCV + 48 * l:R_SCV + 48 * l + 48].rearrange("p (b j) -> p b j", j=3)
            P.ts('dve', xcs, CS[:, :, 0], cvec(R_CONVW + 4 * l, n), cvec(R_CONVB + l, n), ALU.mult, ALU.add)
            P.stt(xcs, CS[:, :, 1], cvec(R_CONVW + 4 * l + 1, n), xcs, ALU.mult, ALU.add)
            P.stt(xcs, CS[:, :, 2], cvec(R_CONVW + 4 * l + 2, n), xcs, ALU.mult, ALU.add)
            P.stt(xcs, xbs, cvec(R_CONVW + 4 * l + 3, n), xcs, ALU.mult, ALU.add)
            P.copy('dve', xcsb, xcs)
            psg = bank()
            P.mm(psg[:, 0:NS], Wv[:, 0, 256:384], xcsb)
            P.mm(psg[:, NS:2 * NS], Wv[:, 0, 384:512], xcsb)
            P.act(rs, psg[:, 0:NS], AF.Tanh, scale=0.5, bias=HBA[l][:, n:n + 1])
            P.act(igs, psg[:, NS:2 * NS], AF.Tanh, scale=0.5, bias=HBX[l][:, n:n + 1])
            P.act(as_, rs, AF.Exp, scale=C8H[l][:, n:n + 1], bias=C8H[l][:, n:n + 1])
            P.act(a2s, rs, AF.Exp, scale=C8[l][:, n:n + 1], bias=C8[l][:, n:n + 1])
            P.act(a2s, a2s, AF.Ln, scale=-1.0, bias=1.0)
            P.act(a2s, a2s, AF.Exp, scale=0.5)
            P.stt(igs, igs, 1.0, xcs, ALU.add, ALU.mult)
            P.stt(igs, igs, 0.5, a2s, ALU.mult, ALU.mult)
            h0 = V[:, n, R_SRG + NS * l:R_SRG + NS * l + NS]
            P.tt('dve', as_, as_, h0, ALU.mult)
            hn = VO[:, n, O_LRUS + NS * l:O_LRUS + NS * l + NS]
            P.tt('dve', hn, as_, igs, ALU.add)
            P.tt('dve', OBS[:, n, :], hn, gels, ALU.mult)
            VOc = VO[:, n, O_CONVS + 48 * l:O_CONVS + 48 * l + 48].rearrange("p (b j) -> p b j", j=3)
            P.copy('dve', VOc[:, :, 0:2], CS[:, :, 1:3])
            P.copy('dve', VOc[:, :, 2], xbs)
            AR.reset(a0)

        def merge_load(l, j):
            W, key = wslot()
            Wv = wview(W)
            P.dma('pool', Wv, wsrc(wp[l], 6144 + j * 512, 512), key)
            return Wv

        def merge_compute(l, j, Wv, smp):
            AR.reset()
            sg = [AR.f32(TB), AR.f32(TB)]
            for tb in range(2):
                tsl = slice(tb * TB, (tb + 1) * TB)
                pss = [bank() for _ in range(4)]
                srcs = (hT, hT, OA, OB)
                for g in range(4):
                    P.mmg(pss[g][:, :], [(Wv[:, kc, g * 128:(g + 1) * 128], srcs[g][:, kc, tsl]) for kc in range(8)])
                P.act(sg[0], pss[0][:, :], AF.Sigmoid)
                P.act(sg[1], pss[1][:, :], AF.Sigmoid)
                P.tt('dve', sg[0], sg[0], pss[2][:, :], ALU.mult)
                P.tt('dve', sg[1], sg[1], pss[3][:, :], ALU.mult)
                P.tt('pool', MG[:, j, tsl], sg[0], sg[1], ALU.add)
            if smp:
                ps = bank()
                srcs = (hsT, hsT, OAS, OBS)
                for g in range(4):
                    P.mmg(ps[:, g * NS:(g + 1) * NS], [(Wv[:, kc, g * 128:(g + 1) * 128], srcs[g][:, kc, :]) for kc in range(8)])
                s = AR.f32(2 * NS)
                P.act(s, ps[:, 0:2 * NS], AF.Sigmoid)
                P.tt('dve', s, s, ps[:, 2 * NS:4 * NS], ALU.mult)
                P.tt('dve', MGS[:, j, :], s[:, 0:NS], s[:, NS:2 * NS], ALU.add)

        def dense_load(w2d, c0):
            W, key = wslot()
            Wv = wview(W)
            P.dma('pool', Wv, wsrc(w2d, c0, 512), key)
            return Wv

        def resid_add(l, j, gbase, ps_list, ps_s, smp):
            for tb in range(2):
                tsl = slice(tb * TB, (tb + 1) * TB)
                P.stt(xT[:, j, tsl], ps_list[tb][:, :], MOD[l][:, gbase + j, 0:1], xT[:, j, tsl], ALU.mult, ALU.add)
            if smp:
                t = AR.f32(NS)
                P.tt('dve', t, ps_s[:, 0:NS], MOD[l][:, gbase + j, 1:17], ALU.mult)
                P.tt('dve', xsT[:, j, :], xsT[:, j, :], t, ALU.add)

        def out_compute(l, jg, Wv, smp):
            AR.reset()
            for jj in range(4):
                j = jg * 4 + jj
                pl = []
                for tb in range(2):
                    tsl = slice(tb * TB, (tb + 1) * TB)
                    ps = bank()
                    P.mmg(ps[:, :], [(Wv[:, kc, jj * 128:(jj + 1) * 128], MG[:, kc, tsl]) for kc in range(8)])
                    pl.append(ps)
                ps_s = None
                if smp:
                    ps_s = bank()
                    P.mmg(ps_s[:, 0:NS], [(Wv[:, kc, jj * 128:(jj + 1) * 128], MGS[:, kc, :]) for kc in range(8)])
                resid_add(l, j, 16, pl, ps_s, smp)

        def ff1_compute(l, fg, Wv, smp):
            AR.reset()
            rl = [AR.bf16(TB), AR.bf16(TB)]
            rls = AR.bf16(NS)
            k = 0
            for jj in range(4):
                jf = fg * 4 + jj
                for tb in range(2):
                    tsl = slice(tb * TB, (tb + 1) * TB)
                    ps = bank()
                    P.mmg(ps[:, :], [(Wv[:, kc, jj * 128:(jj + 1) * 128], hT[:, kc, tsl]) for kc in range(8)])
                    P.act(rl[k % 2], ps[:, :], AF.Relu)
                    P.tt('pool' if k % 2 else 'dve', FF[:, jf, tsl], rl[k % 2], rl[k % 2], ALU.mult)
                    k += 1
                if smp:
                    ps = bank()
                    P.mmg(ps[:, 0:NS], [(Wv[:, kc, jj * 128:(jj + 1) * 128], hsT[:, kc, :]) for kc in range(8)])
                    P.act(rls, ps[:, 0:NS], AF.Relu)
                    P.tt('dve', FFS[:, jf, :], rls, rls, ALU.mult)

        def ff2_load(l, j):
            W, key = wslot()
            Wv = W.rearrange("p (k n) -> p k n", k=32)
            P.dma('pool', Wv, w_ff2p[l, j].rearrange("p (k n) -> p k n", k=32), key)
            return Wv

        def ff2_compute(l, j, Wv, smp):
            AR.reset()
            pl = []
            for tb in range(2):
                tsl = slice(tb * TB, (tb + 1) * TB)
                ps = bank()
                P.mmg(ps[:, :], [(Wv[:, kc, :], FF[:, kc, tsl]) for kc in range(32)])
                pl.append(ps)
            ps_s = None
            if smp:
                ps_s = bank()
                P.mmg(ps_s[:, 0:NS], [(Wv[:, kc, :], FFS[:, kc, :]) for kc in range(32)])
            resid_add(l, j, 40, pl, ps_s, smp)

        def load_x(hf):
            AR.reset()
            XS = [AR.f32(D) for _ in range(4)]
            for q in range(2):
                for tt_ in range(4):
                    r0 = hf * TH + q * TB + tt_ * 128
                    P.dma('sp', XS[tt_], xp[r0:r0 + 128, :], f'xin{tt_ % 2}')
                for j in range(8):
                    ps = bank()
                    for tt_ in range(4):
                        P.tr(ps[:, tt_ * 128:(tt_ + 1) * 128], XS[tt_][:, j * 128:(j + 1) * 128], IDF)
                    P.copy('act' if j % 2 else 'dve', xT[:, j, q * TB:(q + 1) * TB], ps[:, :])

        def store_y(hf):
            AR.reset()
            yT = AR.f32(8 * TB).rearrange("p (c t) -> p c t", c=8)
            sq = AR.bf16(8 * TB).rearrange("p (c t) -> p c t", c=8)
            rt = AR.f32(TB)
            YS = [AR.f32(D), AR.f32(D)]
            for tb in range(2):
                tsl = slice(tb * TB, (tb + 1) * TB)
                P.act(sq[:, 0:5, :], xT[:, 0:5, tsl], AF.Square)
                P.tt('dve', sq[:, 5:8, :], xT[:, 5:8, tsl], xT[:, 5:8, tsl], ALU.mult)
                ps = bank()
                P.mmg(ps[:, :], [(ONESB, sq[:, c, :]) for c in range(8)])
                P.act(rt, ps[:, :], AF.Ln, scale=1.0 / D, bias=EPS)
                P.act(rt, rt, AF.Exp, scale=-0.5)
                for c in range(8):
                    P.stt(yT[:, c, :], xT[:, c, tsl], V[:, c, R_FNG:R_FNG + 1], rt, ALU.mult, ALU.mult)
                for tt_ in range(4):
                    ys = YS[tt_ % 2]
                    for half in range(2):
                        ps = bank()
                        for i in range(4):
                            c = half * 4 + i
                            P.tr(ps[:, i * 128:(i + 1) * 128], yT[:, c, tt_ * 128:(tt_ + 1) * 128], IDF)
                        P.copy('act' if half else 'dve', ys[:, half * 512:(half + 1) * 512], ps[:, :])
                    r0 = hf * TH + tb * TB + tt_ * 128
                    P.dma('sp', yp[r0:r0 + 128, :], ys, okey())

        import os
        items = []
        INTER = os.environ.get('K_INTER', '0') == '1'

        def add_item(lf, cf):
            items.append((lf, cf))

        HALF1_ITEM = None
        for hf in range(2):
            smp = (hf == 0)
            if hf == 1:
                HALF1_ITEM = len(items)
            add_item(None, (lambda hf=hf: load_x(hf)))
            for l in range(2):
                if hf == 0:
                    for g in [int(x) for x in os.environ.get('K_MODG', '0,1,2,3,4,5,6,7,8,9,10,11').split(',')]:
                        add_item((lambda l=l, g=g: mod_load(l, g)), (lambda W, l=l, g=g: mod_compute(l, g, W)))
                    add_item(None, (lambda l=l: mod_finish(l)))

                def n1(l=l, smp=smp):
                    norm_prompt(A1P[l], MOD[l][:, 0:8, 0], hT)
                    if smp:
                        norm_samples(A1S[l], MOD[l][:, 0:8, 1:17], hsT)
                add_item(None, n1)
                for h in range(8):
                    def hc(W, l=l, h=h, smp=smp, hf=hf):
                        S = hgrn_alloc()
                        for tb in range(2):
                            hgrn_tb(l, h, tb, W, S)
                        if smp:
                            hgrn_smp(l, h, W)
                        if hf == 1:
                            P.dma('sp', hgp[l, h], SST[l][:, h, :], okey())
                    add_item((lambda l=l, h=h: hgrn_load(l, h)), hc)
                lru_items = []
                for n in range(8):
                    def lc(W, l=l, n=n, smp=smp, hf=hf):
                        if hf == 1 and INTER:
                            S = lru_alloc(AR2, 1)
                        else:
                            S = lru_alloc()
                        for tb in range(2):
                            lru_tb(l, n, tb, W, S[(2 * n + tb) % len(S)])
                        if smp:
                            lru_smp(l, n, W)
                        if hf == 1:
                            P.copy('dve', VO[:, n, O_LRUP + l:O_LRUP + l + 1], HC[l][:, n:n + 1])
                            P.copy('dve', VO[:, n, O_CONVP + 3 * l:O_CONVP + 3 * l + 3], HIST[l][:, n, :])
                    lru_items.append(((lambda l=l, n=n: lru_load(l, n)), lc))
                if hf == 1 and INTER:
                    hitems = items[-8:]
                    del items[-8:]
                    for hi, li in zip(hitems, lru_items):
                        items.append(hi)
                        items.append(li)
                else:
                    items.extend(lru_items)
                for j in range(8):
                    add_item((lambda l=l, j=j: merge_load(l, j)), (lambda W, l=l, j=j, smp=smp: merge_compute(l, j, W, smp)))
                for jg in range(2):
                    add_item((lambda l=l, jg=jg: dense_load(w_out[l], jg * 512)),
                             (lambda W, l=l, jg=jg, smp=smp: out_compute(l, jg, W, smp)))

                def n2(l=l, smp=smp):
                    norm_prompt(A2P[l], MOD[l][:, 24:32, 0], hT)
                    if smp:
                        norm_samples(A2S[l], MOD[l][:, 24:32, 1:17], hsT)
                add_item(None, n2)
                for fg in range(8):
                    add_item((lambda l=l, fg=fg: dense_load(w_ff1[l], fg * 512)),
                             (lambda W, l=l, fg=fg, smp=smp: ff1_compute(l, fg, W, smp)))
                for j in range(8):
                    add_item((lambda l=l, j=j: ff2_load(l, j)), (lambda W, l=l, j=j, smp=smp: ff2_compute(l, j, W, smp)))

            def fin(hf=hf, smp=smp):
                store_y(hf)
                if smp:
                    norm_samples(bc(V[:, :, R_FNG], 2, [128, 8, NS]), None, VO[:, :, O_YS:O_YS + NS])
            add_item(None, fin)

        import bisect
        loaded = {}
        load_idx = [i for i, it in enumerate(items) if it[0] is not None]
        ptr = 0
        import os
        PRE = int(os.environ.get('K_PRE', '1'))
        for i, (lf, cf) in enumerate(items):
            if limit is not None and i >= limit:
                break
            cur = bisect.bisect_right(load_idx, i)
            while ptr < min(len(load_idx), cur + PRE) and (limit is None or load_idx[ptr] < limit):
                loaded[load_idx[ptr]] = items[load_idx[ptr]][0]()
                ptr += 1
            P.mark(f"item{i}")
            if HALF1_ITEM is not None and i >= HALF1_ITEM:
                bstate['half'] = 1
            if i == HALF1_ITEM:
                P.warm_from = len(P.ops)
            if lf is not None:
                cf(loaded.pop(i))
            else:
                cf()

        AR.reset()
        OS = [AR.f32(D), AR.f32(D)]
        for ti, (r0, nr) in enumerate(((0, 128), (128, NRO - 128))):
            for half in range(2):
                ps = bank()
                for i in range(4):
                    c = half * 4 + i
                    P.tr(ps[0:nr, i * 128:(i + 1) * 128], VO[:, c, r0:r0 + nr], IDF)
                P.copy('act' if half else 'dve', OS[ti][0:nr, half * 512:(half + 1) * 512], ps[0:nr, :])
            P.dma('sp', orows[r0:r0 + nr, :], OS[ti][0:nr, :], f'orow{ti}')

        with nc.Block():
            P.emit(st, reorder=(os.environ.get('K_REORDER', '1') == '1'))
    return nc, dbg_out


_CACHE = {}


def _consts():
    c = np.zeros((128, 1280), np.float32)
    c[:, 0:128] = np.eye(128, dtype=np.float32)
    c[:, 128:256] = 1.0
    s = np.arange(CH)[:, None]
    t = np.arange(CH)[None, :]
    m = (s <= t).astype(np.float32)
    c[0:CH, 256:768] = np.tile(m, (1, NCH))
    r = np.ones(TB, np.float32)
    r[::CH] = 0.0
    c[:, 768:1280] = r[None, :]
    return c


def _pack_weights(inp):
    w_in = inp["w_in"]
    wp = np.empty((2, D, 10240), np.float32)
    for h in range(8):
        for g in range(4):
            wp[:, :, h * 512 + g * 128:h * 512 + (g + 1) * 128] = w_in[:, :, g * 1024 + h * 128:g * 1024 + (h + 1) * 128]
    for n in range(8):
        wp[:, :, 4096 + n * 256:4096 + n * 256 + 128] = w_in[:, :, 4096 + n * 128:4096 + (n + 1) * 128]
        wp[:, :, 4096 + n * 256 + 128:4096 + (n + 1) * 256] = w_in[:, :, 5120 + n * 128:5120 + (n + 1) * 128]
    for j in range(8):
        b = 6144 + j * 512
        wp[:, :, b:b + 128] = w_in[:, :, 6144 + j * 128:6144 + (j + 1) * 128]
        wp[:, :, b + 128:b + 256] = w_in[:, :, 7168 + j * 128:7168 + (j + 1) * 128]
        wp[:, :, b + 256:b + 384] = inp["w_branch_a"][:, :, j * 128:(j + 1) * 128]
        wp[:, :, b + 384:b + 512] = inp["w_branch_b"][:, :, j * 128:(j + 1) * 128]
    w2 = inp["w_ff2"].reshape(2, 32, 128, 8, 128)
    w_ff2p = np.ascontiguousarray(w2.transpose(0, 3, 2, 1, 4)).reshape(2, 8, 128, 4 * D)
    return {"wp": wp, "w_ff2p": w_ff2p}


def _pack_rows(inp, core):
    rows = np.zeros((NR, D), np.float32)
    for l in range(2):
        rows[R_N1G + l] = inp["norm1_g"][l]
        rows[R_N2G + l] = inp["norm2_g"][l]
        rows[R_CONVW + 4 * l:R_CONVW + 4 * l + 4] = inp["lru_conv_w"][l]
        rows[R_CONVB + l] = inp["lru_conv_b"][l]
        rows[R_BA + l] = inp["lru_ba"][l]
        rows[R_BX + l] = inp["lru_bx"][l]
        rows[R_LAM + l] = inp["lru_lambda"][l]
        rows[R_HGL + l] = inp["hg_lower"][l]
        rows[R_BMOD + 6 * l:R_BMOD + 6 * l + 6] = inp["b_mod"][l].reshape(6, D)
        rows[R_ONORM + l, 0:128] = inp["hg_onorm_g"][l]
        sl = slice(core * NS, (core + 1) * NS)
        rows[R_SRG + NS * l:R_SRG + NS * l + NS] = inp["state_rglru"][l, sl]
        rows[R_SCV + 48 * l:R_SCV + 48 * l + 48] = inp["state_conv"][l, sl].reshape(48, D)
    rows[R_FNG] = inp["final_norm_g"]
    rows[R_CP] = inp["c_prompt"][core]
    rows[R_CS:R_CS + NS] = inp["c_sample"][core * NS:(core + 1) * NS]
    rows[R_XS:R_XS + NS] = inp["x_sample"][core * NS:(core + 1) * NS, 0]
    return rows


def run(inputs, dbg=None, trace=False, cores=NCORE):
    inp = {k: np.ascontiguousarray(np.asarray(v, dtype=np.float32)) for k, v in inputs.items()}
    key = tuple(sorted(dbg)) if dbg else None
    if key not in _CACHE:
        _CACHE[key] = build_nc(dbg)
    nc, dbg_out = _CACHE[key]
    cst = _consts()
    shared = {k: inp[k] for k in ("w_mod", "lru_wa", "lru_wx", "w_out", "w_ff1")}
    shared.update(_pack_weights(inp))
    in_maps = []
    for c in range(cores):
        m = dict(shared)
        m["xp"] = inp["x_prompt"][c]
        m["rows"] = _pack_rows(inp, c)
        m["shg"] = np.ascontiguousarray(inp["state_hgrn"][:, c * NS:(c + 1) * NS])
        m["cst"] = cst
        in_maps.append(m)
    res = run_bass_kernel_spmd(nc, in_maps, core_ids=list(range(cores)), trace=trace)
    return res


def kernel(**inputs):
    res = run(inputs)
    R = res.results
    y_prompt = np.stack([R[c]["yp"] for c in range(NCORE)], 0)
    orow = [R[c]["orows"] for c in range(NCORE)]
    y_sample = np.concatenate([o[O_YS:O_YS + NS] for o in orow], 0)[:, None, :]
    hg_p = np.stack([R[c]["hgp"] for c in range(NCORE)], 1)
    lru_p = np.stack([o[O_LRUP:O_LRUP + 2] for o in orow], 1)
    conv_p = np.stack([o[O_CONVP:O_CONVP + 6].reshape(2, 3, D) for o in orow], 1)
    hg_s = np.concatenate([R[c]["hgs"] for c in range(NCORE)], 1)
    lru_s = np.concatenate([o[O_LRUS:O_LRUS + 2 * NS].reshape(2, NS, D) for o in orow], 1)
    conv_s = np.concatenate([o[O_CONVS:O_CONVS + 96].reshape(2, NS, 3, D) for o in orow], 1)
    f = lambda a: np.ascontiguousarray(a, dtype=np.float32)
    return (f(y_prompt), f(y_sample), f(hg_p), f(lru_p), f(conv_p), f(hg_s), f(lru_s), f(conv_s))
```

```python
import numpy as np
from contextlib import ExitStack
import concourse.bass as bass
import concourse.mybir as mybir
from concourse.bass_utils import run_bass_kernel_spmd

F32 = mybir.dt.float32
BF16 = mybir.dt.bfloat16
AF = mybir.ActivationFunctionType
ALU = mybir.AluOpType

D = 1024
SEQ = 2048
TH = 1024
TB = 512
NS = 16
NCORE = 8
EPS = 1e-6
CH = 32
NCH = TB // CH

R_N1G, R_N2G, R_CONVW, R_CONVB, R_BA, R_BX, R_LAM, R_HGL, R_FNG, R_BMOD, R_ONORM = 0, 2, 4, 12, 14, 16, 18, 20, 22, 23, 35
R_CP, R_CS, R_XS, R_SRG, R_SCV = 37, 38, 54, 70, 102
NR = 198
O_YS, O_LRUP, O_CONVP, O_LRUS, O_CONVS = 0, 16, 18, 24, 56
NRO = 152
SEG = 20000

ENGNAME = {'pe': 'tensor', 'act': 'scalar', 'dve': 'vector', 'pool': 'gpsimd', 'sp': 'sync'}


class Op:
    __slots__ = ('eng', 'fn', 'dma', 'signal', 'signo', 'we', 'wd', 'deps', 'dvals', 'cost', 'tset', 'nbytes',
                 'start', 'fin', 'idx')


def _free_elems(ap):
    n = 1
    for st, c in ap.ap[1:]:
        n *= c
    return n


class Prog:
    def __init__(self, nc):
        self.nc = nc
        self.ops = []
        self.hist = {}
        self.dma_cnt = {}
        self.last_dma = {}
        self.marks = []

    def mark(self, label):
        self.marks.append((label, len(self.ops)))

    def regs(self, aps):
        out = []
        for ap in aps:
            if ap is None or not hasattr(ap, 'ap'):
                continue
            if 'DRAM' in str(ap.space).upper():
                continue
            pat = ap.ap
            ps = pat[0][0]
            es = 4 if ap.dtype == F32 else 2
            p0 = ap.offset // ps
            f0 = ap.offset % ps
            span = sum((c - 1) * abs(st) for st, c in pat[1:]) + 1
            if 'PSUM' in str(ap.space).upper():
                out.append((ap.tensor.name, 0, 128, 0, 2048))
            else:
                out.append((ap.tensor.name, p0, p0 + pat[0][1], f0 * es, (f0 + span) * es))
        return out

    def add(self, eng, fn, ins, outs, dma=None, cost=0.2, tset=None, nbytes=0):
        idx = len(self.ops)
        R = self.regs(ins)
        W = self.regs(outs)
        deps = {}
        for (n, p0, p1, a, b) in R:
            psum = n.startswith('ps')
            for k, v in self.hist.get(n, {}).items():
                if k[0] < p1 and p0 < k[1] and k[2] < b and a < k[3]:
                    if k[5]:
                        deps[v] = True
                    elif psum and k[4] != eng:
                        for vv in v:
                            deps.setdefault(vv, False)
        for (n, p0, p1, a, b) in W:
            for k, v in self.hist.get(n, {}).items():
                if k[0] < p1 and p0 < k[1] and k[2] < b and a < k[3]:
                    if k[5]:
                        deps.setdefault(v, False)
                    else:
                        for vv in v:
                            deps.setdefault(vv, False)
        for (n, p0, p1, a, b) in W:
            L = self.hist.setdefault(n, {})
            for k in [k for k in L if p0 <= k[0] and k[1] <= p1 and a <= k[2] and k[3] <= b]:
                del L[k]
            L[(p0, p1, a, b, eng, True)] = idx
        for (n, p0, p1, a, b) in R:
            self.hist.setdefault(n, {}).setdefault((p0, p1, a, b, eng, False), []).append(idx)
        op = Op()
        op.eng = eng; op.fn = fn; op.dma = dma; op.signal = False; op.signo = 0
        op.cost = cost; op.tset = tset; op.nbytes = nbytes; op.idx = idx
        dvals = {}
        for pidx in list(deps):
            pop = self.ops[pidx]
            if pop.dma is not None:
                val = self.dma_cnt[pop.dma]
                dvals[pop.dma] = val
                deps.setdefault(self.last_dma[pop.dma], deps[pidx])
        if dma is not None:
            self.dma_cnt[dma] = self.dma_cnt.get(dma, 0) + 16
            self.last_dma[dma] = idx
        op.deps = deps
        op.dvals = dvals
        self.ops.append(op)

    def schedule(self, window=900):
        import heapq, os
        ops = self.ops
        n = len(ops)
        succ = [[] for _ in range(n)]
        nrem = [0] * n
        for i, op in enumerate(ops):
            nrem[i] = len(op.deps)
            for p in op.deps:
                succ[p].append(i)
        ready = [0.0] * n
        cp = [0.0] * n
        for i in range(n - 1, -1, -1):
            m = 0.0
            for j in succ[i]:
                if cp[j] > m:
                    m = cp[j]
            cp[i] = m + ops[i].cost + 0.12
        USECP = os.environ.get('K_CP', '0') == '1'
        inorder = {'sp': [], 'pool': []}
        for i, op in enumerate(ops):
            if op.eng in inorder:
                inorder[op.eng].append(i)
        ptr = {'sp': 0, 'pool': 0}
        pend = {e: [] for e in ('pe', 'act', 'dve')}
        avail = {e: [] for e in ('pe', 'act', 'dve')}
        free = {e: 0.0 for e in ENGNAME}
        cur_set = [None]
        dma_free = [0.0]
        done = [False] * n
        nsched = 0
        low = {e: 0 for e in ('pe', 'act', 'dve')}
        eng_ops = {e: [i for i, op in enumerate(ops) if op.eng == e] for e in ('pe', 'act', 'dve')}
        eng_ptr = {e: 0 for e in ('pe', 'act', 'dve')}
        released = [False] * n
        for i, op in enumerate(ops):
            if nrem[i] == 0 and op.eng in pend:
                heapq.heappush(pend[op.eng], (0.0, i)); released[i] = True
        LAT = 0.12
        order = []

        def tswitch(op):
            t = op.tset
            if t is None:
                return False
            c = cur_set[0]
            if t == c:
                return False
            if t == 'tanh' and c in ('silu', 'gelu', 'sigmoid'):
                return False
            return True

        while nsched < n:
            best = None
            for e in ('pe', 'act', 'dve'):
                while eng_ptr[e] < len(eng_ops[e]) and done[eng_ops[e][eng_ptr[e]]]:
                    eng_ptr[e] += 1
                lo = eng_ops[e][eng_ptr[e]] if eng_ptr[e] < len(eng_ops[e]) else n
                P_, A_ = pend[e], avail[e]
                while P_ and P_[0][0] <= free[e]:
                    r, i = heapq.heappop(P_)
                    heapq.heappush(A_, i)
                cand = None
                if A_:
                    if USECP:
                        inw = [j for j in A_ if j <= lo + window]
                        i = None
                        if inw:
                            if e == 'act':
                                ns = [j for j in inw if not tswitch(ops[j])]
                                best_ns = max(ns, key=lambda j: cp[j]) if ns else None
                                best_all = max(inw, key=lambda j: cp[j])
                                i = best_ns if (best_ns is not None and cp[best_ns] + 6.0 >= cp[best_all]) else best_all
                            else:
                                i = max(inw, key=lambda j: cp[j])
                    else:
                        i = A_[0]
                        if i > lo + window:
                            i = None
                        if i is not None and e == 'act' and tswitch(ops[i]):
                            alt = [j for j in heapq.nsmallest(6, A_) if j <= lo + window and not tswitch(ops[j])]
                            if alt:
                                i = alt[0]
                    if i is not None:
                        cand = (free[e], i)
                if cand is None and P_:
                    r, i = P_[0]
                    if i <= lo + window:
                        cand = (r, i)
                    else:
                        j = min(P_, key=lambda x: x[1])
                        cand = (max(j[0], free[e]), j[1])
                if cand is not None and (best is None or (cand[0], cand[1]) < (best[0], best[1])):
                    best = (cand[0], cand[1], e)
            for e in ('sp', 'pool'):
                if ptr[e] < len(inorder[e]):
                    i = inorder[e][ptr[e]]
                    if nrem[i] == 0:
                        st_ = max(free[e], ready[i])
                        if best is None or (st_, i) < (best[0], best[1]):
                            best = (st_, i, e)
            assert best is not None, "scheduler deadlock"
            st_, i, e = best
            op = ops[i]
            if e in pend:
                if i in avail[e]:
                    avail[e].remove(i); heapq.heapify(avail[e])
                else:
                    pend[e].remove((ready[i], i)); heapq.heapify(pend[e])
            else:
                ptr[e] += 1
            c = op.cost
            if e == 'act' and op.tset is not None:
                if tswitch(op):
                    c += 1.3
                    cur_set[0] = 'silu' if op.tset == 'tanh' else op.tset
            op.start = st_
            if op.dma is not None:
                free[e] = st_ + 0.06
                t0 = max(st_, dma_free[0])
                dma_free[0] = t0 + op.nbytes / 300e3
                op.fin = dma_free[0] + 2.0
            else:
                free[e] = st_ + c
                op.fin = st_ + c
            done[i] = True
            nsched += 1
            order.append(i)
            for sidx in succ[i]:
                nrem[sidx] -= 1
                ready[sidx] = max(ready[sidx], op.fin + LAT)
                if nrem[sidx] == 0 and ops[sidx].eng in pend:
                    heapq.heappush(pend[ops[sidx].eng], (ready[sidx], sidx))
        self.ndummy = {}
        last_fin = None
        for i in order:
            op = ops[i]
            if op.eng != 'pe':
                continue
            if last_fin is not None:
                gap = op.start - last_fin
                if gap > 1.0 and i >= getattr(self, 'warm_from', 1 << 60):
                    self.ndummy[i] = min(int(gap * 0.6 / 0.13), 24)
            last_fin = op.fin
        self.order = order
        self.makespan = max(op.fin for op in ops)
        return order

    def emit(self, stack, reorder=True):
        import os
        nc = self.nc
        ops = self.ops
        if reorder:
            order = self.schedule()
        else:
            order = list(range(len(ops)))
        seen = {e: {} for e in ENGNAME}
        for i in order:
            op = ops[i]
            eng = op.eng
            sn = seen[eng]
            we = {}
            wd = {}
            for k, v in op.dvals.items():
                if sn.get(('d', k), 0) < v:
                    wd[k] = v
            for pidx, raw in op.deps.items():
                pop = ops[pidx]
                if pop.dma is not None:
                    continue
                pe_ = pop.eng
                if pe_ == eng and (eng == 'pe' or not raw):
                    continue
                we.setdefault(pe_, []).append(pidx)
            op.we = we
            op.wd = wd
            for k, v in wd.items():
                sn[('d', k)] = v
        pos = {}
        cnt = {e: 0 for e in ENGNAME}
        for i in order:
            cnt[ops[i].eng] += 1
            pos[i] = cnt[ops[i].eng]
        seenp = {e: {} for e in ENGNAME}
        for i in order:
            op = ops[i]
            red = {}
            for pe_, lst in op.we.items():
                p = max(lst, key=lambda x: pos[x])
                if seenp[op.eng].get(pe_, 0) < pos[p]:
                    red[pe_] = p
                    seenp[op.eng][pe_] = pos[p]
                    ops[p].signal = True
            op.we = red
        cnt = {e: 0 for e in ENGNAME}
        for i in order:
            op = ops[i]
            if op.dma is None and op.signal:
                cnt[op.eng] += 1
                op.signo = cnt[op.eng]
        esem = {}
        for e in ENGNAME:
            esem[e] = [stack.enter_context(nc.semaphore(f"s_{e}{i}")) for i in range(cnt[e] // SEG + 1)]
        dsem = {k: stack.enter_context(nc.semaphore(f"d_{k}")) for k in self.dma_cnt}
        mk = dict((i, l) for l, i in self.marks)
        mout = []
        names = {}
        for oi in order:
            op = ops[oi]
            E = getattr(nc, ENGNAME[op.eng])
            if getattr(self, 'warm', None) is not None and reorder:
                for _ in range(self.ndummy.get(oi, 0)):
                    self.warm()
            for pe_, pidx in op.we.items():
                sn = ops[pidx].signo - 1
                E.wait_ge(esem[pe_][sn // SEG], sn % SEG + 1)
            for k, v in op.wd.items():
                E.wait_ge(dsem[k], v)
            ins = op.fn()
            if os.environ.get('K_MARKS'):
                try:
                    names[oi] = str(ins.ins.name)
                except Exception as ex:
                    names[oi] = '?'
            if op.dma is not None:
                ins.then_inc(dsem[op.dma], 16)
            elif op.signal:
                sn = op.signo - 1
                ins.then_inc(esem[op.eng][sn // SEG], 1)
        for k, v in self.dma_cnt.items():
            nc.sync.wait_ge(dsem[k], v)
        if os.environ.get('K_MARKS'):
            import json
            json.dump({'marks': self.marks, 'names': names, 'makespan': getattr(self, 'makespan', None)},
                      open(os.environ['K_MARKS'], 'w'))

    def act(self, out, in_, func, scale=1.0, bias=None):
        nc = self.nc
        ins = [in_]
        kw = {}
        if hasattr(scale, 'ap'):
            ins.append(scale)
        if bias is not None:
            kw['bias'] = bias
            if hasattr(bias, 'ap'):
                ins.append(bias)
        tset = {AF.Silu: 'silu', AF.Tanh: 'tanh', AF.Ln: 'lnexp', AF.Exp: 'lnexp', AF.Sigmoid: 'sigmoid',
                AF.Gelu_apprx_tanh: 'gelu'}.get(func)
        self.add('act', lambda: nc.scalar.activation(out=out, in_=in_, func=func, scale=scale, **kw), ins, [out],
                 cost=0.25 + 0.1 * (len(ins) - 1) + _free_elems(in_) / 1200.0, tset=tset)

    def _e(self, eng):
        return getattr(self.nc, ENGNAME[eng])

    def tt(self, eng, out, in0, in1, op):
        E = self._e(eng)
        self.add(eng, lambda: E.tensor_tensor(out, in0, in1, op), [in0, in1], [out], cost=self._vc(eng, out))

    def ts(self, eng, out, in0, s1, s2, op0, op1=None):
        E = self._e(eng)
        ins = [in0] + [s for s in (s1, s2) if hasattr(s, 'ap')]
        if op1 is None:
            self.add(eng, lambda: E.tensor_scalar(out, in0, s1, None, op0), ins, [out], cost=self._vc(eng, out))
        else:
            self.add(eng, lambda: E.tensor_scalar(out, in0, s1, s2, op0, op1), ins, [out], cost=self._vc(eng, out))

    def stt(self, out, in0, scalar, in1, op0, op1):
        nc = self.nc
        ins = [in0, in1] + ([scalar] if hasattr(scalar, 'ap') else [])
        self.add('dve', lambda: nc.vector.scalar_tensor_tensor(out, in0, scalar, in1, op0, op1), ins, [out], cost=self._vc('dve', out))

    def scan(self, out, d0, d1, init, op0, op1):
        nc = self.nc
        ins = [d0, d1] + ([init] if hasattr(init, 'ap') else [])
        self.add('dve', lambda: nc.vector.tensor_tensor_scan(out, d0, d1, init, op0, op1), ins, [out], cost=0.1 + 2 * _free_elems(out) / 960.0)

    def copy(self, eng, out, in_):
        if eng == 'act':
            return self.act(out, in_, AF.Copy)
        E = self._e(eng)
        self.add(eng, lambda: E.tensor_copy(out, in_), [in_], [out], cost=self._vc(eng, out))

    def recip(self, out, in_):
        nc = self.nc
        self.add('dve', lambda: nc.vector.reciprocal(out, in_), [in_], [out], cost=0.1 + _free_elems(out) / 220.0)

    def memset(self, eng, out, val):
        E = self._e(eng)
        self.add(eng, lambda: E.memset(out, val), [], [out], cost=self._vc(eng, out))

    def mm(self, out, lhsT, rhs, start=True, stop=True):
        nc = self.nc
        self.add('pe', lambda: nc.tensor.matmul(out, lhsT, rhs, start=start, stop=stop), [lhsT, rhs], [out],
                 cost=max(0.064, 0.014 + _free_elems(rhs) / 2400.0))

    def mmg(self, out, pairs):
        n = len(pairs)
        for i, (l, r) in enumerate(pairs):
            self.mm(out, l, r, start=(i == 0), stop=(i == n - 1))

    def tr(self, out, in_, ident):
        nc = self.nc
        self.add('pe', lambda: nc.tensor.transpose(out, in_, ident), [in_, ident], [out],
                 cost=0.06 + 4 * max(_free_elems(ident), 48) / 2400.0 * (1 if in_.dtype == F32 else 0.25))

    def _vc(self, eng, out):
        n = _free_elems(out)
        if eng == 'pool':
            return 0.35 + n / 480.0
        return 0.09 + n / 960.0

    def dma(self, q, out, in_, key, slow=False):
        E = self._e(q)
        src = in_ if 'DRAM' in str(in_.space).upper() else out
        nb = 1
        for st_, c in src.ap:
            nb *= c
        nb *= 4 if src.dtype == F32 else 2
        if slow:
            self.add(q, lambda: E.dma_start(out=out, in_=in_, allow_slow_non_contiguous=True), [in_], [out], dma=key, nbytes=nb)
        else:
            self.add(q, lambda: E.dma_start(out=out, in_=in_), [in_], [out], dma=key, nbytes=nb)


class Arena:
    def __init__(self, base_ap_f32, n):
        self.base = base_ap_f32
        self.n = n
        self.pos = 0

    def reset(self, pos=0):
        self.pos = pos

    def f32(self, n, parts=128):
        a = self.base[0:parts, self.pos:self.pos + n]
        self.pos += n
        assert self.pos <= self.n, ("arena overflow", self.pos, self.n)
        return a

    def bf16(self, n, parts=128):
        m = (n + 1) // 2
        a = self.base[0:parts, self.pos:self.pos + m].bitcast(BF16)
        self.pos += m
        assert self.pos <= self.n, ("arena overflow", self.pos, self.n)
        return a


def bc(ap, axis, shape):
    return ap.unsqueeze(axis).to_broadcast(shape)


def build_nc(dbg=None, limit=None):
    nc = bass.Bass("TRN2", target_bir_lowering=False)
    dt = lambda n, s, k="ExternalInput", d=F32: nc.dram_tensor(n, s, d, kind=k).ap()
    xp = dt("xp", [SEQ, D])
    rows = dt("rows", [NR, D])
    shg = dt("shg", [2, NS, 8, 128, 128])
    cst = dt("cst", [128, 1280])
    w_mod = dt("w_mod", [2, D, 6 * D])
    wp = dt("wp", [2, D, 10240])
    lru_wa = dt("lru_wa", [2, 8, 128, 128])
    lru_wx = dt("lru_wx", [2, 8, 128, 128])
    w_out = dt("w_out", [2, D, D])
    w_ff1 = dt("w_ff1", [2, D, 4 * D])
    w_ff2p = dt("w_ff2p", [2, 8, 128, 4 * D])
    yp = dt("yp", [SEQ, D], "ExternalOutput")
    orows = dt("orows", [NRO, D], "ExternalOutput")
    hgp = dt("hgp", [2, 8, 128, 128], "ExternalOutput")
    hgs = dt("hgs", [2, NS, 8, 128, 128], "ExternalOutput")
    dbg_out = {}

    MSZ = 53200
    with ExitStack() as st:
        M = st.enter_context(nc.sbuf_tensor("M", [128, MSZ], F32))
        banks = [st.enter_context(nc.psum_tensor(f"ps{i}", [128, 512], F32)) for i in range(8)]
        P = Prog(nc)

        pos = [0]

        def carve_f32(n):
            a = M[:, pos[0]:pos[0] + n]
            pos[0] += n
            return a

        def carve_bf16(n):
            a = M[:, pos[0]:pos[0] + n // 2].bitcast(BF16)
            pos[0] += n // 2
            return a

        xT = carve_f32(8 * TH).rearrange("p (c t) -> p c t", c=8)
        hT = carve_bf16(8 * TH).rearrange("p (c t) -> p c t", c=8)
        BIGf = carve_f32(16 * TH)
        BIG = BIGf.bitcast(BF16).rearrange("p (c t) -> p c t", c=32)
        OA = BIG[:, 0:8, :]
        OB = BIG[:, 8:16, :]
        MG = BIG[:, 16:24, :]
        FF = BIG
        V = carve_f32(8 * NR).rearrange("p (c r) -> p c r", c=8)
        VO = carve_f32(8 * NRO).rearrange("p (c r) -> p c r", c=8)
        MOD = [carve_f32(48 * 17).rearrange("p (m b) -> p m b", m=48) for _ in range(2)]
        CF = carve_f32(1280)
        IDF = CF[:, 0:128]
        MASK = CF[:, 256:768]
        RESET = CF[:, 768:1280]
        CBt = carve_bf16(256)
        IDB = CBt[:, 0:128]
        ONESB = CBt[:, 128:256]
        xsT = carve_f32(8 * NS).rearrange("p (c b) -> p c b", c=8)
        hsT = carve_bf16(8 * NS).rearrange("p (c b) -> p c b", c=8)
        OAS = carve_bf16(8 * NS).rearrange("p (c b) -> p c b", c=8)
        OBS = carve_bf16(8 * NS).rearrange("p (c b) -> p c b", c=8)
        MGS = carve_bf16(8 * NS).rearrange("p (c b) -> p c b", c=8)
        FFS = carve_bf16(32 * NS).rearrange("p (c b) -> p c b", c=32)
        CSb = carve_bf16(8 * 18).rearrange("p (c b) -> p c b", c=8)
        SST = [carve_f32(8 * 128).rearrange("p (h v) -> p h v", h=8) for _ in range(2)]
        HIST = [carve_f32(24).rearrange("p (c j) -> p c j", c=8) for _ in range(2)]
        HC = [carve_f32(8) for _ in range(2)]
        A1P = [carve_f32(8) for _ in range(2)]
        A2P = [carve_f32(8) for _ in range(2)]
        A1S = [carve_f32(8 * NS).rearrange("p (c b) -> p c b", c=8) for _ in range(2)]
        A2S = [carve_f32(8 * NS).rearrange("p (c b) -> p c b", c=8) for _ in range(2)]
        LB = [carve_f32(8) for _ in range(2)]
        OML = [carve_f32(8) for _ in range(2)]
        C8 = [carve_f32(8) for _ in range(2)]
        C16 = [carve_f32(8) for _ in range(2)]
        C8H = [carve_f32(8) for _ in range(2)]
        HBA = [carve_f32(8) for _ in range(2)]
        HBX = [carve_f32(8) for _ in range(2)]
        WS = [carve_bf16(8 * 512) for _ in range(2)]
        arena_n = MSZ - pos[0]
        AR = Arena(M[:, pos[0]:MSZ], arena_n)
        AR2 = Arena(BIGf[:, 12288:16384], 4096)

        import os
        WARM = os.environ.get('K_WARM', '1') == '1'
        NROT = 7
        B8 = os.environ.get('K_B8', '0') == '1'
        if B8:
            WARM = False
        bstate = {'b': 0, 'w': 0, 'o': 0}

        def bank():
            b = banks[bstate['b'] % (8 if (B8 and bstate.get('half') == 1) else NROT)]
            bstate['b'] += 1
            return b

        PS7 = banks[7]
        P.warm = (lambda: nc.tensor.matmul(banks[7][:, 256:512], CBt[:, 128:256], CBt[:, 0:256], start=True, stop=True)) if WARM else None

        def wslot():
            i = bstate['w'] % 2
            bstate['w'] += 1
            return WS[i], f"w{i}"

        def okey():
            bstate['o'] += 1
            return f"o{bstate['o'] % 12}"

        def dump(name, ap, shape):
            if dbg is None or name not in dbg:
                return
            o = nc.dram_tensor("dbg_" + name, list(shape), ap.dtype, kind="ExternalOutput").ap()
            dbg_out[name] = o
            P.dma('sp', o, ap, 'dbg')

        P.dma('sp', CF, cst[:, :], 'ld0')
        P.dma('pool', CBt, cst[:, 0:256], 'ld1')
        for l in range(2):
            P.memset('dve', SST[l], 0.0)
            P.memset('dve', HIST[l], 0.0)
            P.memset('dve', HC[l], 0.0)
        P.memset('dve', LB[0], 0.5)
        P.memset('dve', VO, 0.0)
        P.memset('dve', OML[0], 0.5)

        AR.reset()
        RS = [AR.f32(D), AR.f32(D)]
        for ti, (r0, nr) in enumerate(((0, 128), (128, NR - 128))):
            P.dma('sp', RS[ti][0:nr, :], rows[r0:r0 + nr, :], f'ldr{ti}')
            for c in range(8):
                ps = bank()
                P.tr(ps[:, 0:nr], RS[ti][0:nr, c * 128:(c + 1) * 128], IDF[0:nr, 0:nr])
                P.copy('dve' if c % 2 else 'act', V[:, c, r0:r0 + nr], ps[:, 0:nr])
        P.copy('dve', xsT, V[:, :, R_XS:R_XS + NS])
        tmp8 = AR.f32(8)
        P.tt('dve', tmp8, V[:, :, R_HGL + 1], V[:, :, R_HGL], ALU.subtract)
        P.act(LB[1], tmp8, AF.Sigmoid)
        P.act(OML[1], tmp8, AF.Sigmoid, scale=-1.0)
        P.ts('dve', OML[1], OML[1], 0.5, None, ALU.mult)
        P.tt('dve', LB[1], LB[1], OML[1], ALU.add)
        for l in range(2):
            P.act(tmp8, V[:, :, R_LAM + l], AF.Exp, scale=-1.0)
            P.act(tmp8, tmp8, AF.Ln, bias=1.0)
            P.ts('dve', C8[l], tmp8, -8.0, None, ALU.mult)
            P.ts('dve', C16[l], tmp8, -16.0, None, ALU.mult)
            P.ts('dve', C8H[l], tmp8, -4.0, None, ALU.mult)
            P.ts('dve', HBA[l], V[:, :, R_BA + l], 0.5, None, ALU.mult)
            P.ts('dve', HBX[l], V[:, :, R_BX + l], 0.5, None, ALU.mult)
        P.act(CSb[:, :, 0:17], V[:, :, R_CP:R_CP + 17], AF.Silu)

        def wview(W):
            return W.rearrange("p (k n) -> p k n", k=8)

        def wsrc(w2d, c0, n):
            return w2d[:, c0:c0 + n].rearrange("(k p) n -> p k n", p=128)

        def mod_load(l, g):
            W, key = wslot()
            Wv = wview(W)
            P.dma('pool', Wv, wsrc(w_mod[l], g * 512, 512), key)
            return Wv

        def mod_compute(l, g, Wv):
            ps = bank()
            for fc in range(4):
                P.mmg(ps[:, fc * 32:fc * 32 + 17],
                      [(Wv[:, kc, fc * 128:(fc + 1) * 128], CSb[:, kc, 0:17]) for kc in range(8)])
            for fc in range(4):
                m = 4 * g + fc
                i, c = m // 8, m % 8
                P.act(MOD[l][:, m, :], ps[:, fc * 32:fc * 32 + 17], AF.Identity,
                      bias=V[:, c, R_BMOD + 6 * l + i:R_BMOD + 6 * l + i + 1])

        def mod_finish(l):
            P.stt(A1P[l], MOD[l][:, 8:16, 0], 1.0, V[:, :, R_N1G + l], ALU.add, ALU.mult)
            P.stt(A2P[l], MOD[l][:, 32:40, 0], 1.0, V[:, :, R_N2G + l], ALU.add, ALU.mult)
            P.stt(A1S[l], MOD[l][:, 8:16, 1:17], 1.0, bc(V[:, :, R_N1G + l], 2, [128, 8, NS]), ALU.add, ALU.mult)
            P.stt(A2S[l], MOD[l][:, 32:40, 1:17], 1.0, bc(V[:, :, R_N2G + l], 2, [128, 8, NS]), ALU.add, ALU.mult)

        def norm_prompt(Acol, sh_ap, out3):
            AR.reset()
            sq = AR.bf16(8 * TB).rearrange("p (c t) -> p c t", c=8)
            rt = AR.f32(TB)
            tmps = [AR.f32(TB), AR.f32(TB)]
            for tb in range(2):
                tsl = slice(tb * TB, (tb + 1) * TB)
                P.act(sq[:, 0:5, :], xT[:, 0:5, tsl], AF.Square)
                P.tt('dve', sq[:, 5:8, :], xT[:, 5:8, tsl], xT[:, 5:8, tsl], ALU.mult)
                ps = bank()
                P.mmg(ps[:, :], [(ONESB, sq[:, c, :]) for c in range(8)])
                P.act(rt, ps[:, :], AF.Ln, scale=1.0 / D, bias=EPS)
                P.act(rt, rt, AF.Exp, scale=-0.5)
                for c in range(8):
                    t = tmps[c % 2]
                    P.stt(t, xT[:, c, tsl], Acol[:, c:c + 1], rt, ALU.mult, ALU.mult)
                    if sh_ap is None:
                        P.copy('act', out3[:, c, tsl], t)
                    elif c in (3, 7):
                        P.ts('dve', out3[:, c, tsl], t, sh_ap[:, c:c + 1], None, ALU.add)
                    else:
                        P.act(out3[:, c, tsl], t, AF.Identity, bias=sh_ap[:, c:c + 1])

        def norm_samples(A3, sh3, out3):
            sq = AR.bf16(8 * NS).rearrange("p (c b) -> p c b", c=8)
            rt = AR.f32(NS)
            t1 = AR.f32(8 * NS).rearrange("p (c b) -> p c b", c=8)
            P.act(sq, xsT, AF.Square)
            ps = bank()
            P.mmg(ps[:, 0:NS], [(ONESB, sq[:, c, :]) for c in range(8)])
            P.act(rt, ps[:, 0:NS], AF.Ln, scale=1.0 / D, bias=EPS)
            P.act(rt, rt, AF.Exp, scale=-0.5)
            P.tt('dve', t1, xsT, bc(rt, 1, [128, 8, NS]), ALU.mult)
            P.tt('dve', t1, t1, A3, ALU.mult)
            if sh3 is None:
                P.copy('dve', out3, t1)
            else:
                P.tt('dve', out3, t1, sh3, ALU.add)

        def hgrn_load(l, h):
            W, key = wslot()
            Wv = wview(W)
            P.dma('pool', Wv, wsrc(wp[l], h * 512, 512), key)
            return Wv

        def hgrn_alloc():
            AR.reset()
            S = {}
            for n in ('F0', 'F1', 'F2', 'F3', 'osb', 'rt'):
                S[n] = AR.f32(TB)
            for n in ('q', 'kt', 'vT', 'osq'):
                S[n] = AR.bf16(TB)
            S['qh2'] = [AR.bf16(TB), AR.bf16(TB)]
            S['sog2'] = [AR.bf16(TB), AR.bf16(TB)]
            S['Dt2'] = [AR.f32(NCH), AR.f32(NCH)]
            S['AT'] = AR.bf16(TB)
            S['ktok'] = AR.bf16(NCH * 128).rearrange("p (c k) -> p c k", c=NCH)
            S['vtok'] = AR.bf16(NCH * 128).rearrange("p (c k) -> p c k", c=NCH)
            S['Ub'] = AR.f32(NCH * 128).rearrange("p (c v) -> p c v", c=NCH)
            S['SB'] = AR.bf16(NCH * 128).rearrange("p (c v) -> p c v", c=NCH)
            return S

        def hgrn_tb(l, h, tb, Wv, S):
            tsl = slice(tb * TB, (tb + 1) * TB)
            S = dict(S)
            S['qh'] = S['qh2'][tb]; S['sog'] = S['sog2'][tb]; S['Dt'] = S['Dt2'][tb]
            pss = [bank() for _ in range(4)]
            for g in range(4):
                P.mmg(pss[g][:, :], [(Wv[:, kc, g * 128:(g + 1) * 128], hT[:, kc, tsl]) for kc in range(8)])
            F0, F1, F2, F3 = (S[n] for n in ('F0', 'F1', 'F2', 'F3'))
            P.act(S['q'], pss[0][:, :], AF.Silu)
            P.act(F0, pss[1][:, :], AF.Tanh, scale=0.5)
            P.act(S['vT'], pss[2][:, :], AF.Copy)
            P.act(S['sog'], pss[3][:, :], AF.Silu)
            P.ts('dve', F0, F0, OML[l][:, h:h + 1], LB[l][:, h:h + 1], ALU.mult, ALU.add)
            P.act(F1, F0, AF.Ln)
            P.ts('dve', F0, F0, -1.0, 1.0, ALU.mult, ALU.add)
            P.scan(F2, RESET, F1, 0.0, ALU.mult, ALU.add)
            P.act(F3, F2, AF.Exp)
            P.act(F1, F2, AF.Exp, scale=-1.0)
            P.tt('dve', S['qh'], S['q'], F3, ALU.mult)
            P.tt('dve', S['kt'], F0, F1, ALU.mult)
            P.copy('dve', S['Dt'], F3[:, CH - 1::CH])
            qh, kt, AT, ktok, vtok, Ub, SB, Dt = (S[n] for n in ('qh', 'kt', 'AT', 'ktok', 'vtok', 'Ub', 'SB', 'Dt'))
            psA = bank()
            for c in range(NCH):
                cs = slice(c * CH, (c + 1) * CH)
                P.mm(psA[0:CH, cs], kt[:, cs], qh[:, cs])
            P.tt('dve', AT[0:CH, :], psA[0:CH, :], MASK[0:CH, :], ALU.mult)
            for src, dst in ((kt, ktok), (S['vT'], vtok)):
                for half in range(2):
                    psT = bank()[:, :].bitcast(BF16)
                    for i in range(8):
                        c = half * 8 + i
                        P.tr(psT[0:CH, i * 128:(i + 1) * 128], src[:, c * CH:(c + 1) * CH], IDB)
                    P.copy('act' if half else 'dve',
                           dst[0:CH, half * 8:(half + 1) * 8, :].rearrange("p c k -> p (c k)"), psT[0:CH, :])
            psUs = []
            for q4 in range(4):
                psU = bank()
                psUs.append(psU)
                for i in range(4):
                    c = 4 * q4 + i
                    P.mm(psU[:, i * 128:(i + 1) * 128], ktok[0:CH, c, :], vtok[0:CH, c, :])
            Sp = SST[l][:, h, :]
            P.copy('pool', SB[:, 0, :], Sp)
            P.tt('dve', Ub[:, 0, :], psUs[0][:, 0:128], Sp, ALU.add)
            for c in range(1, NCH):
                P.stt(Ub[:, c, :], Ub[:, c - 1, :], Dt[:, c - 1:c], psUs[c // 4][:, (c % 4) * 128:(c % 4 + 1) * 128],
                      ALU.mult, ALU.add)
            for c in range(NCH - 1):
                if c % 2 == 0:
                    P.act(SB[:, c + 1, :], Ub[:, c, :], AF.Identity, scale=Dt[:, c:c + 1])
                else:
                    P.ts('pool', SB[:, c + 1, :], Ub[:, c, :], Dt[:, c:c + 1], 1.0, ALU.mult, ALU.mult)
            P.ts('dve', Sp, Ub[:, NCH - 1, :], Dt[:, NCH - 1:NCH], None, ALU.mult)
            psO = bank()
            for c in range(NCH):
                cs = slice(c * CH, (c + 1) * CH)
                P.mm(psO[:, cs], vtok[0:CH, c, :], AT[0:CH, cs], start=True, stop=False)
                P.mm(psO[:, cs], SB[:, c, :], qh[:, cs], start=False, stop=True)
            osb, rt_ = S['osb'], S['rt']
            P.act(S['osq'], psO[:, :], AF.Square)
            P.copy('dve', osb, psO[:, :])
            psS = bank()
            P.mm(psS[:, :], ONESB, S['osq'])
            P.act(rt_, psS[:, :], AF.Ln, scale=1.0 / 128, bias=EPS)
            P.act(rt_, rt_, AF.Exp, scale=-0.5)
            P.tt('dve', osb, osb, rt_, ALU.mult)
            P.stt(OA[:, h, tsl], osb, V[:, 0, R_ONORM + l:R_ONORM + l + 1], S['sog'], ALU.mult, ALU.mult)

        def hgrn_smp(l, h, Wv, AR=AR2):
            AR.reset()
            a0 = 0
            ps = bank()
            for g in range(4):
                P.mmg(ps[:, g * NS:(g + 1) * NS], [(Wv[:, kc, g * 128:(g + 1) * 128], hsT[:, kc, :]) for kc in range(8)])
            qs = AR.bf16(NS); fs = AR.f32(NS); vTs = AR.bf16(NS); sogs = AR.bf16(NS); kTs = AR.bf16(NS)
            P.act(qs, ps[:, 0:NS], AF.Silu)
            P.act(fs, ps[:, NS:2 * NS], AF.Tanh, scale=0.5)
            P.act(vTs, ps[:, 2 * NS:3 * NS], AF.Copy)
            P.act(sogs, ps[:, 3 * NS:4 * NS], AF.Silu)
            P.ts('dve', fs, fs, OML[l][:, h:h + 1], LB[l][:, h:h + 1], ALU.mult, ALU.add)
            P.ts('dve', kTs, fs, -1.0, 1.0, ALU.mult, ALU.add)
            psT = bank()[:, :].bitcast(BF16)
            P.tr(psT[0:NS, 0:128], kTs, IDB)
            P.tr(psT[0:NS, 128:256], vTs, IDB)
            kv = AR.bf16(256)
            P.copy('dve', kv[0:NS, :], psT[0:NS, 0:256])
            vmask = AR.bf16(NS * 128).rearrange("p (b v) -> p b v", b=NS)
            P.tt('dve', vmask[0:NS], bc(kv[0:NS, 128:256], 1, [NS, NS, 128]), bc(IDF[0:NS, 0:NS], 2, [NS, NS, 128]), ALU.mult)
            Sin = AR.f32(8 * 128).rearrange("p (b v) -> p b v", b=8)
            Sn = AR.f32(8 * 128).rearrange("p (b v) -> p b v", b=8)
            Snb = AR.bf16(8 * 128).rearrange("p (b v) -> p b v", b=8)
            psOs = PS7
            for bg in range(2):
                P.dma('sp', Sin, shg[l, bg * 8:(bg + 1) * 8, h].rearrange("b k v -> k b v"), 'sin')
                P.tt('dve', Sn, Sin, bc(fs[:, bg * 8:(bg + 1) * 8], 2, [128, 8, 128]), ALU.mult)
                for q in range(2):
                    psKV = bank()
                    b0 = bg * 8 + q * 4
                    P.mm(psKV[:, :], kv[0:NS, 0:128], vmask[0:NS, b0:b0 + 4, :].rearrange("p b v -> p (b v)"))
                    sl = Sn[:, q * 4:(q + 1) * 4, :].rearrange("p b v -> p (b v)")
                    P.tt('dve', sl, sl, psKV[:, :], ALU.add)
                P.copy('act', Snb, Sn)
                P.dma('sp', hgs[l, bg * 8:(bg + 1) * 8, h].rearrange("b k v -> k b v"), Sn, okey())
                for b in range(8):
                    bb = bg * 8 + b
                    P.mm(psOs[:, bb:bb + 1], Snb[:, b, :], qs[:, bb:bb + 1])
            osq = AR.bf16(NS); osb = AR.f32(NS); rt = AR.f32(NS)
            P.act(osq, psOs[:, 0:NS], AF.Square)
            P.copy('dve', osb, psOs[:, 0:NS])
            psS = bank()
            P.mm(psS[:, 0:NS], ONESB, osq)
            P.act(rt, psS[:, 0:NS], AF.Ln, scale=1.0 / 128, bias=EPS)
            P.act(rt, rt, AF.Exp, scale=-0.5)
            P.tt('dve', osb, osb, rt, ALU.mult)
            P.stt(OAS[:, h, :], osb, V[:, 0, R_ONORM + l:R_ONORM + l + 1], sogs, ALU.mult, ALU.mult)
            AR.reset(a0)

        def lru_load(l, n):
            W, key = wslot()
            Wv = wview(W)
            P.dma('pool', Wv[:, :, 0:256], wsrc(wp[l], 4096 + n * 256, 256), key)
            P.dma('pool', Wv[:, 0, 256:384], lru_wa[l, n], key)
            P.dma('pool', Wv[:, 0, 384:512], lru_wx[l, n], key)
            return Wv

        def cvec(r, n):
            return V[:, n, r:r + 1]

        def lru_alloc(A=None, nset=3):
            A = A or AR
            A.reset()
            out = []
            for _ in range(nset):
                d = {'XP': A.f32(TB + 4), 'gel': A.bf16(TB), 'xc': A.f32(TB), 'xcb': A.bf16(TB), 'r': A.f32(TB),
                     'ig': A.f32(TB), 'a': A.f32(TB), 'hs': A.f32(TB)}
                d['a2'] = d['r']
                out.append(d)
            return out

        def lru_tb(l, n, tb, Wv, S):
            tsl = slice(tb * TB, (tb + 1) * TB)
            ps1, ps2 = bank(), bank()
            P.mmg(ps1[:, :], [(Wv[:, kc, 0:128], hT[:, kc, tsl]) for kc in range(8)])
            P.mmg(ps2[:, :], [(Wv[:, kc, 128:256], hT[:, kc, tsl]) for kc in range(8)])
            XP, xc, ig, a, a2, hs = S['XP'], S['xc'], S['ig'], S['a'], S['a2'], S['hs']
            P.copy('dve', XP[:, 0:3], HIST[l][:, n, :])
            P.act(XP[:, 3:3 + TB], ps1[:, :], AF.Copy)
            P.act(S['gel'], ps2[:, :], AF.Gelu_apprx_tanh)
            P.copy('dve', HIST[l][:, n, :], XP[:, TB:TB + 3])
            P.ts('dve', xc, XP[:, 0:TB], cvec(R_CONVW + 4 * l, n), cvec(R_CONVB + l, n), ALU.mult, ALU.add)
            for j in range(1, 4):
                P.stt(xc, XP[:, j:j + TB], cvec(R_CONVW + 4 * l + j, n), xc, ALU.mult, ALU.add)
            P.copy('dve', S['xcb'], xc)
            psr, psi = bank(), bank()
            P.mm(psr[:, :], Wv[:, 0, 256:384], S['xcb'])
            P.mm(psi[:, :], Wv[:, 0, 384:512], S['xcb'])
            P.act(S['r'], psr[:, :], AF.Tanh, scale=0.5, bias=HBA[l][:, n:n + 1])
            P.act(ig, psi[:, :], AF.Tanh, scale=0.5, bias=HBX[l][:, n:n + 1])
            P.act(a, S['r'], AF.Exp, scale=C8H[l][:, n:n + 1], bias=C8H[l][:, n:n + 1])
            P.act(a2, S['r'], AF.Exp, scale=C8[l][:, n:n + 1], bias=C8[l][:, n:n + 1])
            P.act(a2, a2, AF.Ln, scale=-1.0, bias=1.0)
            P.act(a2, a2, AF.Exp, scale=0.5)
            P.stt(ig, ig, 1.0, xc, ALU.add, ALU.mult)
            P.stt(ig, ig, 0.5, a2, ALU.mult, ALU.mult)
            P.scan(hs, a, ig, HC[l][:, n:n + 1], ALU.mult, ALU.add)
            P.copy('dve', HC[l][:, n:n + 1], hs[:, TB - 1:TB])
            P.tt('dve', OB[:, n, tsl], hs, S['gel'], ALU.mult)

        def lru_smp(l, n, Wv):
            a0 = AR.pos
            ps = bank()
            P.mmg(ps[:, 0:NS], [(Wv[:, kc, 0:128], hsT[:, kc, :]) for kc in range(8)])
            P.mmg(ps[:, NS:2 * NS], [(Wv[:, kc, 128:256], hsT[:, kc, :]) for kc in range(8)])
            xbs = AR.f32(NS); gels = AR.f32(NS); xcs = AR.f32(NS); xcsb = AR.bf16(NS)
            rs = AR.f32(NS); igs = AR.f32(NS); as_ = AR.f32(NS); a2s = AR.f32(NS)
            P.act(xbs, ps[:, 0:NS], AF.Copy)
            P.act(gels, ps[:, NS:2 * NS], AF.Gelu_apprx_tanh)
            CS = V[:, n, R_SCV + 48 * l:R_SCV + 48 * l + 48].rearrange("p (b j) -> p b j", j=3)
            P.ts('dve', xcs, CS[:, :, 0], cvec(R_CONVW + 4 * l, n), cvec(R_CONVB + l, n), ALU.mult, ALU.add)
            P.stt(xcs, CS[:, :, 1], cvec(R_CONVW + 4 * l + 1, n), xcs, ALU.mult, ALU.add)
            P.stt(xcs, CS[:, :, 2], cvec(R_CONVW + 4 * l + 2, n), xcs, ALU.mult, ALU.add)
            P.stt(xcs, xbs, cvec(R_CONVW + 4 * l + 3, n), xcs, ALU.mult, ALU.add)
            P.copy('dve', xcsb, xcs)
            psg = bank()
            P.mm(psg[:, 0:NS], Wv[:, 0, 256:384], xcsb)
            P.mm(psg[:, NS:2 * NS], Wv[:, 0, 384:512], xcsb)
            P.act(rs, psg[:, 0:NS], AF.Tanh, scale=0.5, bias=HBA[l][:, n:n + 1])
            P.act(igs, psg[:, NS:2 * NS], AF.Tanh, scale=0.5, bias=HBX[l][:, n:n + 1])
            P.act(as_, rs, AF.Exp, scale=C8H[l][:, n:n + 1], bias=C8H[l][:, n:n + 1])
            P.act(a2s, rs, AF.Exp, scale=C8[l][:, n:n + 1], bias=C8[l][:, n:n + 1])
            P.act(a2s, a2s, AF.Ln, scale=-1.0, bias=1.0)
            P.act(a2s, a2s, AF.Exp, scale=0.5)
            P.stt(igs, igs, 1.0, xcs, ALU.add, ALU.mult)
            P.stt(igs, igs, 0.5, a2s, ALU.mult, ALU.mult)
            h0 = V[:, n, R_SRG + NS * l:R_SRG + NS * l + NS]
            P.tt('dve', as_, as_, h0, ALU.mult)
            hn = VO[:, n, O_LRUS + NS * l:O_LRUS + NS * l + NS]
            P.tt('dve', hn, as_, igs, ALU.add)
            P.tt('dve', OBS[:, n, :], hn, gels, ALU.mult)
            VOc = VO[:, n, O_CONVS + 48 * l:O_CONVS + 48 * l + 48].rearrange("p (b j) -> p b j", j=3)
            P.copy('dve', VOc[:, :, 0:2], CS[:, :, 1:3])
            P.copy('dve', VOc[:, :, 2], xbs)
            AR.reset(a0)

        def merge_load(l, j):
            W, key = wslot()
            Wv = wview(W)
            P.dma('pool', Wv, wsrc(wp[l], 6144 + j * 512, 512), key)
            return Wv

        def merge_compute(l, j, Wv, smp):
            AR.reset()
            sg = [AR.f32(TB), AR.f32(TB)]
            for tb in range(2):
                tsl = slice(tb * TB, (tb + 1) * TB)
                pss = [bank() for _ in range(4)]
                srcs = (hT, hT, OA, OB)
                for g in range(4):
                    P.mmg(pss[g][:, :], [(Wv[:, kc, g * 128:(g + 1) * 128], srcs[g][:, kc, tsl]) for kc in range(8)])
                P.act(sg[0], pss[0][:, :], AF.Sigmoid)
                P.act(sg[1], pss[1][:, :], AF.Sigmoid)
                P.tt('dve', sg[0], sg[0], pss[2][:, :], ALU.mult)
                P.tt('dve', sg[1], sg[1], pss[3][:, :], ALU.mult)
                P.tt('pool', MG[:, j, tsl], sg[0], sg[1], ALU.add)
            if smp:
                ps = bank()
                srcs = (hsT, hsT, OAS, OBS)
                for g in range(4):
                    P.mmg(ps[:, g * NS:(g + 1) * NS], [(Wv[:, kc, g * 128:(g + 1) * 128], srcs[g][:, kc, :]) for kc in range(8)])
                s = AR.f32(2 * NS)
                P.act(s, ps[:, 0:2 * NS], AF.Sigmoid)
                P.tt('dve', s, s, ps[:, 2 * NS:4 * NS], ALU.mult)
                P.tt('dve', MGS[:, j, :], s[:, 0:NS], s[:, NS:2 * NS], ALU.add)

        def dense_load(w2d, c0):
            W, key = wslot()
            Wv = wview(W)
            P.dma('pool', Wv, wsrc(w2d, c0, 512), key)
            return Wv

        def resid_add(l, j, gbase, ps_list, ps_s, smp):
            for tb in range(2):
                tsl = slice(tb * TB, (tb + 1) * TB)
                P.stt(xT[:, j, tsl], ps_list[tb][:, :], MOD[l][:, gbase + j, 0:1], xT[:, j, tsl], ALU.mult, ALU.add)
            if smp:
                t = AR.f32(NS)
                P.tt('dve', t, ps_s[:, 0:NS], MOD[l][:, gbase + j, 1:17], ALU.mult)
                P.tt('dve', xsT[:, j, :], xsT[:, j, :], t, ALU.add)

        def out_compute(l, jg, Wv, smp):
            AR.reset()
            for jj in range(4):
                j = jg * 4 + jj
                pl = []
                for tb in range(2):
                    tsl = slice(tb * TB, (tb + 1) * TB)
                    ps = bank()
                    P.mmg(ps[:, :], [(Wv[:, kc, jj * 128:(jj + 1) * 128], MG[:, kc, tsl]) for kc in range(8)])
                    pl.append(ps)
                ps_s = None
                if smp:
                    ps_s = bank()
                    P.mmg(ps_s[:, 0:NS], [(Wv[:, kc, jj * 128:(jj + 1) * 128], MGS[:, kc, :]) for kc in range(8)])
                resid_add(l, j, 16, pl, ps_s, smp)

        def ff1_compute(l, fg, Wv, smp):
            AR.reset()
            rl = [AR.bf16(TB), AR.bf16(TB)]
            rls = AR.bf16(NS)
            k = 0
            for jj in range(4):
                jf = fg * 4 + jj
                for tb in range(2):
                    tsl = slice(tb * TB, (tb + 1) * TB)
                    ps = bank()
                    P.mmg(ps[:, :], [(Wv[:, kc, jj * 128:(jj + 1) * 128], hT[:, kc, tsl]) for kc in range(8)])
                    P.act(rl[k % 2], ps[:, :], AF.Relu)
                    P.tt('pool' if k % 2 else 'dve', FF[:, jf, tsl], rl[k % 2], rl[k % 2], ALU.mult)
                    k += 1
                if smp:
                    ps = bank()
                    P.mmg(ps[:, 0:NS], [(Wv[:, kc, jj * 128:(jj + 1) * 128], hsT[:, kc, :]) for kc in range(8)])
                    P.act(rls, ps[:, 0:NS], AF.Relu)
                    P.tt('dve', FFS[:, jf, :], rls, rls, ALU.mult)

        def ff2_load(l, j):
            W, key = wslot()
            Wv = W.rearrange("p (k n) -> p k n", k=32)
            P.dma('pool', Wv, w_ff2p[l, j].rearrange("p (k n) -> p k n", k=32), key)
            return Wv

        def ff2_compute(l, j, Wv, smp):
            AR.reset()
            pl = []
            for tb in range(2):
                tsl = slice(tb * TB, (tb + 1) * TB)
                ps = bank()
                P.mmg(ps[:, :], [(Wv[:, kc, :], FF[:, kc, tsl]) for kc in range(32)])
                pl.append(ps)
            ps_s = None
            if smp:
                ps_s = bank()
                P.mmg(ps_s[:, 0:NS], [(Wv[:, kc, :], FFS[:, kc, :]) for kc in range(32)])
            resid_add(l, j, 40, pl, ps_s, smp)

        def load_x(hf):
            AR.reset()
            XS = [AR.f32(D) for _ in range(4)]
            for q in range(2):
                for tt_ in range(4):
                    r0 = hf * TH + q * TB + tt_ * 128
                    P.dma('sp', XS[tt_], xp[r0:r0 + 128, :], f'xin{tt_ % 2}')
                for j in range(8):
                    ps = bank()
                    for tt_ in range(4):
                        P.tr(ps[:, tt_ * 128:(tt_ + 1) * 128], XS[tt_][:, j * 128:(j + 1) * 128], IDF)
                    P.copy('act' if j % 2 else 'dve', xT[:, j, q * TB:(q + 1) * TB], ps[:, :])

        def store_y(hf):
            AR.reset()
            yT = AR.f32(8 * TB).rearrange("p (c t) -> p c t", c=8)
            sq = AR.bf16(8 * TB).rearrange("p (c t) -> p c t", c=8)
            rt = AR.f32(TB)
            YS = [AR.f32(D), AR.f32(D)]
            for tb in range(2):
                tsl = slice(tb * TB, (tb + 1) * TB)
                P.act(sq, xT[:, :, tsl], AF.Square)
                ps = bank()
                P.mmg(ps[:, :], [(ONESB, sq[:, c, :]) for c in range(8)])
                P.act(rt, ps[:, :], AF.Ln, scale=1.0 / D, bias=EPS)
                P.act(rt, rt, AF.Exp, scale=-0.5)
                for c in range(8):
                    P.stt(yT[:, c, :], xT[:, c, tsl], V[:, c, R_FNG:R_FNG + 1], rt, ALU.mult, ALU.mult)
                for tt_ in range(4):
                    ys = YS[tt_ % 2]
                    for half in range(2):
                        ps = bank()
                        for i in range(4):
                            c = half * 4 + i
                            P.tr(ps[:, i * 128:(i + 1) * 128], yT[:, c, tt_ * 128:(tt_ + 1) * 128], IDF)
                        P.copy('act' if half else 'dve', ys[:, half * 512:(half + 1) * 512], ps[:, :])
                    r0 = hf * TH + tb * TB + tt_ * 128
                    P.dma('sp', yp[r0:r0 + 128, :], ys, okey())

        import os
        items = []
        INTER = os.environ.get('K_INTER', '0') == '1'

        def add_item(lf, cf):
            items.append((lf, cf))

        HALF1_ITEM = None
        for hf in range(2):
            smp = (hf == 0)
            if hf == 1:
                HALF1_ITEM = len(items)
            add_item(None, (lambda hf=hf: load_x(hf)))
            for l in range(2):
                if hf == 0:
                    for g in [int(x) for x in os.environ.get('K_MODG', '0,1,2,3,4,5,6,7,8,9,10,11').split(',')]:
                        add_item((lambda l=l, g=g: mod_load(l, g)), (lambda W, l=l, g=g: mod_compute(l, g, W)))
                    add_item(None, (lambda l=l: mod_finish(l)))

                def n1(l=l, smp=smp):
                    norm_prompt(A1P[l], MOD[l][:, 0:8, 0], hT)
                    if smp:
                        norm_samples(A1S[l], MOD[l][:, 0:8, 1:17], hsT)
                add_item(None, n1)
                for h in range(8):
                    def hc(W, l=l, h=h, smp=smp, hf=hf):
                        S = hgrn_alloc()
                        for tb in range(2):
                            hgrn_tb(l, h, tb, W, S)
                        if smp:
                            hgrn_smp(l, h, W)
                        if hf == 1:
                            P.dma('sp', hgp[l, h], SST[l][:, h, :], okey())
                    add_item((lambda l=l, h=h: hgrn_load(l, h)), hc)
                lru_items = []
                for n in range(8):
                    def lc(W, l=l, n=n, smp=smp, hf=hf):
                        if hf == 1 and INTER:
                            S = lru_alloc(AR2, 1)
                        else:
                            S = lru_alloc()
                        for tb in range(2):
                            lru_tb(l, n, tb, W, S[(2 * n + tb) % len(S)])
                        if smp:
                            lru_smp(l, n, W)
                        if hf == 1:
                            P.copy('dve', VO[:, n, O_LRUP + l:O_LRUP + l + 1], HC[l][:, n:n + 1])
                            P.copy('dve', VO[:, n, O_CONVP + 3 * l:O_CONVP + 3 * l + 3], HIST[l][:, n, :])
                    lru_items.append(((lambda l=l, n=n: lru_load(l, n)), lc))
                if hf == 1 and INTER:
                    hitems = items[-8:]
                    del items[-8:]
                    for hi, li in zip(hitems, lru_items):
                        items.append(hi)
                        items.append(li)
                else:
                    items.extend(lru_items)
                for j in range(8):
                    add_item((lambda l=l, j=j: merge_load(l, j)), (lambda W, l=l, j=j, smp=smp: merge_compute(l, j, W, smp)))
                for jg in range(2):
                    add_item((lambda l=l, jg=jg: dense_load(w_out[l], jg * 512)),
                             (lambda W, l=l, jg=jg, smp=smp: out_compute(l, jg, W, smp)))

                def n2(l=l, smp=smp):
                    norm_prompt(A2P[l], MOD[l][:, 24:32, 0], hT)
                    if smp:
                        norm_samples(A2S[l], MOD[l][:, 24:32, 1:17], hsT)
                add_item(None, n2)
                for fg in range(8):
                    add_item((lambda l=l, fg=fg: dense_load(w_ff1[l], fg * 512)),
                             (lambda W, l=l, fg=fg, smp=smp: ff1_compute(l, fg, W, smp)))
                for j in range(8):
                    add_item((lambda l=l, j=j: ff2_load(l, j)), (lambda W, l=l, j=j, smp=smp: ff2_compute(l, j, W, smp)))

            def fin(hf=hf, smp=smp):
                store_y(hf)
                if smp:
                    norm_samples(bc(V[:, :, R_FNG], 2, [128, 8, NS]), None, VO[:, :, O_YS:O_YS + NS])
            add_item(None, fin)

        import bisect
        loaded = {}
        load_idx = [i for i, it in enumerate(items) if it[0] is not None]
        ptr = 0
        import os
        PRE = int(os.environ.get('K_PRE', '1'))
        for i, (lf, cf) in enumerate(items):
            if limit is not None and i >= limit:
                break
            cur = bisect.bisect_right(load_idx, i)
            while ptr < min(len(load_idx), cur + PRE) and (limit is None or load_idx[ptr] < limit):
                loaded[load_idx[ptr]] = items[load_idx[ptr]][0]()
                ptr += 1
            P.mark(f"item{i}")
            if HALF1_ITEM is not None and i >= HALF1_ITEM:
                bstate['half'] = 1
            if i == HALF1_ITEM:
                P.warm_from = len(P.ops)
            if lf is not None:
                cf(loaded.pop(i))
            else:
                cf()

        AR.reset()
        OS = [AR.f32(D), AR.f32(D)]
        for ti, (r0, nr) in enumerate(((0, 128), (128, NRO - 128))):
            for half in range(2):
                ps = bank()
                for i in range(4):
                    c = half * 4 + i
                    P.tr(ps[0:nr, i * 128:(i + 1) * 128], VO[:, c, r0:r0 + nr], IDF)
                P.copy('act' if half else 'dve', OS[ti][0:nr, half * 512:(half + 1) * 512], ps[0:nr, :])
            P.dma('sp', orows[r0:r0 + nr, :], OS[ti][0:nr, :], f'orow{ti}')

        with nc.Block():
            P.emit(st, reorder=(os.environ.get('K_REORDER', '1') == '1'))
    return nc, dbg_out


_CACHE = {}


def _consts():
    c = np.zeros((128, 1280), np.float32)
    c[:, 0:128] = np.eye(128, dtype=np.float32)
    c[:, 128:256] = 1.0
    s = np.arange(CH)[:, None]
    t = np.arange(CH)[None, :]
    m = (s <= t).astype(np.float32)
    c[0:CH, 256:768] = np.tile(m, (1, NCH))
    r = np.ones(TB, np.float32)
    r[::CH] = 0.0
    c[:, 768:1280] = r[None, :]
    return c


def _pack_weights(inp):
    w_in = inp["w_in"]
    wp = np.empty((2, D, 10240), np.float32)
    for h in range(8):
        for g in range(4):
            wp[:, :, h * 512 + g * 128:h * 512 + (g + 1) * 128] = w_in[:, :, g * 1024 + h * 128:g * 1024 + (h + 1) * 128]
    for n in range(8):
        wp[:, :, 4096 + n * 256:4096 + n * 256 + 128] = w_in[:, :, 4096 + n * 128:4096 + (n + 1) * 128]
        wp[:, :, 4096 + n * 256 + 128:4096 + (n + 1) * 256] = w_in[:, :, 5120 + n * 128:5120 + (n + 1) * 128]
    for j in range(8):
        b = 6144 + j * 512
        wp[:, :, b:b + 128] = w_in[:, :, 6144 + j * 128:6144 + (j + 1) * 128]
        wp[:, :, b + 128:b + 256] = w_in[:, :, 7168 + j * 128:7168 + (j + 1) * 128]
        wp[:, :, b + 256:b + 384] = inp["w_branch_a"][:, :, j * 128:(j + 1) * 128]
        wp[:, :, b + 384:b + 512] = inp["w_branch_b"][:, :, j * 128:(j + 1) * 128]
    w2 = inp["w_ff2"].reshape(2, 32, 128, 8, 128)
    w_ff2p = np.ascontiguousarray(w2.transpose(0, 3, 2, 1, 4)).reshape(2, 8, 128, 4 * D)
    return {"wp": wp, "w_ff2p": w_ff2p}


def _pack_rows(inp, core):
    rows = np.zeros((NR, D), np.float32)
    for l in range(2):
        rows[R_N1G + l] = inp["norm1_g"][l]
        rows[R_N2G + l] = inp["norm2_g"][l]
        rows[R_CONVW + 4 * l:R_CONVW + 4 * l + 4] = inp["lru_conv_w"][l]
        rows[R_CONVB + l] = inp["lru_conv_b"][l]
        rows[R_BA + l] = inp["lru_ba"][l]
        rows[R_BX + l] = inp["lru_bx"][l]
        rows[R_LAM + l] = inp["lru_lambda"][l]
        rows[R_HGL + l] = inp["hg_lower"][l]
        rows[R_BMOD + 6 * l:R_BMOD + 6 * l + 6] = inp["b_mod"][l].reshape(6, D)
        rows[R_ONORM + l, 0:128] = inp["hg_onorm_g"][l]
        sl = slice(core * NS, (core + 1) * NS)
        rows[R_SRG + NS * l:R_SRG + NS * l + NS] = inp["state_rglru"][l, sl]
        rows[R_SCV + 48 * l:R_SCV + 48 * l + 48] = inp["state_conv"][l, sl].reshape(48, D)
    rows[R_FNG] = inp["final_norm_g"]
    rows[R_CP] = inp["c_prompt"][core]
    rows[R_CS:R_CS + NS] = inp["c_sample"][core * NS:(core + 1) * NS]
    rows[R_XS:R_XS + NS] = inp["x_sample"][core * NS:(core + 1) * NS, 0]
    return rows


def run(inputs, dbg=None, trace=False, cores=NCORE):
    inp = {k: np.ascontiguousarray(np.asarray(v, dtype=np.float32)) for k, v in inputs.items()}
    key = tuple(sorted(dbg)) if dbg else None
    if key not in _CACHE:
        _CACHE[key] = build_nc(dbg)
    nc, dbg_out = _CACHE[key]
    cst = _consts()
    shared = {k: inp[k] for k in ("w_mod", "lru_wa", "lru_wx", "w_out", "w_ff1")}
    shared.update(_pack_weights(inp))
    in_maps = []
    for c in range(cores):
        m = dict(shared)
        m["xp"] = inp["x_prompt"][c]
        m["rows"] = _pack_rows(inp, c)
        m["shg"] = np.ascontiguousarray(inp["state_hgrn"][:, c * NS:(c + 1) * NS])
        m["cst"] = cst
        in_maps.append(m)
    res = run_bass_kernel_spmd(nc, in_maps, core_ids=list(range(cores)), trace=trace)
    return res


def kernel(**inputs):
    res = run(inputs)
    R = res.results
    y_prompt = np.stack([R[c]["yp"] for c in range(NCORE)], 0)
    orow = [R[c]["orows"] for c in range(NCORE)]
    y_sample = np.concatenate([o[O_YS:O_YS + NS] for o in orow], 0)[:, None, :]
    hg_p = np.stack([R[c]["hgp"] for c in range(NCORE)], 1)
    lru_p = np.stack([o[O_LRUP:O_LRUP + 2] for o in orow], 1)
    conv_p = np.stack([o[O_CONVP:O_CONVP + 6].reshape(2, 3, D) for o in orow], 1)
    hg_s = np.concatenate([R[c]["hgs"] for c in range(NCORE)], 1)
    lru_s = np.concatenate([o[O_LRUS:O_LRUS + 2 * NS].reshape(2, NS, D) for o in orow], 1)
    conv_s = np.concatenate([o[O_CONVS:O_CONVS + 96].reshape(2, NS, 3, D) for o in orow], 1)
    f = lambda a: np.ascontiguousarray(a, dtype=np.float32)
    return (f(y_prompt), f(y_sample), f(hg_p), f(lru_p), f(conv_p), f(hg_s), f(lru_s), f(conv_s))
```

```python
import numpy as np
from contextlib import ExitStack
import concourse.bass as bass
import concourse.mybir as mybir
from concourse.bass_utils import run_bass_kernel_spmd

F32 = mybir.dt.float32
BF16 = mybir.dt.bfloat16
AF = mybir.ActivationFunctionType
ALU = mybir.AluOpType

D = 1024
SEQ = 2048
TH = 1024
TB = 512
NS = 16
NCORE = 8
EPS = 1e-6
CH = 32
NCH = TB // CH

R_N1G, R_N2G, R_CONVW, R_CONVB, R_BA, R_BX, R_LAM, R_HGL, R_FNG, R_BMOD, R_ONORM = 0, 2, 4, 12, 14, 16, 18, 20, 22, 23, 35
R_CP, R_CS, R_XS, R_SRG, R_SCV = 37, 38, 54, 70, 102
NR = 198
O_YS, O_LRUP, O_CONVP, O_LRUS, O_CONVS = 0, 16, 18, 24, 56
NRO = 152
SEG = 20000

ENGNAME = {'pe': 'tensor', 'act': 'scalar', 'dve': 'vector', 'pool': 'gpsimd', 'sp': 'sync'}


class Op:
    __slots__ = ('eng', 'fn', 'dma', 'signal', 'signo', 'we', 'wd', 'deps', 'dvals', 'cost', 'tset', 'nbytes',
                 'start', 'fin', 'idx')


def _free_elems(ap):
    n = 1
    for st, c in ap.ap[1:]:
        n *= c
    return n


class Prog:
    def __init__(self, nc):
        self.nc = nc
        self.ops = []
        self.hist = {}
        self.dma_cnt = {}
        self.last_dma = {}
        self.marks = []

    def mark(self, label):
        self.marks.append((label, len(self.ops)))

    def regs(self, aps):
        out = []
        for ap in aps:
            if ap is None or not hasattr(ap, 'ap'):
                continue
            if 'DRAM' in str(ap.space).upper():
                continue
            pat = ap.ap
            ps = pat[0][0]
            es = 4 if ap.dtype == F32 else 2
            p0 = ap.offset // ps
            f0 = ap.offset % ps
            span = sum((c - 1) * abs(st) for st, c in pat[1:]) + 1
            if 'PSUM' in str(ap.space).upper():
                out.append((ap.tensor.name, 0, 128, 0, 2048))
            else:
                out.append((ap.tensor.name, p0, p0 + pat[0][1], f0 * es, (f0 + span) * es))
        return out

    def add(self, eng, fn, ins, outs, dma=None, cost=0.2, tset=None, nbytes=0):
        idx = len(self.ops)
        R = self.regs(ins)
        W = self.regs(outs)
        deps = {}
        for (n, p0, p1, a, b) in R:
            psum = n.startswith('ps')
            for k, v in self.hist.get(n, {}).items():
                if k[0] < p1 and p0 < k[1] and k[2] < b and a < k[3]:
                    if k[5]:
                        deps[v] = True
                    elif psum and k[4] != eng:
                        for vv in v:
                            deps.setdefault(vv, False)
        for (n, p0, p1, a, b) in W:
            for k, v in self.hist.get(n, {}).items():
                if k[0] < p1 and p0 < k[1] and k[2] < b and a < k[3]:
                    if k[5]:
                        deps.setdefault(v, False)
                    else:
                        for vv in v:
                            deps.setdefault(vv, False)
        for (n, p0, p1, a, b) in W:
            L = self.hist.setdefault(n, {})
            for k in [k for k in L if p0 <= k[0] and k[1] <= p1 and a <= k[2] and k[3] <= b]:
                del L[k]
            L[(p0, p1, a, b, eng, True)] = idx
        for (n, p0, p1, a, b) in R:
            self.hist.setdefault(n, {}).setdefault((p0, p1, a, b, eng, False), []).append(idx)
        op = Op()
        op.eng = eng; op.fn = fn; op.dma = dma; op.signal = False; op.signo = 0
        op.cost = cost; op.tset = tset; op.nbytes = nbytes; op.idx = idx
        dvals = {}
        for pidx in list(deps):
            pop = self.ops[pidx]
            if pop.dma is not None:
                val = self.dma_cnt[pop.dma]
                dvals[pop.dma] = val
                deps.setdefault(self.last_dma[pop.dma], deps[pidx])
        if dma is not None:
            self.dma_cnt[dma] = self.dma_cnt.get(dma, 0) + 16
            self.last_dma[dma] = idx
        op.deps = deps
        op.dvals = dvals
        self.ops.append(op)

    def schedule(self, window=900):
        import heapq, os
        ops = self.ops
        n = len(ops)
        succ = [[] for _ in range(n)]
        nrem = [0] * n
        for i, op in enumerate(ops):
            nrem[i] = len(op.deps)
            for p in op.deps:
                succ[p].append(i)
        ready = [0.0] * n
        cp = [0.0] * n
        for i in range(n - 1, -1, -1):
            m = 0.0
            for j in succ[i]:
                if cp[j] > m:
                    m = cp[j]
            cp[i] = m + ops[i].cost + 0.12
        USECP = os.environ.get('K_CP', '0') == '1'
        inorder = {'sp': [], 'pool': []}
        for i, op in enumerate(ops):
            if op.eng in inorder:
                inorder[op.eng].append(i)
        ptr = {'sp': 0, 'pool': 0}
        pend = {e: [] for e in ('pe', 'act', 'dve')}
        avail = {e: [] for e in ('pe', 'act', 'dve')}
        free = {e: 0.0 for e in ENGNAME}
        cur_set = [None]
        dma_free = [0.0]
        done = [False] * n
        nsched = 0
        low = {e: 0 for e in ('pe', 'act', 'dve')}
        eng_ops = {e: [i for i, op in enumerate(ops) if op.eng == e] for e in ('pe', 'act', 'dve')}
        eng_ptr = {e: 0 for e in ('pe', 'act', 'dve')}
        released = [False] * n
        for i, op in enumerate(ops):
            if nrem[i] == 0 and op.eng in pend:
                heapq.heappush(pend[op.eng], (0.0, i)); released[i] = True
        LAT = 0.12
        order = []

        def tswitch(op):
            t = op.tset
            if t is None:
                return False
            c = cur_set[0]
            if t == c:
                return False
            if t == 'tanh' and c in ('silu', 'gelu', 'sigmoid'):
                return False
            return True

        while nsched < n:
            best = None
            for e in ('pe', 'act', 'dve'):
                while eng_ptr[e] < len(eng_ops[e]) and done[eng_ops[e][eng_ptr[e]]]:
                    eng_ptr[e] += 1
                lo = eng_ops[e][eng_ptr[e]] if eng_ptr[e] < len(eng_ops[e]) else n
                P_, A_ = pend[e], avail[e]
                while P_ and P_[0][0] <= free[e]:
                    r, i = heapq.heappop(P_)
                    heapq.heappush(A_, i)
                cand = None
                if A_:
                    if USECP:
                        inw = [j for j in A_ if j <= lo + window]
                        i = None
                        if inw:
                            if e == 'act':
                                ns = [j for j in inw if not tswitch(ops[j])]
                                best_ns = max(ns, key=lambda j: cp[j]) if ns else None
                                best_all = max(inw, key=lambda j: cp[j])
                                i = best_ns if (best_ns is not None and cp[best_ns] + 6.0 >= cp[best_all]) else best_all
                            else:
                                i = max(inw, key=lambda j: cp[j])
                    else:
                        i = A_[0]
                        if i > lo + window:
                            i = None
                        if i is not None and e == 'act' and tswitch(ops[i]):
                            alt = [j for j in heapq.nsmallest(6, A_) if j <= lo + window and not tswitch(ops[j])]
                            if alt:
                                i = alt[0]
                    if i is not None:
                        cand = (free[e], i)
                if cand is None and P_:
                    r, i = P_[0]
                    if i <= lo + window:
                        cand = (r, i)
                    else:
                        j = min(P_, key=lambda x: x[1])
                        cand = (max(j[0], free[e]), j[1])
                if cand is not None and (best is None or (cand[0], cand[1]) < (best[0], best[1])):
                    best = (cand[0], cand[1], e)
            for e in ('sp', 'pool'):
                if ptr[e] < len(inorder[e]):
                    i = inorder[e][ptr[e]]
                    if nrem[i] == 0:
                        st_ = max(free[e], ready[i])
                        if best is None or (st_, i) < (best[0], best[1]):
                            best = (st_, i, e)
            assert best is not None, "scheduler deadlock"
            st_, i, e = best
            op = ops[i]
            if e in pend:
                if i in avail[e]:
                    avail[e].remove(i); heapq.heapify(avail[e])
                else:
                    pend[e].remove((ready[i], i)); heapq.heapify(pend[e])
            else:
                ptr[e] += 1
            c = op.cost
            if e == 'act' and op.tset is not None:
                if tswitch(op):
                    c += 1.3
                    cur_set[0] = 'silu' if op.tset == 'tanh' else op.tset
            op.start = st_
            if op.dma is not None:
                free[e] = st_ + 0.06
                t0 = max(st_, dma_free[0])
                dma_free[0] = t0 + op.nbytes / 300e3
                op.fin = dma_free[0] + 2.0
            else:
                free[e] = st_ + c
                op.fin = st_ + c
            done[i] = True
            nsched += 1
            order.append(i)
            for sidx in succ[i]:
                nrem[sidx] -= 1
                ready[sidx] = max(ready[sidx], op.fin + LAT)
                if nrem[sidx] == 0 and ops[sidx].eng in pend:
                    heapq.heappush(pend[ops[sidx].eng], (ready[sidx], sidx))
        self.ndummy = {}
        last_fin = None
        for i in order:
            op = ops[i]
            if op.eng != 'pe':
                continue
            if last_fin is not None:
                gap = op.start - last_fin
                if gap > 1.0 and i >= getattr(self, 'warm_from', 1 << 60):
                    self.ndummy[i] = min(int(gap * 0.6 / 0.13), 24)
            last_fin = op.fin
        self.order = order
        self.makespan = max(op.fin for op in ops)
        return order

    def emit(self, stack, reorder=True):
        import os
        nc = self.nc
        ops = self.ops
        if reorder:
            order = self.schedule()
        else:
            order = list(range(len(ops)))
        seen = {e: {} for e in ENGNAME}
        for i in order:
            op = ops[i]
            eng = op.eng
            sn = seen[eng]
            we = {}
            wd = {}
            for k, v in op.dvals.items():
                if sn.get(('d', k), 0) < v:
                    wd[k] = v
            for pidx, raw in op.deps.items():
                pop = ops[pidx]
                if pop.dma is not None:
                    continue
                pe_ = pop.eng
                if pe_ == eng and (eng == 'pe' or not raw):
                    continue
                we.setdefault(pe_, []).append(pidx)
            op.we = we
            op.wd = wd
            for k, v in wd.items():
                sn[('d', k)] = v
        pos = {}
        cnt = {e: 0 for e in ENGNAME}
        for i in order:
            cnt[ops[i].eng] += 1
            pos[i] = cnt[ops[i].eng]
        seenp = {e: {} for e in ENGNAME}
        for i in order:
            op = ops[i]
            red = {}
            for pe_, lst in op.we.items():
                p = max(lst, key=lambda x: pos[x])
                if seenp[op.eng].get(pe_, 0) < pos[p]:
                    red[pe_] = p
                    seenp[op.eng][pe_] = pos[p]
                    ops[p].signal = True
            op.we = red
        cnt = {e: 0 for e in ENGNAME}
        for i in order:
            op = ops[i]
            if op.dma is None and op.signal:
                cnt[op.eng] += 1
                op.signo = cnt[op.eng]
        esem = {}
        for e in ENGNAME:
            esem[e] = [stack.enter_context(nc.semaphore(f"s_{e}{i}")) for i in range(cnt[e] // SEG + 1)]
        dsem = {k: stack.enter_context(nc.semaphore(f"d_{k}")) for k in self.dma_cnt}
        mk = dict((i, l) for l, i in self.marks)
        mout = []
        names = {}
        for oi in order:
            op = ops[oi]
            E = getattr(nc, ENGNAME[op.eng])
            if getattr(self, 'warm', None) is not None and reorder:
                for _ in range(self.ndummy.get(oi, 0)):
                    self.warm()
            for pe_, pidx in op.we.items():
                sn = ops[pidx].signo - 1
                E.wait_ge(esem[pe_][sn // SEG], sn % SEG + 1)
            for k, v in op.wd.items():
                E.wait_ge(dsem[k], v)
            ins = op.fn()
            if os.environ.get('K_MARKS'):
                try:
                    names[oi] = str(ins.ins.name)
                except Exception as ex:
                    names[oi] = '?'
            if op.dma is not None:
                ins.then_inc(dsem[op.dma], 16)
            elif op.signal:
                sn = op.signo - 1
                ins.then_inc(esem[op.eng][sn // SEG], 1)
        for k, v in self.dma_cnt.items():
            nc.sync.wait_ge(dsem[k], v)
        if os.environ.get('K_MARKS'):
            import json
            json.dump({'marks': self.marks, 'names': names, 'makespan': getattr(self, 'makespan', None)},
                      open(os.environ['K_MARKS'], 'w'))

    def act(self, out, in_, func, scale=1.0, bias=None):
        nc = self.nc
        ins = [in_]
        kw = {}
        if hasattr(scale, 'ap'):
            ins.append(scale)
        if bias is not None:
            kw['bias'] = bias
            if hasattr(bias, 'ap'):
                ins.append(bias)
        tset = {AF.Silu: 'silu', AF.Tanh: 'tanh', AF.Ln: 'lnexp', AF.Exp: 'lnexp', AF.Sigmoid: 'sigmoid',
                AF.Gelu_apprx_tanh: 'gelu'}.get(func)
        self.add('act', lambda: nc.scalar.activation(out=out, in_=in_, func=func, scale=scale, **kw), ins, [out],
                 cost=0.25 + 0.1 * (len(ins) - 1) + _free_elems(in_) / 1200.0, tset=tset)

    def _e(self, eng):
        return getattr(self.nc, ENGNAME[eng])

    def tt(self, eng, out, in0, in1, op):
        E = self._e(eng)
        self.add(eng, lambda: E.tensor_tensor(out, in0, in1, op), [in0, in1], [out], cost=self._vc(eng, out))

    def ts(self, eng, out, in0, s1, s2, op0, op1=None):
        E = self._e(eng)
        ins = [in0] + [s for s in (s1, s2) if hasattr(s, 'ap')]
        if op1 is None:
            self.add(eng, lambda: E.tensor_scalar(out, in0, s1, None, op0), ins, [out], cost=self._vc(eng, out))
        else:
            self.add(eng, lambda: E.tensor_scalar(out, in0, s1, s2, op0, op1), ins, [out], cost=self._vc(eng, out))

    def stt(self, out, in0, scalar, in1, op0, op1):
        nc = self.nc
        ins = [in0, in1] + ([scalar] if hasattr(scalar, 'ap') else [])
        self.add('dve', lambda: nc.vector.scalar_tensor_tensor(out, in0, scalar, in1, op0, op1), ins, [out], cost=self._vc('dve', out))

    def scan(self, out, d0, d1, init, op0, op1):
        nc = self.nc
        ins = [d0, d1] + ([init] if hasattr(init, 'ap') else [])
        self.add('dve', lambda: nc.vector.tensor_tensor_scan(out, d0, d1, init, op0, op1), ins, [out], cost=0.1 + 2 * _free_elems(out) / 960.0)

    def copy(self, eng, out, in_):
        if eng == 'act':
            return self.act(out, in_, AF.Copy)
        E = self._e(eng)
        self.add(eng, lambda: E.tensor_copy(out, in_), [in_], [out], cost=self._vc(eng, out))

    def recip(self, out, in_):
        nc = self.nc
        self.add('dve', lambda: nc.vector.reciprocal(out, in_), [in_], [out], cost=0.1 + _free_elems(out) / 220.0)

    def memset(self, eng, out, val):
        E = self._e(eng)
        self.add(eng, lambda: E.memset(out, val), [], [out], cost=self._vc(eng, out))

    def mm(self, out, lhsT, rhs, start=True, stop=True):
        nc = self.nc
        self.add('pe', lambda: nc.tensor.matmul(out, lhsT, rhs, start=start, stop=stop), [lhsT, rhs], [out],
                 cost=max(0.064, 0.014 + _free_elems(rhs) / 2400.0))

    def mmg(self, out, pairs):
        n = len(pairs)
        for i, (l, r) in enumerate(pairs):
            self.mm(out, l, r, start=(i == 0), stop=(i == n - 1))

    def tr(self, out, in_, ident):
        nc = self.nc
        self.add('pe', lambda: nc.tensor.transpose(out, in_, ident), [in_, ident], [out],
                 cost=0.06 + 4 * max(_free_elems(ident), 48) / 2400.0 * (1 if in_.dtype == F32 else 0.25))

    def _vc(self, eng, out):
        n = _free_elems(out)
        if eng == 'pool':
            return 0.35 + n / 480.0
        return 0.09 + n / 960.0

    def dma(self, q, out, in_, key, slow=False):
        E = self._e(q)
        src = in_ if 'DRAM' in str(in_.space).upper() else out
        nb = 1
        for st_, c in src.ap:
            nb *= c
        nb *= 4 if src.dtype == F32 else 2
        if slow:
            self.add(q, lambda: E.dma_start(out=out, in_=in_, allow_slow_non_contiguous=True), [in_], [out], dma=key, nbytes=nb)
        else:
            self.add(q, lambda: E.dma_start(out=out, in_=in_), [in_], [out], dma=key, nbytes=nb)


class Arena:
    def __init__(self, base_ap_f32, n):
        self.base = base_ap_f32
        self.n = n
        self.pos = 0

    def reset(self, pos=0):
        self.pos = pos

    def f32(self, n, parts=128):
        a = self.base[0:parts, self.pos:self.pos + n]
        self.pos += n
        assert self.pos <= self.n, ("arena overflow", self.pos, self.n)
        return a

    def bf16(self, n, parts=128):
        m = (n + 1) // 2
        a = self.base[0:parts, self.pos:self.pos + m].bitcast(BF16)
        self.pos += m
        assert self.pos <= self.n, ("arena overflow", self.pos, self.n)
        return a


def bc(ap, axis, shape):
    return ap.unsqueeze(axis).to_broadcast(shape)


def build_nc(dbg=None, limit=None):
    nc = bass.Bass("TRN2", target_bir_lowering=False)
    dt = lambda n, s, k="ExternalInput", d=F32: nc.dram_tensor(n, s, d, kind=k).ap()
    xp = dt("xp", [SEQ, D])
    rows = dt("rows", [NR, D])
    shg = dt("shg", [2, NS, 8, 128, 128])
    cst = dt("cst", [128, 1280])
    w_mod = dt("w_mod", [2, D, 6 * D])
    wp = dt("wp", [2, D, 10240])
    lru_wa = dt("lru_wa", [2, 8, 128, 128])
    lru_wx = dt("lru_wx", [2, 8, 128, 128])
    w_out = dt("w_out", [2, D, D])
    w_ff1 = dt("w_ff1", [2, D, 4 * D])
    w_ff2p = dt("w_ff2p", [2, 8, 128, 4 * D])
    yp = dt("yp", [SEQ, D], "ExternalOutput")
    orows = dt("orows", [NRO, D], "ExternalOutput")
    hgp = dt("hgp", [2, 8, 128, 128], "ExternalOutput")
    hgs = dt("hgs", [2, NS, 8, 128, 128], "ExternalOutput")
    dbg_out = {}

    MSZ = 53200
    with ExitStack() as st:
        M = st.enter_context(nc.sbuf_tensor("M", [128, MSZ], F32))
        banks = [st.enter_context(nc.psum_tensor(f"ps{i}", [128, 512], F32)) for i in range(8)]
        P = Prog(nc)

        pos = [0]

        def carve_f32(n):
            a = M[:, pos[0]:pos[0] + n]
            pos[0] += n
            return a

        def carve_bf16(n):
            a = M[:, pos[0]:pos[0] + n // 2].bitcast(BF16)
            pos[0] += n // 2
            return a

        xT = carve_f32(8 * TH).rearrange("p (c t) -> p c t", c=8)
        hT = carve_bf16(8 * TH).rearrange("p (c t) -> p c t", c=8)
        BIGf = carve_f32(16 * TH)
        BIG = BIGf.bitcast(BF16).rearrange("p (c t) -> p c t", c=32)
        OA = BIG[:, 0:8, :]
        OB = BIG[:, 8:16, :]
        MG = BIG[:, 16:24, :]
        FF = BIG
        V = carve_f32(8 * NR).rearrange("p (c r) -> p c r", c=8)
        VO = carve_f32(8 * NRO).rearrange("p (c r) -> p c r", c=8)
        MOD = [carve_f32(48 * 17).rearrange("p (m b) -> p m b", m=48) for _ in range(2)]
        CF = carve_f32(1280)
        IDF = CF[:, 0:128]
        MASK = CF[:, 256:768]
        RESET = CF[:, 768:1280]
        CBt = carve_bf16(256)
        IDB = CBt[:, 0:128]
        ONESB = CBt[:, 128:256]
        xsT = carve_f32(8 * NS).rearrange("p (c b) -> p c b", c=8)
        hsT = carve_bf16(8 * NS).rearrange("p (c b) -> p c b", c=8)
        OAS = carve_bf16(8 * NS).rearrange("p (c b) -> p c b", c=8)
        OBS = carve_bf16(8 * NS).rearrange("p (c b) -> p c b", c=8)
        MGS = carve_bf16(8 * NS).rearrange("p (c b) -> p c b", c=8)
        FFS = carve_bf16(32 * NS).rearrange("p (c b) -> p c b", c=32)
        CSb = carve_bf16(8 * 18).rearrange("p (c b) -> p c b", c=8)
        SST = [carve_f32(8 * 128).rearrange("p (h v) -> p h v", h=8) for _ in range(2)]
        HIST = [carve_f32(24).rearrange("p (c j) -> p c j", c=8) for _ in range(2)]
        HC = [carve_f32(8) for _ in range(2)]
        A1P = [carve_f32(8) for _ in range(2)]
        A2P = [carve_f32(8) for _ in range(2)]
        A1S = [carve_f32(8 * NS).rearrange("p (c b) -> p c b", c=8) for _ in range(2)]
        A2S = [carve_f32(8 * NS).rearrange("p (c b) -> p c b", c=8) for _ in range(2)]
        LB = [carve_f32(8) for _ in range(2)]
        OML = [carve_f32(8) for _ in range(2)]
        C8 = [carve_f32(8) for _ in range(2)]
        C16 = [carve_f32(8) for _ in range(2)]
        C8H = [carve_f32(8) for _ in range(2)]
        HBA = [carve_f32(8) for _ in range(2)]
        HBX = [carve_f32(8) for _ in range(2)]
        WS = [carve_bf16(8 * 512) for _ in range(2)]
        arena_n = MSZ - pos[0]
        AR = Arena(M[:, pos[0]:MSZ], arena_n)
        AR2 = Arena(BIGf[:, 12288:16384], 4096)

        import os
        WARM = os.environ.get('K_WARM', '1') == '1'
        NROT = 7
        B8 = os.environ.get('K_B8', '0') == '1'
        if B8:
            WARM = False
        bstate = {'b': 0, 'w': 0, 'o': 0}

        def bank():
            b = banks[bstate['b'] % (8 if (B8 and bstate.get('half') == 1) else NROT)]
            bstate['b'] += 1
            return b

        PS7 = banks[7]
        P.warm = (lambda: nc.tensor.matmul(banks[7][:, 256:512], CBt[:, 128:256], CBt[:, 0:256], start=True, stop=True)) if WARM else None

        def wslot():
            i = bstate['w'] % 2
            bstate['w'] += 1
            return WS[i], f"w{i}"

        def okey():
            bstate['o'] += 1
            return f"o{bstate['o'] % 12}"

        def dump(name, ap, shape):
            if dbg is None or name not in dbg:
                return
            o = nc.dram_tensor("dbg_" + name, list(shape), ap.dtype, kind="ExternalOutput").ap()
            dbg_out[name] = o
            P.dma('sp', o, ap, 'dbg')

        P.dma('sp', CF, cst[:, :], 'ld0')
        P.dma('pool', CBt, cst[:, 0:256], 'ld1')
        for l in range(2):
            P.memset('dve', SST[l], 0.0)
            P.memset('dve', HIST[l], 0.0)
            P.memset('dve', HC[l], 0.0)
        P.memset('dve', LB[0], 0.5)
        P.memset('dve', VO, 0.0)
        P.memset('dve', OML[0], 0.5)

        AR.reset()
        RS = [AR.f32(D), AR.f32(D)]
        for ti, (r0, nr) in enumerate(((0, 128), (128, NR - 128))):
            P.dma('sp', RS[ti][0:nr, :], rows[r0:r0 + nr, :], f'ldr{ti}')
            for c in range(8):
                ps = bank()
                P.tr(ps[:, 0:nr], RS[ti][0:nr, c * 128:(c + 1) * 128], IDF[0:nr, 0:nr])
                P.copy('dve' if c % 2 else 'act', V[:, c, r0:r0 + nr], ps[:, 0:nr])
        P.copy('dve', xsT, V[:, :, R_XS:R_XS + NS])
        tmp8 = AR.f32(8)
        P.tt('dve', tmp8, V[:, :, R_HGL + 1], V[:, :, R_HGL], ALU.subtract)
        P.act(LB[1], tmp8, AF.Sigmoid)
        P.act(OML[1], tmp8, AF.Sigmoid, scale=-1.0)
        P.ts('dve', OML[1], OML[1], 0.5, None, ALU.mult)
        P.tt('dve', LB[1], LB[1], OML[1], ALU.add)
        for l in range(2):
            P.act(tmp8, V[:, :, R_LAM + l], AF.Exp, scale=-1.0)
            P.act(tmp8, tmp8, AF.Ln, bias=1.0)
            P.ts('dve', C8[l], tmp8, -8.0, None, ALU.mult)
            P.ts('dve', C16[l], tmp8, -16.0, None, ALU.mult)
            P.ts('dve', C8H[l], tmp8, -4.0, None, ALU.mult)
            P.ts('dve', HBA[l], V[:, :, R_BA + l], 0.5, None, ALU.mult)
            P.ts('dve', HBX[l], V[:, :, R_BX + l], 0.5, None, ALU.mult)
        P.act(CSb[:, :, 0:17], V[:, :, R_CP:R_CP + 17], AF.Silu)

        def wview(W):
            return W.rearrange("p (k n) -> p k n", k=8)

        def wsrc(w2d, c0, n):
            return w2d[:, c0:c0 + n].rearrange("(k p) n -> p k n", p=128)

        def mod_load(l, g):
            W, key = wslot()
            Wv = wview(W)
            P.dma('pool', Wv, wsrc(w_mod[l], g * 512, 512), key)
            return Wv

        def mod_compute(l, g, Wv):
            ps = bank()
            for fc in range(4):
                P.mmg(ps[:, fc * 32:fc * 32 + 17],
                      [(Wv[:, kc, fc * 128:(fc + 1) * 128], CSb[:, kc, 0:17]) for kc in range(8)])
            for fc in range(4):
                m = 4 * g + fc
                i, c = m // 8, m % 8
                P.act(MOD[l][:, m, :], ps[:, fc * 32:fc * 32 + 17], AF.Identity,
                      bias=V[:, c, R_BMOD + 6 * l + i:R_BMOD + 6 * l + i + 1])

        def mod_finish(l):
            P.stt(A1P[l], MOD[l][:, 8:16, 0], 1.0, V[:, :, R_N1G + l], ALU.add, ALU.mult)
            P.stt(A2P[l], MOD[l][:, 32:40, 0], 1.0, V[:, :, R_N2G + l], ALU.add, ALU.mult)
            P.stt(A1S[l], MOD[l][:, 8:16, 1:17], 1.0, bc(V[:, :, R_N1G + l], 2, [128, 8, NS]), ALU.add, ALU.mult)
            P.stt(A2S[l], MOD[l][:, 32:40, 1:17], 1.0, bc(V[:, :, R_N2G + l], 2, [128, 8, NS]), ALU.add, ALU.mult)

        def norm_prompt(Acol, sh_ap, out3):
            AR.reset()
            sq = AR.bf16(8 * TB).rearrange("p (c t) -> p c t", c=8)
            rt = AR.f32(TB)
            tmps = [AR.f32(TB), AR.f32(TB)]
            for tb in range(2):
                tsl = slice(tb * TB, (tb + 1) * TB)
                P.act(sq[:, 0:5, :], xT[:, 0:5, tsl], AF.Square)
                P.tt('dve', sq[:, 5:8, :], xT[:, 5:8, tsl], xT[:, 5:8, tsl], ALU.mult)
                ps = bank()
                P.mmg(ps[:, :], [(ONESB, sq[:, c, :]) for c in range(8)])
                P.act(rt, ps[:, :], AF.Ln, scale=1.0 / D, bias=EPS)
                P.act(rt, rt, AF.Exp, scale=-0.5)
                for c in range(8):
                    t = tmps[c % 2]
                    P.stt(t, xT[:, c, tsl], Acol[:, c:c + 1], rt, ALU.mult, ALU.mult)
                    if sh_ap is None:
                        P.copy('act', out3[:, c, tsl], t)
                    elif c in (3, 7):
                        P.ts('dve', out3[:, c, tsl], t, sh_ap[:, c:c + 1], None, ALU.add)
                    else:
                        P.act(out3[:, c, tsl], t, AF.Identity, bias=sh_ap[:, c:c + 1])

        def norm_samples(A3, sh3, out3):
            sq = AR.bf16(8 * NS).rearrange("p (c b) -> p c b", c=8)
            rt = AR.f32(NS)
            t1 = AR.f32(8 * NS).rearrange("p (c b) -> p c b", c=8)
            P.act(sq, xsT, AF.Square)
            ps = bank()
            P.mmg(ps[:, 0:NS], [(ONESB, sq[:, c, :]) for c in range(8)])
            P.act(rt, ps[:, 0:NS], AF.Ln, scale=1.0 / D, bias=EPS)
            P.act(rt, rt, AF.Exp, scale=-0.5)
            P.tt('dve', t1, xsT, bc(rt, 1, [128, 8, NS]), ALU.mult)
            P.tt('dve', t1, t1, A3, ALU.mult)
            if sh3 is None:
                P.copy('dve', out3, t1)
            else:
                P.tt('dve', out3, t1, sh3, ALU.add)

        def hgrn_load(l, h):
            W, key = wslot()
            Wv = wview(W)
            P.dma('pool', Wv, wsrc(wp[l], h * 512, 512), key)
            return Wv

        def hgrn_alloc():
            AR.reset()
            S = {}
            for n in ('F0', 'F1', 'F2', 'F3', 'osb', 'rt'):
                S[n] = AR.f32(TB)
            for n in ('q', 'kt', 'vT', 'osq'):
                S[n] = AR.bf16(TB)
            S['qh2'] = [AR.bf16(TB), AR.bf16(TB)]
            S['sog2'] = [AR.bf16(TB), AR.bf16(TB)]
            S['Dt2'] = [AR.f32(NCH), AR.f32(NCH)]
            S['AT'] = AR.bf16(TB)
            S['ktok'] = AR.bf16(NCH * 128).rearrange("p (c k) -> p c k", c=NCH)
            S['vtok'] = AR.bf16(NCH * 128).rearrange("p (c k) -> p c k", c=NCH)
            S['Ub'] = AR.f32(NCH * 128).rearrange("p (c v) -> p c v", c=NCH)
            S['SB'] = AR.bf16(NCH * 128).rearrange("p (c v) -> p c v", c=NCH)
            return S

        def hgrn_tb(l, h, tb, Wv, S):
            tsl = slice(tb * TB, (tb + 1) * TB)
            S = dict(S)
            S['qh'] = S['qh2'][tb]; S['sog'] = S['sog2'][tb]; S['Dt'] = S['Dt2'][tb]
            pss = [bank() for _ in range(4)]
            for g in range(4):
                P.mmg(pss[g][:, :], [(Wv[:, kc, g * 128:(g + 1) * 128], hT[:, kc, tsl]) for kc in range(8)])
            F0, F1, F2, F3 = (S[n] for n in ('F0', 'F1', 'F2', 'F3'))
            P.act(S['q'], pss[0][:, :], AF.Silu)
            P.act(F0, pss[1][:, :], AF.Tanh, scale=0.5)
            P.act(S['vT'], pss[2][:, :], AF.Copy)
            P.act(S['sog'], pss[3][:, :], AF.Silu)
            P.ts('dve', F0, F0, OML[l][:, h:h + 1], LB[l][:, h:h + 1], ALU.mult, ALU.add)
            P.act(F1, F0, AF.Ln)
            P.ts('dve', F0, F0, -1.0, 1.0, ALU.mult, ALU.add)
            P.scan(F2, RESET, F1, 0.0, ALU.mult, ALU.add)
            P.act(F3, F2, AF.Exp)
            P.act(F1, F2, AF.Exp, scale=-1.0)
            P.tt('dve', S['qh'], S['q'], F3, ALU.mult)
            P.tt('dve', S['kt'], F0, F1, ALU.mult)
            P.copy('dve', S['Dt'], F3[:, CH - 1::CH])
            qh, kt, AT, ktok, vtok, Ub, SB, Dt = (S[n] for n in ('qh', 'kt', 'AT', 'ktok', 'vtok', 'Ub', 'SB', 'Dt'))
            psA = bank()
            for c in range(NCH):
                cs = slice(c * CH, (c + 1) * CH)
                P.mm(psA[0:CH, cs], kt[:, cs], qh[:, cs])
            P.tt('dve', AT[0:CH, :], psA[0:CH, :], MASK[0:CH, :], ALU.mult)
            for src, dst in ((kt, ktok), (S['vT'], vtok)):
                for half in range(2):
                    psT = bank()[:, :].bitcast(BF16)
                    for i in range(8):
                        c = half * 8 + i
                        P.tr(psT[0:CH, i * 128:(i + 1) * 128], src[:, c * CH:(c + 1) * CH], IDB)
                    P.copy('act' if half else 'dve',
                           dst[0:CH, half * 8:(half + 1) * 8, :].rearrange("p c k -> p (c k)"), psT[0:CH, :])
            psUs = []
            for q4 in range(4):
                psU = bank()
                psUs.append(psU)
                for i in range(4):
                    c = 4 * q4 + i
                    P.mm(psU[:, i * 128:(i + 1) * 128], ktok[0:CH, c, :], vtok[0:CH, c, :])
            Sp = SST[l][:, h, :]
            P.copy('pool', SB[:, 0, :], Sp)
            P.tt('dve', Ub[:, 0, :], psUs[0][:, 0:128], Sp, ALU.add)
            for c in range(1, NCH):
                P.stt(Ub[:, c, :], Ub[:, c - 1, :], Dt[:, c - 1:c], psUs[c // 4][:, (c % 4) * 128:(c % 4 + 1) * 128],
                      ALU.mult, ALU.add)
            for c in range(NCH - 1):
                if c % 4 != 3:
                    P.act(SB[:, c + 1, :], Ub[:, c, :], AF.Identity, scale=Dt[:, c:c + 1])
                else:
                    P.ts('pool', SB[:, c + 1, :], Ub[:, c, :], Dt[:, c:c + 1], 1.0, ALU.mult, ALU.mult)
            P.ts('dve', Sp, Ub[:, NCH - 1, :], Dt[:, NCH - 1:NCH], None, ALU.mult)
            psO = bank()
            for c in range(NCH):
                cs = slice(c * CH, (c + 1) * CH)
                P.mm(psO[:, cs], vtok[0:CH, c, :], AT[0:CH, cs], start=True, stop=False)
                P.mm(psO[:, cs], SB[:, c, :], qh[:, cs], start=False, stop=True)
            osb, rt_ = S['osb'], S['rt']
            P.act(S['osq'], psO[:, :], AF.Square)
            P.copy('dve', osb, psO[:, :])
            psS = bank()
            P.mm(psS[:, :], ONESB, S['osq'])
            P.act(rt_, psS[:, :], AF.Ln, scale=1.0 / 128, bias=EPS)
            P.act(rt_, rt_, AF.Exp, scale=-0.5)
            P.tt('dve', osb, osb, rt_, ALU.mult)
            P.stt(OA[:, h, tsl], osb, V[:, 0, R_ONORM + l:R_ONORM + l + 1], S['sog'], ALU.mult, ALU.mult)

        def hgrn_smp(l, h, Wv, AR=AR2):
            AR.reset()
            a0 = 0
            ps = bank()
            for g in range(4):
                P.mmg(ps[:, g * NS:(g + 1) * NS], [(Wv[:, kc, g * 128:(g + 1) * 128], hsT[:, kc, :]) for kc in range(8)])
            qs = AR.bf16(NS); fs = AR.f32(NS); vTs = AR.bf16(NS); sogs = AR.bf16(NS); kTs = AR.bf16(NS)
            P.act(qs, ps[:, 0:NS], AF.Silu)
            P.act(fs, ps[:, NS:2 * NS], AF.Tanh, scale=0.5)
            P.act(vTs, ps[:, 2 * NS:3 * NS], AF.Copy)
            P.act(sogs, ps[:, 3 * NS:4 * NS], AF.Silu)
            P.ts('dve', fs, fs, OML[l][:, h:h + 1], LB[l][:, h:h + 1], ALU.mult, ALU.add)
            P.ts('dve', kTs, fs, -1.0, 1.0, ALU.mult, ALU.add)
            psT = bank()[:, :].bitcast(BF16)
            P.tr(psT[0:NS, 0:128], kTs, IDB)
            P.tr(psT[0:NS, 128:256], vTs, IDB)
            kv = AR.bf16(256)
            P.copy('dve', kv[0:NS, :], psT[0:NS, 0:256])
            vmask = AR.bf16(NS * 128).rearrange("p (b v) -> p b v", b=NS)
            P.tt('dve', vmask[0:NS], bc(kv[0:NS, 128:256], 1, [NS, NS, 128]), bc(IDF[0:NS, 0:NS], 2, [NS, NS, 128]), ALU.mult)
            Sin = AR.f32(8 * 128).rearrange("p (b v) -> p b v", b=8)
            Sn = AR.f32(8 * 128).rearrange("p (b v) -> p b v", b=8)
            Snb = AR.bf16(8 * 128).rearrange("p (b v) -> p b v", b=8)
            psOs = PS7
            for bg in range(2):
                P.dma('sp', Sin, shg[l, bg * 8:(bg + 1) * 8, h].rearrange("b k v -> k b v"), 'sin')
                P.tt('dve', Sn, Sin, bc(fs[:, bg * 8:(bg + 1) * 8], 2, [128, 8, 128]), ALU.mult)
                for q in range(2):
                    psKV = bank()
                    b0 = bg * 8 + q * 4
                    P.mm(psKV[:, :], kv[0:NS, 0:128], vmask[0:NS, b0:b0 + 4, :].rearrange("p b v -> p (b v)"))
                    sl = Sn[:, q * 4:(q + 1) * 4, :].rearrange("p b v -> p (b v)")
                    P.tt('dve', sl, sl, psKV[:, :], ALU.add)
                P.copy('act', Snb, Sn)
                P.dma('sp', hgs[l, bg * 8:(bg + 1) * 8, h].rearrange("b k v -> k b v"), Sn, okey())
                for b in range(8):
                    bb = bg * 8 + b
                    P.mm(psOs[:, bb:bb + 1], Snb[:, b, :], qs[:, bb:bb + 1])
            osq = AR.bf16(NS); osb = AR.f32(NS); rt = AR.f32(NS)
            P.act(osq, psOs[:, 0:NS], AF.Square)
            P.copy('dve', osb, psOs[:, 0:NS])
            psS = bank()
            P.mm(psS[:, 0:NS], ONESB, osq)
            P.act(rt, psS[:, 0:NS], AF.Ln, scale=1.0 / 128, bias=EPS)
            P.act(rt, rt, AF.Exp, scale=-0.5)
            P.tt('dve', osb, osb, rt, ALU.mult)
            P.stt(OAS[:, h, :], osb, V[:, 0, R_ONORM + l:R_ONORM + l + 1], sogs, ALU.mult, ALU.mult)
            AR.reset(a0)

        def lru_load(l, n):
            W, key = wslot()
            Wv = wview(W)
            P.dma('pool', Wv[:, :, 0:256], wsrc(wp[l], 4096 + n * 256, 256), key)
            P.dma('pool', Wv[:, 0, 256:384], lru_wa[l, n], key)
            P.dma('pool', Wv[:, 0, 384:512], lru_wx[l, n], key)
            return Wv

        def cvec(r, n):
            return V[:, n, r:r + 1]

        def lru_alloc(A=None, nset=3):
            A = A or AR
            A.reset()
            out = []
            for _ in range(nset):
                d = {'XP': A.f32(TB + 4), 'gel': A.bf16(TB), 'xc': A.f32(TB), 'xcb': A.bf16(TB), 'r': A.f32(TB),
                     'ig': A.f32(TB), 'a': A.f32(TB), 'hs': A.f32(TB)}
                d['a2'] = d['r']
                out.append(d)
            return out

        def lru_tb(l, n, tb, Wv, S):
            tsl = slice(tb * TB, (tb + 1) * TB)
            ps1, ps2 = bank(), bank()
            P.mmg(ps1[:, :], [(Wv[:, kc, 0:128], hT[:, kc, tsl]) for kc in range(8)])
            P.mmg(ps2[:, :], [(Wv[:, kc, 128:256], hT[:, kc, tsl]) for kc in range(8)])
            XP, xc, ig, a, a2, hs = S['XP'], S['xc'], S['ig'], S['a'], S['a2'], S['hs']
            P.copy('dve', XP[:, 0:3], HIST[l][:, n, :])
            P.act(XP[:, 3:3 + TB], ps1[:, :], AF.Copy)
            P.act(S['gel'], ps2[:, :], AF.Gelu_apprx_tanh)
            P.copy('dve', HIST[l][:, n, :], XP[:, TB:TB + 3])
            P.ts('dve', xc, XP[:, 0:TB], cvec(R_CONVW + 4 * l, n), cvec(R_CONVB + l, n), ALU.mult, ALU.add)
            for j in range(1, 4):
                P.stt(xc, XP[:, j:j + TB], cvec(R_CONVW + 4 * l + j, n), xc, ALU.mult, ALU.add)
            P.copy('dve', S['xcb'], xc)
            psr, psi = bank(), bank()
            P.mm(psr[:, :], Wv[:, 0, 256:384], S['xcb'])
            P.mm(psi[:, :], Wv[:, 0, 384:512], S['xcb'])
            P.act(S['r'], psr[:, :], AF.Tanh, scale=0.5, bias=HBA[l][:, n:n + 1])
            P.act(ig, psi[:, :], AF.Tanh, scale=0.5, bias=HBX[l][:, n:n + 1])
            P.act(a, S['r'], AF.Exp, scale=C8H[l][:, n:n + 1], bias=C8H[l][:, n:n + 1])
            P.act(a2, S['r'], AF.Exp, scale=C8[l][:, n:n + 1], bias=C8[l][:, n:n + 1])
            P.act(a2, a2, AF.Ln, scale=-1.0, bias=1.0)
            P.act(a2, a2, AF.Exp, scale=0.5)
            P.stt(ig, ig, 1.0, xc, ALU.add, ALU.mult)
            P.stt(ig, ig, 0.5, a2, ALU.mult, ALU.mult)
            P.scan(hs, a, ig, HC[l][:, n:n + 1], ALU.mult, ALU.add)
            P.copy('dve', HC[l][:, n:n + 1], hs[:, TB - 1:TB])
            P.tt('dve', OB[:, n, tsl], hs, S['gel'], ALU.mult)

        def lru_smp(l, n, Wv):
            a0 = AR.pos
            ps = bank()
            P.mmg(ps[:, 0:NS], [(Wv[:, kc, 0:128], hsT[:, kc, :]) for kc in range(8)])
            P.mmg(ps[:, NS:2 * NS], [(Wv[:, kc, 128:256], hsT[:, kc, :]) for kc in range(8)])
            xbs = AR.f32(NS); gels = AR.f32(NS); xcs = AR.f32(NS); xcsb = AR.bf16(NS)
            rs = AR.f32(NS); igs = AR.f32(NS); as_ = AR.f32(NS); a2s = AR.f32(NS)
            P.act(xbs, ps[:, 0:NS], AF.Copy)
            P.act(gels, ps[:, NS:2 * NS], AF.Gelu_apprx_tanh)
            CS = V[:, n, R_SCV + 48 * l:R_SCV + 48 * l + 48].rearrange("p (b j) -> p b j", j=3)
            P.ts('dve', xcs, CS[:, :, 0], cvec(R_CONVW + 4 * l, n), cvec(R_CONVB + l, n), ALU.mult, ALU.add)
            P.stt(xcs, CS[:, :, 1], cvec(R_CONVW + 4 * l + 1, n), xcs, ALU.mult, ALU.add)
            P.stt(xcs, CS[:, :, 2], cvec(R_CONVW + 4 * l + 2, n), xcs, ALU.mult, ALU.add)
            P.stt(xcs, xbs, cvec(R_CONVW + 4 * l + 3, n), xcs, ALU.mult, ALU.add)
            P.copy('dve', xcsb, xcs)
            psg = bank()
            P.mm(psg[:, 0:NS], Wv[:, 0, 256:384], xcsb)
            P.mm(psg[:, NS:2 * NS], Wv[:, 0, 384:512], xcsb)
            P.act(rs, psg[:, 0:NS], AF.Tanh, scale=0.5, bias=HBA[l][:, n:n + 1])
            P.act(igs, psg[:, NS:2 * NS], AF.Tanh, scale=0.5, bias=HBX[l][:, n:n + 1])
            P.act(as_, rs, AF.Exp, scale=C8H[l][:, n:n + 1], bias=C8H[l][:, n:n + 1])
            P.act(a2s, rs, AF.Exp, scale=C8[l][:, n:n + 1], bias=C8[l][:, n:n + 1])
            P.act(a2s, a2s, AF.Ln, scale=-1.0, bias=1.0)
            P.act(a2s, a2s, AF.Exp, scale=0.5)
            P.stt(igs, igs, 1.0, xcs, ALU.add, ALU.mult)
            P.stt(igs, igs, 0.5, a2s, ALU.mult, ALU.mult)
            h0 = V[:, n, R_SRG + NS * l:R_SRG + NS * l + NS]
            P.tt('dve', as_, as_, h0, ALU.mult)
            hn = VO[:, n, O_LRUS + NS * l:O_LRUS + NS * l + NS]
            P.tt('dve', hn, as_, igs, ALU.add)
            P.tt('dve', OBS[:, n, :], hn, gels, ALU.mult)
            VOc = VO[:, n, O_CONVS + 48 * l:O_CONVS + 48 * l + 48].rearrange("p (b j) -> p b j", j=3)
            P.copy('dve', VOc[:, :, 0:2], CS[:, :, 1:3])
            P.copy('dve', VOc[:, :, 2], xbs)
            AR.reset(a0)

        def merge_load(l, j):
            W, key = wslot()
            Wv = wview(W)
            P.dma('pool', Wv, wsrc(wp[l], 6144 + j * 512, 512), key)
            return Wv

        def merge_compute(l, j, Wv, smp):
            AR.reset()
            sg = [AR.f32(TB), AR.f32(TB)]
            for tb in range(2):
                tsl = slice(tb * TB, (tb + 1) * TB)
                pss = [bank() for _ in range(4)]
                srcs = (hT, hT, OA, OB)
                for g in range(4):
                    P.mmg(pss[g][:, :], [(Wv[:, kc, g * 128:(g + 1) * 128], srcs[g][:, kc, tsl]) for kc in range(8)])
                P.act(sg[0], pss[0][:, :], AF.Sigmoid)
                P.act(sg[1], pss[1][:, :], AF.Sigmoid)
                P.tt('dve', sg[0], sg[0], pss[2][:, :], ALU.mult)
                P.tt('dve', sg[1], sg[1], pss[3][:, :], ALU.mult)
                P.tt('pool', MG[:, j, tsl], sg[0], sg[1], ALU.add)
            if smp:
                ps = bank()
                srcs = (hsT, hsT, OAS, OBS)
                for g in range(4):
                    P.mmg(ps[:, g * NS:(g + 1) * NS], [(Wv[:, kc, g * 128:(g + 1) * 128], srcs[g][:, kc, :]) for kc in range(8)])
                s = AR.f32(2 * NS)
                P.act(s, ps[:, 0:2 * NS], AF.Sigmoid)
                P.tt('dve', s, s, ps[:, 2 * NS:4 * NS], ALU.mult)
                P.tt('dve', MGS[:, j, :], s[:, 0:NS], s[:, NS:2 * NS], ALU.add)

        def dense_load(w2d, c0):
            W, key = wslot()
            Wv = wview(W)
            P.dma('pool', Wv, wsrc(w2d, c0, 512), key)
            return Wv

        def resid_add(l, j, gbase, ps_list, ps_s, smp):
            for tb in range(2):
                tsl = slice(tb * TB, (tb + 1) * TB)
                P.stt(xT[:, j, tsl], ps_list[tb][:, :], MOD[l][:, gbase + j, 0:1], xT[:, j, tsl], ALU.mult, ALU.add)
            if smp:
                t = AR.f32(NS)
                P.tt('dve', t, ps_s[:, 0:NS], MOD[l][:, gbase + j, 1:17], ALU.mult)
                P.tt('dve', xsT[:, j, :], xsT[:, j, :], t, ALU.add)

        def out_compute(l, jg, Wv, smp):
            AR.reset()
            for jj in range(4):
                j = jg * 4 + jj
                pl = []
                for tb in range(2):
                    tsl = slice(tb * TB, (tb + 1) * TB)
                    ps = bank()
                    P.mmg(ps[:, :], [(Wv[:, kc, jj * 128:(jj + 1) * 128], MG[:, kc, tsl]) for kc in range(8)])
                    pl.append(ps)
                ps_s = None
                if smp:
                    ps_s = bank()
                    P.mmg(ps_s[:, 0:NS], [(Wv[:, kc, jj * 128:(jj + 1) * 128], MGS[:, kc, :]) for kc in range(8)])
                resid_add(l, j, 16, pl, ps_s, smp)

        def ff1_compute(l, fg, Wv, smp):
            AR.reset()
            rl = [AR.bf16(TB), AR.bf16(TB)]
            rls = AR.bf16(NS)
            k = 0
            for jj in range(4):
                jf = fg * 4 + jj
                for tb in range(2):
                    tsl = slice(tb * TB, (tb + 1) * TB)
                    ps = bank()
                    P.mmg(ps[:, :], [(Wv[:, kc, jj * 128:(jj + 1) * 128], hT[:, kc, tsl]) for kc in range(8)])
                    P.act(rl[k % 2], ps[:, :], AF.Relu)
                    P.tt('pool' if k % 2 else 'dve', FF[:, jf, tsl], rl[k % 2], rl[k % 2], ALU.mult)
                    k += 1
                if smp:
                    ps = bank()
                    P.mmg(ps[:, 0:NS], [(Wv[:, kc, jj * 128:(jj + 1) * 128], hsT[:, kc, :]) for kc in range(8)])
                    P.act(rls, ps[:, 0:NS], AF.Relu)
                    P.tt('dve', FFS[:, jf, :], rls, rls, ALU.mult)

        def ff2_load(l, j):
            W, key = wslot()
            Wv = W.rearrange("p (k n) -> p k n", k=32)
            P.dma('pool', Wv, w_ff2p[l, j].rearrange("p (k n) -> p k n", k=32), key)
            return Wv

        def ff2_compute(l, j, Wv, smp):
            AR.reset()
            pl = []
            for tb in range(2):
                tsl = slice(tb * TB, (tb + 1) * TB)
                ps = bank()
                P.mmg(ps[:, :], [(Wv[:, kc, :], FF[:, kc, tsl]) for kc in range(32)])
                pl.append(ps)
            ps_s = None
            if smp:
                ps_s = bank()
                P.mmg(ps_s[:, 0:NS], [(Wv[:, kc, :], FFS[:, kc, :]) for kc in range(32)])
            resid_add(l, j, 40, pl, ps_s, smp)

        def load_x(hf):
            AR.reset()
            XS = [AR.f32(D) for _ in range(4)]
            for q in range(2):
                for tt_ in range(4):
                    r0 = hf * TH + q * TB + tt_ * 128
                    P.dma('sp', XS[tt_], xp[r0:r0 + 128, :], f'xin{tt_ % 2}')
                for j in range(8):
                    ps = bank()
                    for tt_ in range(4):
                        P.tr(ps[:, tt_ * 128:(tt_ + 1) * 128], XS[tt_][:, j * 128:(j + 1) * 128], IDF)
                    P.copy('act' if j % 2 else 'dve', xT[:, j, q * TB:(q + 1) * TB], ps[:, :])

        def store_y(hf):
            AR.reset()
            yT = AR.f32(8 * TB).rearrange("p (c t) -> p c t", c=8)
            sq = AR.bf16(8 * TB).rearrange("p (c t) -> p c t", c=8)
            rt = AR.f32(TB)
            YS = [AR.f32(D), AR.f32(D)]
            for tb in range(2):
                tsl = slice(tb * TB, (tb + 1) * TB)
                P.act(sq, xT[:, :, tsl], AF.Square)
                ps = bank()
                P.mmg(ps[:, :], [(ONESB, sq[:, c, :]) for c in range(8)])
                P.act(rt, ps[:, :], AF.Ln, scale=1.0 / D, bias=EPS)
                P.act(rt, rt, AF.Exp, scale=-0.5)
                for c in range(8):
                    P.stt(yT[:, c, :], xT[:, c, tsl], V[:, c, R_FNG:R_FNG + 1], rt, ALU.mult, ALU.mult)
                for tt_ in range(4):
                    ys = YS[tt_ % 2]
                    for half in range(2):
                        ps = bank()
                        for i in range(4):
                            c = half * 4 + i
                            P.tr(ps[:, i * 128:(i + 1) * 128], yT[:, c, tt_ * 128:(tt_ + 1) * 128], IDF)
                        P.copy('act' if half else 'dve', ys[:, half * 512:(half + 1) * 512], ps[:, :])
                    r0 = hf * TH + tb * TB + tt_ * 128
                    P.dma('sp', yp[r0:r0 + 128, :], ys, okey())

        import os
        items = []
        INTER = os.environ.get('K_INTER', '0') == '1'

        def add_item(lf, cf):
            items.append((lf, cf))

        HALF1_ITEM = None
        for hf in range(2):
            smp = (hf == 0)
            if hf == 1:
                HALF1_ITEM = len(items)
            add_item(None, (lambda hf=hf: load_x(hf)))
            for l in range(2):
                if hf == 0:
                    for g in [int(x) for x in os.environ.get('K_MODG', '0,1,2,3,4,5,6,7,8,9,10,11').split(',')]:
                        add_item((lambda l=l, g=g: mod_load(l, g)), (lambda W, l=l, g=g: mod_compute(l, g, W)))
                    add_item(None, (lambda l=l: mod_finish(l)))

                def n1(l=l, smp=smp):
                    norm_prompt(A1P[l], MOD[l][:, 0:8, 0], hT)
                    if smp:
                        norm_samples(A1S[l], MOD[l][:, 0:8, 1:17], hsT)
                add_item(None, n1)
                for h in range(8):
                    def hc(W, l=l, h=h, smp=smp, hf=hf):
                        S = hgrn_alloc()
                        for tb in range(2):
                            hgrn_tb(l, h, tb, W, S)
                        if smp:
                            hgrn_smp(l, h, W)
                        if hf == 1:
                            P.dma('sp', hgp[l, h], SST[l][:, h, :], okey())
                    add_item((lambda l=l, h=h: hgrn_load(l, h)), hc)
                lru_items = []
                for n in range(8):
                    def lc(W, l=l, n=n, smp=smp, hf=hf):
                        if hf == 1 and INTER:
                            S = lru_alloc(AR2, 1)
                        else:
                            S = lru_alloc()
                        for tb in range(2):
                            lru_tb(l, n, tb, W, S[(2 * n + tb) % len(S)])
                        if smp:
                            lru_smp(l, n, W)
                        if hf == 1:
                            P.copy('dve', VO[:, n, O_LRUP + l:O_LRUP + l + 1], HC[l][:, n:n + 1])
                            P.copy('dve', VO[:, n, O_CONVP + 3 * l:O_CONVP + 3 * l + 3], HIST[l][:, n, :])
                    lru_items.append(((lambda l=l, n=n: lru_load(l, n)), lc))
                if hf == 1 and INTER:
                    hitems = items[-8:]
                    del items[-8:]
                    for hi, li in zip(hitems, lru_items):
                        items.append(hi)
                        items.append(li)
                else:
                    items.extend(lru_items)
                for j in range(8):
                    add_item((lambda l=l, j=j: merge_load(l, j)), (lambda W, l=l, j=j, smp=smp: merge_compute(l, j, W, smp)))
                for jg in range(2):
                    add_item((lambda l=l, jg=jg: dense_load(w_out[l], jg * 512)),
                             (lambda W, l=l, jg=jg, smp=smp: out_compute(l, jg, W, smp)))

                def n2(l=l, smp=smp):
                    norm_prompt(A2P[l], MOD[l][:, 24:32, 0], hT)
                    if smp:
                        norm_samples(A2S[l], MOD[l][:, 24:32, 1:17], hsT)
                add_item(None, n2)
                for fg in range(8):
                    add_item((lambda l=l, fg=fg: dense_load(w_ff1[l], fg * 512)),
                             (lambda W, l=l, fg=fg, smp=smp: ff1_compute(l, fg, W, smp)))
                for j in range(8):
                    add_item((lambda l=l, j=j: ff2_load(l, j)), (lambda W, l=l, j=j, smp=smp: ff2_compute(l, j, W, smp)))

            def fin(hf=hf, smp=smp):
                store_y(hf)
                if smp:
                    norm_samples(bc(V[:, :, R_FNG], 2, [128, 8, NS]), None, VO[:, :, O_YS:O_YS + NS])
            add_item(None, fin)

        import bisect
        loaded = {}
        load_idx = [i for i, it in enumerate(items) if it[0] is not None]
        ptr = 0
        import os
        PRE = int(os.environ.get('K_PRE', '1'))
        for i, (lf, cf) in enumerate(items):
            if limit is not None and i >= limit:
                break
            cur = bisect.bisect_right(load_idx, i)
            while ptr < min(len(load_idx), cur + PRE) and (limit is None or load_idx[ptr] < limit):
                loaded[load_idx[ptr]] = items[load_idx[ptr]][0]()
                ptr += 1
            P.mark(f"item{i}")
            if HALF1_ITEM is not None and i >= HALF1_ITEM:
                bstate['half'] = 1
            if i == HALF1_ITEM:
                P.warm_from = len(P.ops)
            if lf is not None:
                cf(loaded.pop(i))
            else:
                cf()

        AR.reset()
        OS = [AR.f32(D), AR.f32(D)]
        for ti, (r0, nr) in enumerate(((0, 128), (128, NRO - 128))):
            for half in range(2):
                ps = bank()
                for i in range(4):
                    c = half * 4 + i
                    P.tr(ps[0:nr, i * 128:(i + 1) * 128], VO[:, c, r0:r0 + nr], IDF)
                P.copy('act' if half else 'dve', OS[ti][0:nr, half * 512:(half + 1) * 512], ps[0:nr, :])
            P.dma('sp', orows[r0:r0 + nr, :], OS[ti][0:nr, :], f'orow{ti}')

        with nc.Block():
            P.emit(st, reorder=(os.environ.get('K_REORDER', '1') == '1'))
    return nc, dbg_out


_CACHE = {}


def _consts():
    c = np.zeros((128, 1280), np.float32)
    c[:, 0:128] = np.eye(128, dtype=np.float32)
    c[:, 128:256] = 1.0
    s = np.arange(CH)[:, None]
    t = np.arange(CH)[None, :]
    m = (s <= t).astype(np.float32)
    c[0:CH, 256:768] = np.tile(m, (1, NCH))
    r = np.ones(TB, np.float32)
    r[::CH] = 0.0
    c[:, 768:1280] = r[None, :]
    return c


def _pack_weights(inp):
    w_in = inp["w_in"]
    wp = np.empty((2, D, 10240), np.float32)
    for h in range(8):
        for g in range(4):
            wp[:, :, h * 512 + g * 128:h * 512 + (g + 1) * 128] = w_in[:, :, g * 1024 + h * 128:g * 1024 + (h + 1) * 128]
    for n in range(8):
        wp[:, :, 4096 + n * 256:4096 + n * 256 + 128] = w_in[:, :, 4096 + n * 128:4096 + (n + 1) * 128]
        wp[:, :, 4096 + n * 256 + 128:4096 + (n + 1) * 256] = w_in[:, :, 5120 + n * 128:5120 + (n + 1) * 128]
    for j in range(8):
        b = 6144 + j * 512
        wp[:, :, b:b + 128] = w_in[:, :, 6144 + j * 128:6144 + (j + 1) * 128]
        wp[:, :, b + 128:b + 256] = w_in[:, :, 7168 + j * 128:7168 + (j + 1) * 128]
        wp[:, :, b + 256:b + 384] = inp["w_branch_a"][:, :, j * 128:(j + 1) * 128]
        wp[:, :, b + 384:b + 512] = inp["w_branch_b"][:, :, j * 128:(j + 1) * 128]
    w2 = inp["w_ff2"].reshape(2, 32, 128, 8, 128)
    w_ff2p = np.ascontiguousarray(w2.transpose(0, 3, 2, 1, 4)).reshape(2, 8, 128, 4 * D)
    return {"wp": wp, "w_ff2p": w_ff2p}


def _pack_rows(inp, core):
    rows = np.zeros((NR, D), np.float32)
    for l in range(2):
        rows[R_N1G + l] = inp["norm1_g"][l]
        rows[R_N2G + l] = inp["norm2_g"][l]
        rows[R_CONVW + 4 * l:R_CONVW + 4 * l + 4] = inp["lru_conv_w"][l]
        rows[R_CONVB + l] = inp["lru_conv_b"][l]
        rows[R_BA + l] = inp["lru_ba"][l]
        rows[R_BX + l] = inp["lru_bx"][l]
        rows[R_LAM + l] = inp["lru_lambda"][l]
        rows[R_HGL + l] = inp["hg_lower"][l]
        rows[R_BMOD + 6 * l:R_BMOD + 6 * l + 6] = inp["b_mod"][l].reshape(6, D)
        rows[R_ONORM + l, 0:128] = inp["hg_onorm_g"][l]
        sl = slice(core * NS, (core + 1) * NS)
        rows[R_SRG + NS * l:R_SRG + NS * l + NS] = inp["state_rglru"][l, sl]
        rows[R_SCV + 48 * l:R_SCV + 48 * l + 48] = inp["state_conv"][l, sl].reshape(48, D)
    rows[R_FNG] = inp["final_norm_g"]
    rows[R_CP] = inp["c_prompt"][core]
    rows[R_CS:R_CS + NS] = inp["c_sample"][core * NS:(core + 1) * NS]
    rows[R_XS:R_XS + NS] = inp["x_sample"][core * NS:(core + 1) * NS, 0]
    return rows


def run(inputs, dbg=None, trace=False, cores=NCORE):
    inp = {k: np.ascontiguousarray(np.asarray(v, dtype=np.float32)) for k, v in inputs.items()}
    key = tuple(sorted(dbg)) if dbg else None
    if key not in _CACHE:
        _CACHE[key] = build_nc(dbg)
    nc, dbg_out = _CACHE[key]
    cst = _consts()
    shared = {k: inp[k] for k in ("w_mod", "lru_wa", "lru_wx", "w_out", "w_ff1")}
    shared.update(_pack_weights(inp))
    in_maps = []
    for c in range(cores):
        m = dict(shared)
        m["xp"] = inp["x_prompt"][c]
        m["rows"] = _pack_rows(inp, c)
        m["shg"] = np.ascontiguousarray(inp["state_hgrn"][:, c * NS:(c + 1) * NS])
        m["cst"] = cst
        in_maps.append(m)
    res = run_bass_kernel_spmd(nc, in_maps, core_ids=list(range(cores)), trace=trace)
    return res


def kernel(**inputs):
    res = run(inputs)
    R = res.results
    y_prompt = np.stack([R[c]["yp"] for c in range(NCORE)], 0)
    orow = [R[c]["orows"] for c in range(NCORE)]
    y_sample = np.concatenate([o[O_YS:O_YS + NS] for o in orow], 0)[:, None, :]
    hg_p = np.stack([R[c]["hgp"] for c in range(NCORE)], 1)
    lru_p = np.stack([o[O_LRUP:O_LRUP + 2] for o in orow], 1)
    conv_p = np.stack([o[O_CONVP:O_CONVP + 6].reshape(2, 3, D) for o in orow], 1)
    hg_s = np.concatenate([R[c]["hgs"] for c in range(NCORE)], 1)
    lru_s = np.concatenate([o[O_LRUS:O_LRUS + 2 * NS].reshape(2, NS, D) for o in orow], 1)
    conv_s = np.concatenate([o[O_CONVS:O_CONVS + 96].reshape(2, NS, 3, D) for o in orow], 1)
    f = lambda a: np.ascontiguousarray(a, dtype=np.float32)
    return (f(y_prompt), f(y_sample), f(hg_p), f(lru_p), f(conv_p), f(hg_s), f(lru_s), f(conv_s))
```
